# Optimizing a Trainium2 kernel written in Bass

```python
import math
import jax, jax.numpy as jnp
from jax import lax
import numpy as np

D_MODEL = 1024
BATCH = 8
SEQ = 2048
DEPTH = 4

N_MIXERS = 2
N_GDN_LAYERS = (DEPTH + N_MIXERS - 1) // N_MIXERS
N_MLA_LAYERS = DEPTH // N_MIXERS

GDN_HEADS = 8
GDN_HEAD_DIM = 128
GDN_KEY_DIM = GDN_HEADS * GDN_HEAD_DIM
GDN_VALUE_DIM = GDN_HEADS * GDN_HEAD_DIM
GDN_CONV = 4
GDN_CHUNK = 64
GDN_IN = 2 * GDN_KEY_DIM + 2 * GDN_VALUE_DIM + 2 * GDN_HEADS

MLA_HEADS = 8
MLA_NOPE = 128
MLA_ROPE = 64
MLA_V = 128
MLA_Q_RANK = 384
MLA_KV_RANK = 256
MLA_IN = MLA_Q_RANK + MLA_KV_RANK + MLA_ROPE
ROPE_THETA = 10000.0
Q_BLOCK = 128

D_FF = ((8 * D_MODEL + 3 * 256 - 1) // (3 * 256)) * 256
N_MOD = 6
EPS = 1e-6

kernel_name = "hybrid_gdn_mla_adaln_trunk"


def rmsnorm(x, g):
    xf = x.astype(jnp.float32)
    y = xf * lax.rsqrt(jnp.mean(xf * xf, axis=-1, keepdims=True) + EPS)
    return (y * g.astype(jnp.float32)).astype(x.dtype)


def l2norm(x):
    return x * lax.rsqrt(jnp.sum(x * x, axis=-1, keepdims=True) + EPS)


def causal_depthwise_conv(x, w):
    K = w.shape[-1]
    kern = jnp.transpose(w)[:, None, :].astype(x.dtype)
    return lax.conv_general_dilated(
        x, kern, window_strides=(1,), padding=[(K - 1, 0)],
        dimension_numbers=("NWC", "WIO", "NWC"), feature_group_count=x.shape[-1])


def chunk_gated_delta_rule(q, k, v, g, beta):
    B, H, T, Dk = q.shape
    Dv = v.shape[-1]
    C = GDN_CHUNK
    N = T // C
    q = q.reshape(B, H, N, C, Dk)
    k = k.reshape(B, H, N, C, Dk)
    v = v.reshape(B, H, N, C, Dv)
    g_cum = jnp.cumsum(g.reshape(B, H, N, C), axis=-1)
    beta = beta.reshape(B, H, N, C)

    tril = jnp.tril(jnp.ones((C, C), dtype=bool))
    strict = jnp.tril(jnp.ones((C, C), dtype=bool), k=-1)
    diff = g_cum[..., :, None] - g_cum[..., None, :]
    decay = jnp.exp(jnp.where(tril, diff, -jnp.inf))

    kb = k * beta[..., None]
    L = jnp.where(strict, jnp.einsum("bhnid,bhnjd->bhnij", kb, k) * decay, 0.0)
    A = L + jnp.eye(C, dtype=L.dtype)
    u = lax.linalg.triangular_solve(A, v * beta[..., None], left_side=True, lower=True,
                                    unit_diagonal=True)
    w = lax.linalg.triangular_solve(A, kb * jnp.exp(g_cum)[..., None], left_side=True,
                                    lower=True, unit_diagonal=True)
    attn = jnp.einsum("bhnid,bhnjd->bhnij", q, k) * decay
    q_dec = q * jnp.exp(g_cum)[..., None]
    k_dec = k * jnp.exp(g_cum[..., -1:] - g_cum)[..., None]
    g_last = jnp.exp(g_cum[..., -1])

    def step(S, xs):
        q_i, k_i, w_i, u_i, a_i, gl_i = xs
        v_new = u_i - jnp.einsum("bhck,bhkv->bhcv", w_i, S)
        o_i = jnp.einsum("bhck,bhkv->bhcv", q_i, S) + jnp.einsum("bhij,bhjv->bhiv", a_i, v_new)
        S = S * gl_i[..., None, None] + jnp.einsum("bhck,bhcv->bhkv", k_i, v_new)
        return S, o_i

    xs = tuple(jnp.moveaxis(t, 2, 0) for t in (q_dec, k_dec, w, u, attn, g_last))
    S0 = jnp.zeros((B, H, Dk, Dv), dtype=jnp.float32)
    _, o = lax.scan(step, S0, xs)
    return jnp.moveaxis(o, 0, 2).reshape(B, H, T, Dv)


def gdn_mixer(h, w_in, conv_w, a_log, dt_bias, norm_g, w_out):
    B, T, _ = h.shape
    proj = h @ w_in
    qkv = proj[..., :2 * GDN_KEY_DIM + GDN_VALUE_DIM]
    o0 = 2 * GDN_KEY_DIM + GDN_VALUE_DIM
    gate = proj[..., o0:o0 + GDN_VALUE_DIM]
    a_in = proj[..., o0 + GDN_VALUE_DIM:o0 + GDN_VALUE_DIM + GDN_HEADS]
    b_in = proj[..., o0 + GDN_VALUE_DIM + GDN_HEADS:]

    qkv = jax.nn.silu(causal_depthwise_conv(qkv, conv_w)).astype(jnp.float32)
    q = qkv[..., :GDN_KEY_DIM].reshape(B, T, GDN_HEADS, GDN_HEAD_DIM)
    k = qkv[..., GDN_KEY_DIM:2 * GDN_KEY_DIM].reshape(B, T, GDN_HEADS, GDN_HEAD_DIM)
    v = qkv[..., 2 * GDN_KEY_DIM:].reshape(B, T, GDN_HEADS, GDN_HEAD_DIM)
    q = l2norm(q) * (GDN_HEAD_DIM ** -0.5)
    k = l2norm(k)
    beta = jax.nn.sigmoid(b_in.astype(jnp.float32))
    g = -jnp.exp(a_log.astype(jnp.float32)) * jax.nn.softplus(
        a_in.astype(jnp.float32) + dt_bias.astype(jnp.float32))

    to_bhtd = lambda t: jnp.transpose(t, (0, 2, 1, 3))
    o = chunk_gated_delta_rule(to_bhtd(q), to_bhtd(k), to_bhtd(v),
                               jnp.transpose(g, (0, 2, 1)), jnp.transpose(beta, (0, 2, 1)))
    o = jnp.transpose(o, (0, 2, 1, 3)).astype(h.dtype)
    o = rmsnorm(o, norm_g) * jax.nn.silu(gate.reshape(B, T, GDN_HEADS, GDN_HEAD_DIM))
    return o.reshape(B, T, GDN_VALUE_DIM) @ w_out


def apply_rope(x, cos, sin):
    half = x.shape[-1] // 2
    x1, x2 = x[..., :half], x[..., half:]
    return jnp.concatenate([x1 * cos - x2 * sin, x2 * cos + x1 * sin], axis=-1)


def causal_mla_attention(q_nope, q_rope, k_nope, k_rope, v):
    B, T, H, _ = q_nope.shape
    nb = T // Q_BLOCK
    scale = (MLA_NOPE + MLA_ROPE) ** -0.5
    qn = jnp.moveaxis(q_nope.reshape(B, nb, Q_BLOCK, H, MLA_NOPE), 1, 0)
    qr = jnp.moveaxis(q_rope.reshape(B, nb, Q_BLOCK, H, MLA_ROPE), 1, 0)
    kpos = jnp.arange(T)

    def block(args):
        qn_b, qr_b, start = args
        s = (jnp.einsum("bqhd,bkhd->bhqk", qn_b, k_nope)
             + jnp.einsum("bqhd,bkd->bhqk", qr_b, k_rope)).astype(jnp.float32) * scale
        qpos = start + jnp.arange(Q_BLOCK)
        s = jnp.where(kpos[None, :] <= qpos[:, None], s, -jnp.inf)
        p = jax.nn.softmax(s, axis=-1).astype(v.dtype)
        return jnp.einsum("bhqk,bkhd->bqhd", p, v)

    o = lax.map(block, (qn, qr, jnp.arange(nb) * Q_BLOCK))
    return jnp.moveaxis(o, 0, 1).reshape(B, T, H, MLA_V)


def mla_mixer(h, cos, sin, w_in, q_norm_g, kv_norm_g, w_uq, w_ukv, w_out):
    B, T, _ = h.shape
    proj = h @ w_in
    c_q = proj[..., :MLA_Q_RANK]
    c_kv = proj[..., MLA_Q_RANK:MLA_Q_RANK + MLA_KV_RANK]
    k_rope = proj[..., MLA_Q_RANK + MLA_KV_RANK:]
    q = (rmsnorm(c_q, q_norm_g) @ w_uq).reshape(B, T, MLA_HEADS, MLA_NOPE + MLA_ROPE)
    kv = (rmsnorm(c_kv, kv_norm_g) @ w_ukv).reshape(B, T, MLA_HEADS, MLA_NOPE + MLA_V)
    q_nope, q_rope = q[..., :MLA_NOPE], q[..., MLA_NOPE:]
    k_nope, v = kv[..., :MLA_NOPE], kv[..., MLA_NOPE:]
    q_rope = apply_rope(q_rope, cos[:, :, None, :], sin[:, :, None, :])
    k_rope = apply_rope(k_rope, cos, sin)
    o = causal_mla_attention(q_nope, q_rope, k_nope, k_rope, v)
    return o.reshape(B, T, MLA_HEADS * MLA_V) @ w_out


def swiglu(h, w_gate, w_up, w_down):
    return (jax.nn.silu(h @ w_gate) * (h @ w_up)) @ w_down


def setup_inputs(seed: int = 0) -> dict:
    key = jax.random.key(seed)
    ks = jax.random.split(key, 24)
    f32 = jnp.float32
    nrm = lambda k, shape, s: jax.random.normal(k, shape, f32) * s
    x = jax.random.normal(ks[0], (BATCH, SEQ, D_MODEL), f32)
    c = jax.random.normal(ks[1], (BATCH, D_MODEL), f32)
    positions = (jnp.arange(SEQ, dtype=jnp.int32)[None, :]
                 + jax.random.randint(ks[2], (BATCH, 1), 0, 1024, dtype=jnp.int32))
    ada_w = nrm(ks[3], (DEPTH, D_MODEL, N_MOD * D_MODEL), 0.5 * D_MODEL ** -0.5)
    ada_b = nrm(ks[4], (DEPTH, N_MOD * D_MODEL), 0.02)
    norm_mix_g = 1.0 + nrm(ks[5], (DEPTH, D_MODEL), 0.05)
    norm_ffn_g = 1.0 + nrm(ks[6], (DEPTH, D_MODEL), 0.05)

    gdn_w_in = nrm(ks[7], (N_GDN_LAYERS, D_MODEL, GDN_IN), D_MODEL ** -0.5)
    gdn_conv_w = nrm(ks[8], (N_GDN_LAYERS, 2 * GDN_KEY_DIM + GDN_VALUE_DIM, GDN_CONV),
                     GDN_CONV ** -0.5)
    gdn_a_log = jnp.log(jax.random.uniform(ks[9], (N_GDN_LAYERS, GDN_HEADS), f32, 1.0, 16.0))
    dt = jnp.exp(jax.random.uniform(ks[10], (N_GDN_LAYERS, GDN_HEADS), f32,
                                    math.log(1e-3), math.log(1e-1)))
    gdn_dt_bias = dt + jnp.log(-jnp.expm1(-dt))
    gdn_norm_g = 1.0 + nrm(ks[11], (N_GDN_LAYERS, GDN_HEAD_DIM), 0.05)
    gdn_w_out = nrm(ks[12], (N_GDN_LAYERS, GDN_VALUE_DIM, D_MODEL), GDN_VALUE_DIM ** -0.5)

    mla_w_in = nrm(ks[13], (N_MLA_LAYERS, D_MODEL, MLA_IN), D_MODEL ** -0.5)
    mla_q_norm_g = 1.0 + nrm(ks[14], (N_MLA_LAYERS, MLA_Q_RANK), 0.05)
    mla_kv_norm_g = 1.0 + nrm(ks[15], (N_MLA_LAYERS, MLA_KV_RANK), 0.05)
    mla_w_uq = nrm(ks[16], (N_MLA_LAYERS, MLA_Q_RANK, MLA_HEADS * (MLA_NOPE + MLA_ROPE)),
                   MLA_Q_RANK ** -0.5)
    mla_w_ukv = nrm(ks[17], (N_MLA_LAYERS, MLA_KV_RANK, MLA_HEADS * (MLA_NOPE + MLA_V)),
                    MLA_KV_RANK ** -0.5)
    mla_w_out = nrm(ks[18], (N_MLA_LAYERS, MLA_HEADS * MLA_V, D_MODEL),
                    (MLA_HEADS * MLA_V) ** -0.5)

    ffn_w_gate = nrm(ks[19], (DEPTH, D_MODEL, D_FF), D_MODEL ** -0.5)
    ffn_w_up = nrm(ks[20], (DEPTH, D_MODEL, D_FF), D_MODEL ** -0.5)
    ffn_w_down = nrm(ks[21], (DEPTH, D_FF, D_MODEL), D_FF ** -0.5)
    final_norm_g = 1.0 + nrm(ks[22], (D_MODEL,), 0.05)
    return {
        "x": x, "c": c, "positions": positions,
        "ada_w": ada_w, "ada_b": ada_b, "norm_mix_g": norm_mix_g, "norm_ffn_g": norm_ffn_g,
        "gdn_w_in": gdn_w_in, "gdn_conv_w": gdn_conv_w, "gdn_a_log": gdn_a_log,
        "gdn_dt_bias": gdn_dt_bias, "gdn_norm_g": gdn_norm_g, "gdn_w_out": gdn_w_out,
        "mla_w_in": mla_w_in, "mla_q_norm_g": mla_q_norm_g, "mla_kv_norm_g": mla_kv_norm_g,
        "mla_w_uq": mla_w_uq, "mla_w_ukv": mla_w_ukv, "mla_w_out": mla_w_out,
        "ffn_w_gate": ffn_w_gate, "ffn_w_up": ffn_w_up, "ffn_w_down": ffn_w_down,
        "final_norm_g": final_norm_g,
    }


def reference(x, c, positions, ada_w, ada_b, norm_mix_g, norm_ffn_g,
              gdn_w_in, gdn_conv_w, gdn_a_log, gdn_dt_bias, gdn_norm_g, gdn_w_out,
              mla_w_in, mla_q_norm_g, mla_kv_norm_g, mla_w_uq, mla_w_ukv, mla_w_out,
              ffn_w_gate, ffn_w_up, ffn_w_down, final_norm_g):
    inv_freq = ROPE_THETA ** (-jnp.arange(0, MLA_ROPE, 2, dtype=jnp.float32) / MLA_ROPE)
    ang = positions.astype(jnp.float32)[..., None] * inv_freq
    cos = jnp.cos(ang).astype(x.dtype)
    sin = jnp.sin(ang).astype(x.dtype)
    c_act = jax.nn.silu(c)

    for layer in range(DEPTH):
        mod = c_act @ ada_w[layer] + ada_b[layer]
        shift_m, scale_m, gate_m, shift_f, scale_f, gate_f = [
            m[:, None, :] for m in jnp.split(mod, N_MOD, axis=-1)]

        h = rmsnorm(x, norm_mix_g[layer]) * (1.0 + scale_m) + shift_m
        j = layer // N_MIXERS
        if layer % N_MIXERS == 0:
            y = gdn_mixer(h, gdn_w_in[j], gdn_conv_w[j], gdn_a_log[j], gdn_dt_bias[j],
                          gdn_norm_g[j], gdn_w_out[j])
        else:
            y = mla_mixer(h, cos, sin, mla_w_in[j], mla_q_norm_g[j], mla_kv_norm_g[j],
                          mla_w_uq[j], mla_w_ukv[j], mla_w_out[j])
        x = x + gate_m * y

        h = rmsnorm(x, norm_ffn_g[layer]) * (1.0 + scale_f) + shift_f
        x = x + gate_f * swiglu(h, ffn_w_gate[layer], ffn_w_up[layer], ffn_w_down[layer])

    return rmsnorm(x, final_norm_g)
```

```python
from contextlib import ExitStack
import numpy as np
import concourse.bass as bass
import concourse.mybir as mybir
from concourse.bass_utils import run_bass_kernel_spmd

F32 = mybir.dt.float32
BF16 = mybir.dt.bfloat16
I32 = mybir.dt.int32
AF = mybir.ActivationFunctionType
ALU = mybir.AluOpType
AX = mybir.AxisListType

ENGS = ["pe", "act", "dve", "pool", "sp"]
NDMASEM = 8

D = 1024
T = 2048
NT = 16
DFF = 2816
NFC = 22
EPS = 1e-6


class Op:
    __slots__ = ("eng", "fn", "waits", "dwaits", "idx", "is_dma", "dsem", "dval", "tag", "calls")


class _Rec:
    def __init__(self):
        self.calls = []

    def __getattr__(self, name):
        def f(*a, **k):
            self.calls.append((name, a, k))
            return self
        return f


class Prog:
    def __init__(self, nc):
        self.nc = nc
        self.ops = {e: [] for e in ENGS}
        self.known = {e: {f: 0 for f in ENGS} for e in ENGS}
        self.kdma = {e: {} for e in ENGS}
        self.snaps = {e: [] for e in ENGS}
        self.last_w = {}
        self.readers = {}
        self.ndma = {e: 0 for e in ENGS}
        self.out_tokens = []
        self.milestones = {e: set() for e in ENGS}

    def _need(self, eng, tok, waits, dwaits):
        if tok is None:
            return
        if tok[0] == "e":
            _, f, idx = tok
            if f == eng and eng == "pe":
                return
            if self.known[eng][f] >= idx + 1:
                return
            waits.append((f, idx))
            self.milestones[f].add(idx)
            kn, kd = self.snaps[f][idx]
            for g in ENGS:
                if kn[g] > self.known[eng][g]:
                    self.known[eng][g] = kn[g]
            for k, v in kd.items():
                if self.kdma[eng].get(k, 0) < v:
                    self.kdma[eng][k] = v
            if self.known[eng][f] < idx + 1:
                self.known[eng][f] = idx + 1
        else:
            _, q, semi, val = tok
            key = (q, semi)
            if self.kdma[eng].get(key, 0) >= val:
                return
            dwaits.append((q, semi, val))
            self.kdma[eng][key] = val

    def op(self, eng, fn, reads=(), writes=(), dma=False, tag=None):
        rec = _Rec()
        fn(rec)
        return self.submit(eng, rec.calls, reads, writes, dma=dma, tag=tag)

    def submit(self, eng, calls, reads=(), writes=(), dma=False, tag=None):
        o = Op()
        o.eng = eng
        o.fn = True
        o.tag = tag
        o.is_dma = dma
        o.calls = calls
        assert len(o.calls) == 1
        waits, dwaits = [], []
        cand = []
        for k in reads:
            tok = self.last_w.get(k)
            if tok is not None:
                cand.append(tok)
            if isinstance(k, tuple) and k[0] == "ps":
                for rt in self.readers.get(k, ()):
                    if not (rt[0] == "e" and rt[1] == eng):
                        cand.append(rt)
        for k in writes:
            tok = self.last_w.get(k)
            if tok is not None:
                if not (tok[0] == "e" and tok[1] == eng):
                    cand.append(tok)
            for rt in self.readers.get(k, ()):
                if not (rt[0] == "e" and rt[1] == eng):
                    cand.append(rt)
        best = {}
        for tok in cand:
            kk = (tok[0], tok[1]) if tok[0] == "e" else (tok[0], tok[1], tok[2])
            if kk not in best or tok[-1] > best[kk][-1]:
                best[kk] = tok
        for tok in best.values():
            self._need(eng, tok, waits, dwaits)
        idx = len(self.ops[eng])
        o.idx = idx
        if dma:
            n = self.ndma[eng]
            self.ndma[eng] = n + 1
            semi = n % NDMASEM
            val = 16 * (n // NDMASEM + 1)
            if val > 16:
                self._need(eng, ("d", eng, semi, val - 16), waits, dwaits)
            o.dsem, o.dval = semi, val
            tok = ("d", eng, semi, val)
        else:
            tok = ("e", eng, idx)
        o.waits, o.dwaits = waits, dwaits
        self.ops[eng].append(o)
        self.snaps[eng].append((dict(self.known[eng]), dict(self.kdma[eng])))
        for k in reads:
            lst = self.readers.setdefault(k, [])
            if tok[0] == "e":
                lst[:] = [r for r in lst if not (r[0] == "e" and r[1] == tok[1])]
            lst.append(tok)
        for k in writes:
            self.last_w[k] = tok
            self.readers[k] = []
        return tok

    def pe(self, fn, reads=(), writes=()):
        return self.op("pe", fn, reads, writes)

    def act(self, fn, reads=(), writes=()):
        return self.op("act", fn, reads, writes)

    def dve(self, fn, reads=(), writes=()):
        return self.op("dve", fn, reads, writes)

    def pool(self, fn, reads=(), writes=()):
        return self.op("pool", fn, reads, writes)

    def dma(self, q, out, in_, reads=(), writes=(), is_out=False, **kw):
        tok = self.op(q, lambda e: e.dma_start(out=out, in_=in_, **kw), reads, writes, dma=True)
        if is_out:
            self.out_tokens.append(tok)
        return tok

    def barrier(self):
        toks = []
        for f in ENGS:
            for o in reversed(self.ops[f]):
                if (not o.is_dma) and o.fn is not None:
                    toks.append(("e", f, o.idx))
                    break
            n = self.ndma[f]
            for i in range(max(0, n - NDMASEM), n):
                toks.append(("d", f, i % NDMASEM, 16 * (i // NDMASEM + 1)))
        for e in ENGS:
            waits, dwaits = [], []
            for tok in toks:
                if tok[0] == "e" and tok[1] == e:
                    if e == "pe":
                        continue
                self._need(e, tok, waits, dwaits)
            if not waits and not dwaits:
                continue
            o = Op()
            o.eng = e; o.fn = None; o.waits = waits; o.dwaits = dwaits
            o.idx = len(self.ops[e]); o.is_dma = False; o.tag = "barrier"
            self.ops[e].append(o)
            self.snaps[e].append((dict(self.known[e]), dict(self.kdma[e])))

    def finish(self):
        waits, dwaits = [], []
        for tok in self.out_tokens:
            self._need("sp", tok, waits, dwaits)
        o = Op()
        o.eng = "sp"; o.fn = None; o.waits = waits; o.dwaits = dwaits
        o.idx = len(self.ops["sp"]); o.is_dma = False; o.tag = "finish"
        self.ops["sp"].append(o)
        self.snaps["sp"].append((dict(self.known["sp"]), dict(self.kdma["sp"])))

    def emit(self):
        nc = self.nc
        rank = {}
        for e in ENGS:
            ms = sorted(self.milestones[e])
            rank[e] = {idx: i + 1 for i, idx in enumerate(ms)}
        with ExitStack() as es:
            esem = {e: es.enter_context(nc.semaphore("pg_" + e)) for e in ENGS}
            dsem = {}
            for q in ENGS:
                for i in range(min(NDMASEM, self.ndma[q])):
                    dsem[(q, i)] = es.enter_context(nc.semaphore("dm_%s_%d" % (q, i)))
            block = es.enter_context(nc.Block())

            def run(eng_name):
                def body(eng):
                    for o in self.ops[eng_name]:
                        for (f, idx) in o.waits:
                            eng.wait_ge(esem[f], rank[f][idx])
                        for (q, semi, val) in o.dwaits:
                            eng.wait_ge(dsem[(q, semi)], val)
                        if o.fn is None:
                            continue
                        ins = None
                        for (nm, a, k) in o.calls:
                            ins = getattr(eng, nm)(*a, **k)
                        if o.is_dma:
                            ins.then_inc(dsem[(eng_name, o.dsem)], 16)
                        elif o.idx in rank[eng_name]:
                            ins.then_inc(esem[eng_name], 1)
                return body

            block.tensor(run("pe"))
            block.scalar(run("act"))
            block.vector(run("dve"))
            block.gpsimd(run("pool"))
            block.sync(run("sp"))


class Stream:
    def __init__(self):
        self.ops = []

    def _add(self, eng, fn, reads, writes):
        rec = _Rec()
        fn(rec)
        self.ops.append((eng, rec.calls, tuple(reads), tuple(writes)))

    def pe(self, fn, reads=(), writes=()):
        self._add("pe", fn, reads, writes)

    def act(self, fn, reads=(), writes=()):
        self._add("act", fn, reads, writes)

    def dve(self, fn, reads=(), writes=()):
        self._add("dve", fn, reads, writes)

    def pool(self, fn, reads=(), writes=()):
        self._add("pool", fn, reads, writes)


def merge_streams(P, streams):
    lists = [st.ops for st in streams if st is not None and st.ops]
    idx = [0] * len(lists)
    tot = [len(l) for l in lists]
    while True:
        live = [k for k in range(len(lists)) if idx[k] < tot[k]]
        if not live:
            break
        k = min(live, key=lambda q: idx[q] / tot[q])
        eng, calls, reads, writes = lists[k][idx[k]]
        P.submit(eng, calls, reads, writes)
        idx[k] += 1


class Ctx:
    def __init__(self):
        self.nc = bass.Bass("TRN2", target_bir_lowering=False)
        self.P = Prog(self.nc)
        self.es = ExitStack()
        self.banks = [self.es.enter_context(self.nc.psum_tensor("pb%d" % i, [128, 512], F32)) for i in range(8)]
        self.nbank = 0
        self.uid = 0
        self.stacks = [self.es]
        self.pcount = {}

    def sb(self, name, shape, dt):
        self.uid += 1
        return self.stacks[-1].enter_context(self.nc.sbuf_tensor("%s_%d" % (name, self.uid), shape, dt))

    def scope_begin(self):
        self.stacks.append(ExitStack())

    def scope_end(self):
        self.P.barrier()
        self.stacks.pop().close()

    def din(self, name, shape, dt=F32):
        return self.nc.dram_tensor(name, list(shape), dt, kind="ExternalInput").ap()

    def dout(self, name, shape, dt=F32):
        return self.nc.dram_tensor(name, list(shape), dt, kind="ExternalOutput").ap()

    def ps(self):
        i = self.nbank % 8
        self.nbank += 1
        return self.banks[i], ("ps", i)

    def key(self, base):
        self.uid += 1
        return (base, self.uid)


def load_cols(C, vec_d, n, name, ident32):
    P = C.P
    rows = C.sb(name + "_r", [n, 128], F32)
    cols = C.sb(name + "_c", [128, n], F32)
    P.dma("sp", rows[:], vec_d.rearrange("(c p) -> c p", p=128), writes=[name + "_r"])
    pb, pk = C.ps()
    P.pe(lambda e: e.transpose(pb[:, 0:n], rows[:], ident32[0:n, 0:n]), reads=[name + "_r", "ident32"], writes=[pk])
    P.dve(lambda e: e.tensor_copy(out=cols[:], in_=pb[:, 0:n]), reads=[pk], writes=[name + "_c"])
    return cols, name + "_c"


INPUT_SHAPES = {
    "x": ([T, D], F32), "c": ([D], F32), "positions": ([T], I32),
    "ada_w": ([4, D, 6 * D], F32), "ada_b": ([4, 6 * D], F32), "norm_mix_g": ([4, D], F32), "norm_ffn_g": ([4, D], F32),
    "gdn_w_in": ([2, D, 4112], F32), "gdn_conv_w": ([2, 3072, 4], F32), "gdn_a_log": ([2, 8], F32), "gdn_dt_bias": ([2, 8], F32),
    "gdn_norm_g": ([2, 128], F32), "gdn_w_out": ([2, D, D], F32),
    "mla_w_in": ([2, D, 704], F32), "mla_q_norm_g": ([2, 384], F32), "mla_kv_norm_g": ([2, 256], F32),
    "mla_w_uq": ([2, 384, 1536], F32), "mla_w_ukv": ([2, 256, 2048], F32), "mla_w_out": ([2, D, D], F32),
    "ffn_w_gate": ([4, D, DFF], F32), "ffn_w_up": ([4, D, DFF], F32), "ffn_w_down": ([4, DFF, D], F32),
    "final_norm_g": ([D], F32),
    "ident": ([128, 128], F32), "invf": ([128], F32), "tri": ([128, 128], F32), "gmasks": ([19, 128, 128], F32),
}


def setup(C):
    P = C.P
    C.I = {nm: C.din(nm, shp, dt) for nm, (shp, dt) in INPUT_SHAPES.items()}
    I = C.I
    ident32 = C.sb("ident32", [128, 128], F32)
    identb = C.sb("identb", [128, 128], BF16)
    ones32 = C.sb("ones32", [128, 128], F32)
    X = C.sb("X", [128, NT, D], F32)
    modc = C.sb("modc", [128, 24], F32)
    gs = C.sb("gs", [128, 8], F32)
    gate_bc = C.sb("gate_bc", [128, D], F32)
    ccol = C.sb("ccol", [128, 8], F32)
    C.epsc = C.sb("epsc", [128, 1], F32)
    P.dve(lambda e: e.memset(C.epsc[:], EPS), writes=["epsc"])
    P.dma("sp", ident32[:], I["ident"], writes=["ident32"])
    P.dma("pool", identb[:], I["ident"], writes=["identb"])
    P.dve(lambda e: e.memset(ones32[:], 1.0), writes=["ones32"])
    xv = I["x"].rearrange("(t p) d -> p t d", p=128)
    for i in range(4):
        P.dma("sp", X[:, 4 * i:4 * i + 4, :], xv[:, 4 * i:4 * i + 4, :], writes=[("X", t) for t in range(4 * i, 4 * i + 4)])
    C.scope_begin()
    ccol_raw, ck = load_cols(C, I["c"], 8, "cc", ident32)
    P.act(lambda e: e.activation(out=ccol[:], in_=ccol_raw[:], func=AF.Silu), reads=[ck], writes=["ccol"])
    C.scope_end()
    return dict(X=X, modc=modc, gs=gs, gate_bc=gate_bc, ident32=ident32, identb=identb, ones32=ones32, ccol=ccol)


def mod_compute(C, R, layer, groups, ng_d):
    P = C.P
    I = C.I
    modc, gs, gate_bc, ident32, ones32, ccol = R["modc"], R["gs"], R["gate_bc"], R["ident32"], R["ones32"], R["ccol"]
    adaw_d = I["ada_w"][layer]
    adab_d = I["ada_b"][layer]
    C.scope_begin()
    bcol, bk = load_cols(C, adab_d, 48, "ab", ident32)
    gcol, gk = load_cols(C, ng_d, 8, "ng", ident32)
    aws = [C.sb("aw%d" % i, [128, 8, 128], F32) for i in range(2)]
    pb, pk = C.ps()
    adv = adaw_d.rearrange("(k p) f -> p k f", p=128)
    na = 0
    for gi, g in enumerate(groups):
        for j in range(8):
            aw = aws[na % 2]
            awk = ("aw", na % 2)
            na += 1
            P.dma("sp", aw[:], adv[:, :, g * D + j * 128:g * D + (j + 1) * 128], writes=[awk])
            for k in range(8):
                P.pe(lambda e, j=j, k=k, gi=gi, aw=aw: e.matmul(pb[:, gi * 8 + j:gi * 8 + j + 1], lhsT=aw[:, k, :],
                                                                rhs=ccol[:, k:k + 1], start=(k == 0), stop=(k == 7)),
                     reads=[awk, "ccol"], writes=[pk])
    for gi, g in enumerate(groups):
        P.dve(lambda e, gi=gi, g=g: e.tensor_tensor(out=modc[:, gi * 8:gi * 8 + 8], in0=pb[:, gi * 8:gi * 8 + 8],
                                                    in1=bcol[:, g * 8:g * 8 + 8], op=ALU.add),
              reads=[pk, bk], writes=["modc"])
    P.dve(lambda e: e.scalar_tensor_tensor(out=gs[:], in0=modc[:, 8:16], scalar=1.0, in1=gcol[:], op0=ALU.add, op1=ALU.mult),
          reads=["modc", gk], writes=["gs"])
    dg = C.sb("dg", [128, 128], F32)
    for j in range(8):
        P.dve(lambda e, j=j: e.tensor_scalar(out=dg[:], in0=ident32[:], scalar1=modc[:, 16 + j:17 + j], scalar2=None, op0=ALU.mult),
              reads=["ident32", "modc"], writes=["dg"])
        pb2, pk2 = C.ps()
        P.pe(lambda e, pb2=pb2: e.matmul(pb2[:, 0:128], lhsT=ones32[:], rhs=dg[:], start=True, stop=True),
             reads=["ones32", "dg"], writes=[pk2])
        P.act(lambda e, j=j, pb2=pb2: e.copy(out=gate_bc[:, j * 128:(j + 1) * 128], in_=pb2[:, 0:128]), reads=[pk2], writes=["gate_bc"])
    C.scope_end()


def rstd_col(C, X, t, small, ki, junk):
    P = C.P
    kk = ("small", ki)
    P.act(lambda e: e.activation(out=junk[:], in_=X[:, t, :], func=AF.Square, accum_out=small[:, ki:ki + 1]),
          reads=[("X", t)], writes=["junk", kk])
    P.act(lambda e: e.activation(out=small[:, ki:ki + 1], in_=small[:, ki:ki + 1], func=AF.Ln, scale=1.0 / D, bias=C.epsc[:, 0:1]),
          reads=[kk, "epsc"], writes=[kk])
    P.act(lambda e: e.activation(out=small[:, ki:ki + 1], in_=small[:, ki:ki + 1], func=AF.Exp, scale=-0.5),
          reads=[kk], writes=[kk])
    return kk


def norm_to_hT(C, R, hT):
    P = C.P
    X, modc, gs, identb = R["X"], R["modc"], R["gs"], R["identb"]
    C.scope_begin()
    small = C.sb("nsmall", [128, NT], F32)
    junk = C.sb("junk", [128, D], BF16)
    tmps = [C.sb("tmpn%d" % i, [128, D], F32) for i in range(2)]
    xn = [C.sb("xn%d" % i, [128, D], BF16) for i in range(2)]
    for t in range(NT):
        tmp = tmps[t % 2]
        tk = ("tmpn", t % 2)
        kk = rstd_col(C, X, t, small, t, junk)
        xb = xn[t % 2]
        xk = ("xn", t % 2)
        P.dve(lambda e, xb=xb, t=t: e.tensor_scalar(out=xb[:], in0=X[:, t, :], scalar1=small[:, t:t + 1], scalar2=None, op0=ALU.mult),
              reads=[("X", t), kk], writes=[xk])
        pb, pk = C.ps()
        pbb = pb[:].bitcast(BF16)
        for c in range(8):
            P.pe(lambda e, c=c, xb=xb, pbb=pbb: e.transpose(pbb[:, c * 128:(c + 1) * 128], xb[:, c * 128:(c + 1) * 128], identb[:]),
                 reads=[xk, "identb"], writes=[pk])
        P.dve(lambda e, pbb=pbb, tmp=tmp: e.tensor_tensor(out=tmp[:].rearrange("p (c i) -> p c i", c=8), in0=pbb.rearrange("p (c i) -> p c i", c=8),
                                                          in1=gs[:].unsqueeze(2).to_broadcast([128, 8, 128]), op=ALU.mult),
              reads=[pk, "gs"], writes=[tk])
        P.pool(lambda e, t=t, tmp=tmp: e.tensor_tensor(out=hT[:, :, t * 128:(t + 1) * 128], in0=tmp[:].rearrange("p (c i) -> p c i", c=8),
                                                       in1=modc[:, 0:8].unsqueeze(2).to_broadcast([128, 8, 128]), op=ALU.add),
               reads=[tk, "modc"], writes=[("hT", t)])
    C.scope_end()


def store_x(C, X, name="out"):
    P = C.P
    o_d = C.dout(name, [T, D])
    ov = o_d.rearrange("(t p) d -> p t d", p=128)
    for i in range(4):
        P.dma("sp", ov[:, 4 * i:4 * i + 4, :], X[:, 4 * i:4 * i + 4, :], reads=[("X", t) for t in range(4 * i, 4 * i + 4)], is_out=True)


def ffn_body(C, R, hT, layer, final):
    P = C.P
    I = C.I
    ng_d = I["norm_ffn_g"][layer]
    wg_d = I["ffn_w_gate"][layer]
    wu_d = I["ffn_w_up"][layer]
    wd_d = I["ffn_w_down"][layer]
    mod_compute(C, R, layer, [3, 4, 5], ng_d)
    X, gate_bc = R["X"], R["gate_bc"]
    norm_to_hT(C, R, hT)
    G = 4
    wgu = [C.sb("wgu%d" % i, [128, 8, 2, G * 128], BF16) for i in range(2)]
    wdn = [C.sb("wdn%d" % i, [128, G, D], BF16) for i in range(2)]
    hid = [C.sb("hid%d" % i, [128, G, 512], BF16) for i in range(2)]
    sg = [C.sb("sg%d" % i, [128, 512], F32) for i in range(2)]
    wgv = wg_d.rearrange("(k p) f -> p k f", p=128)
    wuv = wu_d.rearrange("(k p) f -> p k f", p=128)
    wdv = wd_d.rearrange("(c p) n -> p c n", p=128)
    groups = []
    c0 = 0
    while c0 < NFC:
        g = min(G, NFC - c0)
        groups.append((c0, g))
        c0 += g
    nsg = 0

    def load_group(gi):
        c0, g = groups[gi]
        b = gi % 2
        kgu = ("wgu", b)
        kd = ("wdn", b)
        P.dma("pool", wgu[b][:, :, 0, 0:g * 128], wgv[:, :, c0 * 128:(c0 + g) * 128], writes=[kgu])
        P.dma("pool", wgu[b][:, :, 1, 0:g * 128], wuv[:, :, c0 * 128:(c0 + g) * 128], writes=[kgu])
        P.dma("pool", wdn[b][:, 0:g, :], wdv[:, c0:c0 + g, :], writes=[kd])
        P.pool(lambda e: e.tensor_tensor(out=wdn[b][:, 0:g, :], in0=wdn[b][:, 0:g, :],
                                         in1=gate_bc[:].unsqueeze(1).to_broadcast([128, g, D]), op=ALU.mult),
               reads=[kd, "gate_bc"], writes=[kd])

    def gate_up(gi, tb, hb):
        nonlocal nsg
        c0, g = groups[gi]
        b = gi % 2
        kgu = ("wgu", b)
        hk = ("hid", hb)
        for j in range(g):
            pg, pgk = C.ps()
            pu, puk = C.ps()
            for k in range(8):
                P.pe(lambda e, k=k: e.matmul(pg[:], lhsT=wgu[b][:, k, 0, j * 128:(j + 1) * 128], rhs=hT[:, k, tb * 512:(tb + 1) * 512],
                                            start=(k == 0), stop=(k == 7)),
                     reads=[kgu] + [("hT", 4 * tb + i) for i in range(4)], writes=[pgk])
            for k in range(8):
                P.pe(lambda e, k=k: e.matmul(pu[:], lhsT=wgu[b][:, k, 1, j * 128:(j + 1) * 128], rhs=hT[:, k, tb * 512:(tb + 1) * 512],
                                            start=(k == 0), stop=(k == 7)),
                     reads=[kgu] + [("hT", 4 * tb + i) for i in range(4)], writes=[puk])
            sb_ = nsg % 2
            nsg += 1
            sk = ("sg", sb_)
            P.act(lambda e: e.activation(out=sg[sb_][:], in_=pg[:], func=AF.Silu), reads=[pgk], writes=[sk])
            P.dve(lambda e: e.tensor_tensor(out=hid[hb][:, j, :], in0=sg[sb_][:], in1=pu[:], op=ALU.mult), reads=[sk, puk], writes=[hk])

    def down(gi, tb, hb):
        c0, g = groups[gi]
        b = gi % 2
        kd = ("wdn", b)
        hk = ("hid", hb)
        for tt in range(4):
            t = tb * 4 + tt
            for half in range(2):
                po, pok = C.ps()
                for j in range(g):
                    P.pe(lambda e, j=j: e.matmul(po[:], lhsT=hid[hb][:, j, tt * 128:(tt + 1) * 128], rhs=wdn[b][:, j, half * 512:(half + 1) * 512],
                                                start=(j == 0), stop=(j == g - 1)), reads=[hk, kd], writes=[pok])
                P.dve(lambda e: e.tensor_tensor(out=X[:, t, half * 512:(half + 1) * 512], in0=X[:, t, half * 512:(half + 1) * 512], in1=po[:], op=ALU.add),
                      reads=[pok, ("X", t)], writes=[("X", t)])

    items = [(gi, tb) for gi in range(len(groups)) for tb in range(4)]
    load_group(0)
    prev = None
    for ii, (gi, tb) in enumerate(items):
        hb = ii % 2
        gate_up(gi, tb, hb)
        if prev is not None:
            down(*prev)
        if tb == 0 and gi + 1 < len(groups):
            load_group(gi + 1)
        prev = (gi, tb, hb)
    down(*prev)
    if final:
        fg_d = I["final_norm_g"]
        fgb = C.sb("fgb", [128, D], F32)
        P.dma("sp", fgb[:], fg_d.unsqueeze(0).to_broadcast([128, D]), writes=["fgb"])
        small = C.sb("fsmall", [128, NT], F32)
        fjunk = C.sb("fjunk", [128, D], BF16)
        for t in range(NT):
            kk = rstd_col(C, X, t, small, t, fjunk)
            P.dve(lambda e, t=t: e.scalar_tensor_tensor(out=X[:, t, :], in0=X[:, t, :], scalar=small[:, t:t + 1], in1=fgb[:],
                                                        op0=ALU.mult, op1=ALU.mult),
                  reads=[("X", t), kk, "fgb"], writes=[("X", t)])


def _ident():
    return np.eye(128, dtype=np.float32)


QR = 384
KVR = 256
SCALE = float(192 ** -0.5)
TWO_PI = float(2 * np.pi)


def psn(C, pool):
    i = pool[C.pcount.get(tuple(pool), 0) % len(pool)]
    C.pcount[tuple(pool)] = C.pcount.get(tuple(pool), 0) + 1
    return C.banks[i], ("ps", i)


def mla_body(C, R, hT, layer):
    P = C.P
    I = C.I
    j_ = layer // 2
    ng_d = I["norm_mix_g"][layer]
    win_d = I["mla_w_in"][j_]
    qg_d = I["mla_q_norm_g"][j_]
    kvg_d = I["mla_kv_norm_g"][j_]
    wuq_d = I["mla_w_uq"][j_]
    wukv_d = I["mla_w_ukv"][j_]
    wout_d = I["mla_w_out"][j_]
    pos_d = I["positions"]
    invf_d = I["invf"]
    tri_d = I["tri"]
    mod_compute(C, R, layer, [0, 1, 2], ng_d)
    X, gate_bc, ident32, identb = R["X"], R["gate_bc"], R["ident32"], R["identb"]
    norm_to_hT(C, R, hT)
    ALLB = list(range(8))

    trib = C.sb("trib", [128, 128], BF16)
    P.dma("pool", trib[:], tri_d, writes=["trib"])
    wbuf = C.sb("wbuf", [128, 8 * D], BF16)
    w_in = wbuf[:, 0:8 * 704].rearrange("p (k f) -> p k f", k=8)
    w_out = wbuf[:].rearrange("p (k f) -> p k f", k=8)
    P.dma("pool", w_in, win_d.rearrange("(k p) f -> p k f", p=128), writes=["wbuf"])
    w_uq = C.sb("w_uq", [128, 3, 1536], BF16)
    P.dma("pool", w_uq[:], wuq_d.rearrange("(k p) f -> p k f", p=128), writes=["w_uq"])
    w_ukv = C.sb("w_ukv", [128, 2, 2048], BF16)
    P.dma("pool", w_ukv[:], wukv_d.rearrange("(k p) f -> p k f", p=128), writes=["w_ukv"])
    wk2A = C.sb("wk2A", [128, 8, 128], BF16)
    wk2B = C.sb("wk2B", [128, 8, 128], BF16)
    for half in range(2):
        o_ = half * 64
        P.act(lambda e, o_=o_: e.copy(out=wk2A[:, :, o_:o_ + 64], in_=w_in[:, :, 640:704]), reads=["wbuf"], writes=["wk2A"])
        P.act(lambda e, o_=o_: e.mul(out=wk2B[:, :, o_:o_ + 32], in_=w_in[:, :, 672:704], mul=-1.0), reads=["wbuf"], writes=["wk2B"])
        P.act(lambda e, o_=o_: e.copy(out=wk2B[:, :, o_ + 32:o_ + 64], in_=w_in[:, :, 640:672]), reads=["wbuf"], writes=["wk2B"])
    wq2 = C.sb("wq2", [128, 3, 8, 128], BF16)
    wq4 = w_uq[:].rearrange("p k (h f) -> p k h f", h=8)
    P.act(lambda e: e.copy(out=wq2[:, :, :, 0:64], in_=wq4[:, :, :, 128:192]), reads=["w_uq"], writes=["wq2"])
    P.act(lambda e: e.mul(out=wq2[:, :, :, 64:96], in_=wq4[:, :, :, 160:192], mul=-1.0), reads=["w_uq"], writes=["wq2"])
    P.act(lambda e: e.copy(out=wq2[:, :, :, 96:128], in_=wq4[:, :, :, 128:160]), reads=["w_uq"], writes=["wq2"])
    qgc, qgk = load_cols(C, qg_d, 3, "qg", ident32)
    kvgc, kvgk = load_cols(C, kvg_d, 2, "kvg", ident32)
    g5 = C.sb("g5", [128, 5], F32)
    P.dve(lambda e: e.tensor_copy(out=g5[:, 0:3], in_=qgc[:]), reads=[qgk], writes=["g5"])
    P.dve(lambda e: e.tensor_copy(out=g5[:, 3:5], in_=kvgc[:]), reads=[kvgk], writes=["g5"])

    cosT = C.sb("cosT", [128, T], BF16)
    sinT = C.sb("sinT", [128, T], BF16)
    cs2 = C.sb("cs2", [128, T], BF16)
    invf = C.sb("invf", [128, 1], F32)
    P.dma("sp", invf[:], invf_d.rearrange("(p o) -> p o", o=1), writes=["invf"])
    C.scope_begin()
    posi = C.sb("posi", [128, 512], I32)
    xs = C.sb("xs", [128, 512], F32)
    ri = C.sb("ri", [128, 512], I32)
    rf = C.sb("rf", [128, 512], F32)
    for tb in range(4):
        sl = slice(tb * 512, (tb + 1) * 512)
        P.dma("sp", posi[:], pos_d[sl].unsqueeze(0).to_broadcast([128, 512]), writes=["posi"])
        P.dve(lambda e: e.tensor_copy(out=xs[:], in_=posi[:]), reads=["posi"], writes=["xs"])
        P.dve(lambda e: e.tensor_scalar(out=xs[:], in0=xs[:], scalar1=invf[:, 0:1], scalar2=1.0 / TWO_PI, op0=ALU.mult, op1=ALU.mult),
              reads=["xs", "invf"], writes=["xs"])
        for which, tab in ((0, sinT), (1, cosT)):
            if which == 1:
                P.dve(lambda e: e.tensor_scalar(out=xs[:], in0=xs[:], scalar1=0.25, scalar2=None, op0=ALU.add), reads=["xs"], writes=["xs"])
            P.dve(lambda e: e.tensor_copy(out=ri[:], in_=xs[:]), reads=["xs"], writes=["ri"])
            P.dve(lambda e: e.tensor_copy(out=rf[:], in_=ri[:]), reads=["ri"], writes=["rf"])
            P.dve(lambda e: e.tensor_tensor(out=rf[:], in0=xs[:], in1=rf[:], op=ALU.subtract), reads=["xs", "rf"], writes=["rf"])
            P.act(lambda e, tab=tab, sl=sl: e.activation(out=tab[:, sl], in_=rf[:], func=AF.Sin, scale=TWO_PI * (1.0 - 2e-7)),
                  reads=["rf"], writes=[("cs", tb)])
        P.act(lambda e, sl=sl: e.copy(out=cs2[0:64, sl], in_=cosT[0:64, sl]), reads=[("cs", tb)], writes=[("cs2", tb)])
        P.act(lambda e, sl=sl: e.copy(out=cs2[64:128, sl], in_=sinT[64:128, sl]), reads=[("cs", tb)], writes=[("cs2", tb)])
    C.scope_end()

    cT = C.sb("cT", [128, 5, T], BF16)
    C.scope_begin()
    mjunk = C.sb("mjunk", [128, 640], BF16)
    clats = [C.sb("clat%d" % i, [128, 640], F32) for i in range(2)]
    cns = [C.sb("cn%d" % i, [128, 640], BF16) for i in range(2)]
    sms = [C.sb("msmall%d" % i, [128, 4], F32) for i in range(2)]
    for t in range(NT):
        clat, cn, sm = clats[t % 2], cns[t % 2], sms[t % 2]
        kcl, kcn, ksm = ("clat", t % 2), ("cn", t % 2), ("msm", t % 2)
        pa, pak = psn(C, ALLB)
        pb_, pbk = psn(C, ALLB)
        for k in range(8):
            P.pe(lambda e, k=k, t=t, pa=pa: e.matmul(pa[:], lhsT=hT[:, k, t * 128:(t + 1) * 128], rhs=w_in[:, k, 0:512],
                                                   start=(k == 0), stop=(k == 7)), reads=[("hT", t), "wbuf"], writes=[pak])
        for k in range(8):
            P.pe(lambda e, k=k, t=t, pb_=pb_: e.matmul(pb_[:, 0:128], lhsT=hT[:, k, t * 128:(t + 1) * 128], rhs=w_in[:, k, 512:640],
                                                     start=(k == 0), stop=(k == 7)), reads=[("hT", t), "wbuf"], writes=[pbk])
        P.act(lambda e, pa=pa, clat=clat: e.copy(out=clat[:, 0:512], in_=pa[:]), reads=[pak], writes=[kcl])
        P.act(lambda e, pb_=pb_, clat=clat: e.copy(out=clat[:, 512:640], in_=pb_[:, 0:128]), reads=[pbk], writes=[kcl])
        P.act(lambda e, clat=clat, sm=sm: e.activation(out=mjunk[:, 0:384], in_=clat[:, 0:384], func=AF.Square, accum_out=sm[:, 0:1]),
              reads=[kcl], writes=["mjunk", ksm])
        P.act(lambda e, clat=clat, sm=sm: e.activation(out=mjunk[:, 384:640], in_=clat[:, 384:640], func=AF.Square, accum_out=sm[:, 1:2]),
              reads=[kcl], writes=["mjunk", ksm])
        P.act(lambda e, sm=sm: e.activation(out=sm[:, 0:1], in_=sm[:, 0:1], func=AF.Ln, scale=1.0 / QR, bias=C.epsc[:, 0:1]), reads=[ksm, "epsc"], writes=[ksm])
        P.act(lambda e, sm=sm: e.activation(out=sm[:, 1:2], in_=sm[:, 1:2], func=AF.Ln, scale=1.0 / KVR, bias=C.epsc[:, 0:1]), reads=[ksm, "epsc"], writes=[ksm])
        P.act(lambda e, sm=sm: e.activation(out=sm[:, 0:2], in_=sm[:, 0:2], func=AF.Exp, scale=-0.5), reads=[ksm], writes=[ksm])
        P.dve(lambda e, clat=clat, cn=cn, sm=sm: e.tensor_scalar(out=cn[:, 0:384], in0=clat[:, 0:384], scalar1=sm[:, 0:1], scalar2=None, op0=ALU.mult),
              reads=[kcl, ksm], writes=[kcn])
        P.dve(lambda e, clat=clat, cn=cn, sm=sm: e.tensor_scalar(out=cn[:, 384:640], in0=clat[:, 384:640], scalar1=sm[:, 1:2], scalar2=None, op0=ALU.mult),
              reads=[kcl, ksm], writes=[kcn])
        pt, ptk = psn(C, ALLB)
        ptb = pt[:].bitcast(BF16)
        for c in range(5):
            P.pe(lambda e, c=c, ptb=ptb, cn=cn: e.transpose(ptb[:, c * 128:(c + 1) * 128], cn[:, c * 128:(c + 1) * 128], identb[:]),
                 reads=[kcn, "identb"], writes=[ptk])
        P.dve(lambda e, t=t, ptb=ptb: e.tensor_tensor(out=cT[:, :, t * 128:(t + 1) * 128],
                                                      in0=ptb[:, 0:640].rearrange("p (c i) -> p c i", c=5),
                                                      in1=g5[:].unsqueeze(2).to_broadcast([128, 5, 128]), op=ALU.mult),
              reads=[ptk, "g5"], writes=[("cT", t)])
    C.scope_end()
    krT = C.sb("krT", [128, T], BF16)
    C.scope_begin()
    t1 = C.sb("t1", [128, 512], F32)
    t2 = C.sb("t2", [128, 512], F32)
    for tb in range(4):
        sl = slice(tb * 512, (tb + 1) * 512)
        pA, pAk = psn(C, ALLB)
        pB, pBk = psn(C, ALLB)
        hk = [("hT", 4 * tb + i) for i in range(4)]
        for k in range(8):
            P.pe(lambda e, k=k, pA=pA, sl=sl: e.matmul(pA[:], lhsT=wk2A[:, k, :], rhs=hT[:, k, sl], start=(k == 0), stop=(k == 7)),
                 reads=hk + ["wk2A"], writes=[pAk])
        for k in range(8):
            P.pe(lambda e, k=k, pB=pB, sl=sl: e.matmul(pB[:], lhsT=wk2B[:, k, :], rhs=hT[:, k, sl], start=(k == 0), stop=(k == 7)),
                 reads=hk + ["wk2B"], writes=[pBk])
        P.dve(lambda e, pA=pA, sl=sl: e.tensor_tensor(out=t1[:], in0=pA[:], in1=cosT[:, sl], op=ALU.mult), reads=[pAk, ("cs", tb)], writes=["t1"])
        P.dve(lambda e, pB=pB, sl=sl: e.tensor_tensor(out=t2[:], in0=pB[:], in1=sinT[:, sl], op=ALU.mult), reads=[pBk, ("cs", tb)], writes=["t2"])
        P.dve(lambda e, sl=sl: e.tensor_tensor(out=krT[:, sl], in0=t1[:], in1=t2[:], op=ALU.add), reads=["t1", "t2"], writes=[("krT", tb)])
    C.scope_end()

    P.dma("pool", w_out, wout_d.rearrange("(k p) f -> p k f", p=128), writes=["wbuf"])
    P.pool(lambda e: e.tensor_tensor(out=w_out, in0=w_out, in1=gate_bc[:].unsqueeze(1).to_broadcast([128, 8, D]), op=ALU.mult),
           reads=["wbuf", "gate_bc"], writes=["wbuf"])

    oT = hT
    qn = C.sb("qn", [128, T], BF16)
    qr = C.sb("qr", [128, T], BF16)
    kn = C.sb("kn", [128, T], BF16)
    V = C.sb("V", [128, NT, 128], BF16)
    onesb = C.sb("onesb", [128, 128], BF16)
    P.dve(lambda e: e.memset(onesb[:], 1.0), writes=["onesb"])
    pT = [C.sb("pT%d" % i, [128, 512], BF16) for i in range(3)]
    rec = [C.sb("rec%d" % i, [128, 512], F32) for i in range(2)]
    OB = [0, 1]
    SMB = [2, 3]
    SB_ = [4, 5]
    MB = [6, 7]
    npt = 0
    for h in range(8):
        for tb in range(4):
            sl = slice(tb * 512, (tb + 1) * 512)
            ck = [("cT", 4 * tb + i) for i in range(4)]
            p1, p1k = psn(C, MB)
            for k in range(3):
                P.pe(lambda e, k=k, p1=p1, sl=sl, h=h: e.matmul(p1[:], lhsT=w_uq[:, k, h * 192:h * 192 + 128], rhs=cT[:, k, sl],
                                                              start=(k == 0), stop=(k == 2)), reads=ck + ["w_uq"], writes=[p1k])
            P.act(lambda e, p1=p1, sl=sl: e.copy(out=qn[:, sl], in_=p1[:]), reads=[p1k], writes=[("qn", tb)])
            p2, p2k = psn(C, MB)
            for k in range(2):
                P.pe(lambda e, k=k, p2=p2, sl=sl, h=h: e.matmul(p2[:], lhsT=w_ukv[:, k, h * 256:h * 256 + 128], rhs=cT[:, 3 + k, sl],
                                                              start=(k == 0), stop=(k == 1)), reads=ck + ["w_ukv"], writes=[p2k])
            P.act(lambda e, p2=p2, sl=sl: e.copy(out=kn[:, sl], in_=p2[:]), reads=[p2k], writes=[("kn", tb)])
            pA, pAk = psn(C, MB)
            for k in range(3):
                P.pe(lambda e, k=k, pA=pA, sl=sl, h=h: e.matmul(pA[:], lhsT=wq2[:, k, h, :], rhs=cT[:, k, sl],
                                                              start=(k == 0), stop=(k == 2)), reads=ck + ["wq2"], writes=[pAk])
            P.dve(lambda e, pA=pA, sl=sl: e.tensor_tensor(out=qr[:, sl], in0=pA[:], in1=cs2[:, sl], op=ALU.mult),
                  reads=[pAk, ("cs2", tb)], writes=[("qr", tb)])
            p3, p3k = psn(C, MB)
            for i in range(4):
                t = 4 * tb + i
                for k in range(2):
                    P.pe(lambda e, k=k, p3=p3, i=i, t=t, h=h: e.matmul(p3[:, i * 128:(i + 1) * 128], lhsT=cT[:, 3 + k, t * 128:(t + 1) * 128],
                                                                     rhs=w_ukv[:, k, h * 256 + 128:h * 256 + 256], start=(k == 0), stop=(k == 1)),
                         reads=[("cT", t), "w_ukv"], writes=[p3k])
            P.act(lambda e, p3=p3, tb=tb: e.copy(out=V[:, 4 * tb:4 * tb + 4, :], in_=p3[:].rearrange("p (i d) -> p i d", i=4)),
                  reads=[p3k], writes=[("V", tb)])
        items = []
        for qb in range(4):
            for kt in range(4 * qb + 4):
                items.append((qb, kt))
        accs = {}

        def emit_scores(qb, kt):
            nonlocal npt
            q0 = max(kt, 4 * qb)
            n = (4 * qb + 4 - q0) * 128
            qsl = slice(q0 * 128, (4 * qb + 4) * 128)
            ps_, psk = psn(C, SB_)
            P.pe(lambda e: e.matmul(ps_[:, 0:n], lhsT=kn[:, kt * 128:(kt + 1) * 128], rhs=qn[:, qsl], start=True, stop=False),
                 reads=[("kn", kt // 4), ("qn", qb)], writes=[psk])
            P.pe(lambda e: e.matmul(ps_[:, 0:n], lhsT=krT[:, kt * 128:(kt + 1) * 128], rhs=qr[:, qsl], start=False, stop=True),
                 reads=[("krT", kt // 4), ("qr", qb)], writes=[psk])
            pb_i = npt % 3
            npt += 1
            ptile = pT[pb_i]
            pk_ = ("pT", pb_i)
            P.act(lambda e: e.activation(out=ptile[:, 0:n], in_=ps_[:, 0:n], func=AF.Exp, scale=SCALE), reads=[psk], writes=[pk_])
            if kt >= 4 * qb:
                P.pool(lambda e: e.tensor_tensor(out=ptile[:, 0:128], in0=ptile[:, 0:128], in1=trib[:], op=ALU.mult),
                       reads=[pk_, "trib"], writes=[pk_])
            return ptile, pk_

        def emit_pv(qb, kt, ptile, pk_):
            if kt == 0:
                accs[qb] = (psn(C, OB), psn(C, SMB))
            (po, pok), (psm, psmk) = accs[qb]
            nkt = 4 * qb + 4
            q0 = max(kt, 4 * qb)
            n = (4 * qb + 4 - q0) * 128
            off = (q0 - 4 * qb) * 128
            P.pe(lambda e: e.matmul(po[:, off:off + n], lhsT=V[:, kt, :], rhs=ptile[:, 0:n], start=(kt == 0), stop=(kt == nkt - 1)),
                 reads=[pk_, ("V", kt // 4)], writes=[pok])
            P.pe(lambda e: e.matmul(psm[:, off:off + n], lhsT=onesb[:], rhs=ptile[:, 0:n], start=(kt == 0), stop=(kt == nkt - 1)),
                 reads=[pk_, "onesb"], writes=[psmk])
            if kt == nkt - 1:
                rc = rec[qb % 2]
                rck = ("rec", qb % 2)
                P.dve(lambda e: e.reciprocal(out=rc[:], in_=psm[:]), reads=[psmk], writes=[rck])
                P.dve(lambda e: e.tensor_tensor(out=oT[:, h, qb * 512:(qb + 1) * 512], in0=po[:], in1=rc[:], op=ALU.mult),
                      reads=[pok, rck], writes=[("hT", 4 * qb + i) for i in range(4)])

        cur = emit_scores(*items[0])
        for ii, (qb, kt) in enumerate(items):
            nxt = emit_scores(*items[ii + 1]) if ii + 1 < len(items) else None
            emit_pv(qb, kt, *cur)
            cur = nxt

    for t in range(NT):
        for half in range(2):
            po, pok = psn(C, ALLB)
            for h in range(8):
                P.pe(lambda e, po=po, h=h, t=t, half=half: e.matmul(po[:], lhsT=oT[:, h, t * 128:(t + 1) * 128],
                                                                  rhs=w_out[:, h, half * 512:(half + 1) * 512], start=(h == 0), stop=(h == 7)),
                     reads=[("hT", t), "wbuf"], writes=[pok])
            P.dve(lambda e, po=po, t=t, half=half: e.tensor_tensor(out=X[:, t, half * 512:(half + 1) * 512],
                                                                  in0=X[:, t, half * 512:(half + 1) * 512], in1=po[:], op=ALU.add),
                  reads=[pok, ("X", t)], writes=[("X", t)])


def _invf64():
    f = (10000.0 ** (-np.arange(0, 64, 2, dtype=np.float32) / np.float32(64))).astype(np.float32)
    return np.concatenate([f, f, f, f]).astype(np.float32)


def _tri():
    k = np.arange(128)[:, None]
    q = np.arange(128)[None, :]
    return (k <= q).astype(np.float32)


HG = 2
NMASK = 19


def _gmasks():
    idx = np.arange(128)
    p = idx[:, None]
    f = idx[None, :]
    m = []
    m.append((p <= f).astype(np.float32))
    m.append((p > f).astype(np.float32))
    m.append((f < p).astype(np.float32))
    m.append(np.where(f <= p, 0.0, -30000.0).astype(np.float32))
    m.append(np.zeros((128, 128), np.float32))
    b = 1
    while b < 128:
        blk = idx // b
        mU = ((blk[:, None] // 2) == (blk[None, :] // 2)) & ((blk[:, None] % 2) == 0) & ((blk[None, :] % 2) == 1)
        m.append(-mU.astype(np.float32))
        m.append(-mU.T.astype(np.float32))
        b *= 2
    return np.stack(m).astype(np.float32)


def gdn_body(C, R, hT, layer):
    P = C.P
    I = C.I
    j_ = layer // 2
    ng_d = I["norm_mix_g"][layer]
    win_d = I["gdn_w_in"][j_]
    conv_d = I["gdn_conv_w"][j_]
    alog_d = I["gdn_a_log"][j_]
    dtb_d = I["gdn_dt_bias"][j_]
    gng_d = I["gdn_norm_g"][j_]
    wout_d = I["gdn_w_out"][j_]
    gm_d = I["gmasks"]
    mod_compute(C, R, layer, [0, 1, 2], ng_d)
    X, gate_bc, ident32, identb, ones32 = R["X"], R["gate_bc"], R["ident32"], R["identb"], R["ones32"]
    norm_to_hT(C, R, hT)
    ALLB = list(range(8))
    winv = win_d.rearrange("(k p) f -> p k f", p=128)

    tri32 = C.sb("tri32", [128, 2, 128], F32)
    P.dma("sp", tri32[:], gm_d[0:2].rearrange("m p f -> p m f"), writes=["tri32"])
    mk32 = C.sb("mk32", [128, 2, 128], F32)
    P.dma("sp", mk32[:], gm_d[2:4].rearrange("m p f -> p m f"), writes=["mk32"])
    lvm = C.sb("lvm", [128, 14, 128], BF16)
    P.dma("pool", lvm[:], gm_d[5:19].rearrange("m p f -> p m f"), writes=["lvm"])
    onesb = C.sb("onesb", [128, 128], BF16)
    P.dve(lambda e: e.memset(onesb[:], 1.0), writes=["onesb"])
    cw = C.sb("cw", [128, 24, 4], F32)
    cvv = conv_d.rearrange("(c p) k -> p c k", p=128)
    for c_ in range(24):
        P.dma("sp", cw[:, c_, :], cvv[:, c_, :], writes=["cw"])
    gnb = C.sb("gnb", [128, 128], F32)
    P.dma("sp", gnb[:], gng_d.unsqueeze(0).to_broadcast([128, 128]), writes=["gnb"])
    alb = C.sb("alb", [128, 8], F32)
    dtb = C.sb("dtb", [128, 8], F32)
    P.dma("sp", alb[:], alog_d.unsqueeze(0).to_broadcast([128, 8]), writes=["alb"])
    P.dma("sp", dtb[:], dtb_d.unsqueeze(0).to_broadcast([128, 8]), writes=["dtb"])

    col = {nm: C.sb("col_" + nm, [128, NT, 8], F32) for nm in ("g", "beta", "gc", "egc", "cb", "edec", "gl")}
    C.scope_begin()
    wab = C.sb("wab", [128, 8, 16], BF16)
    P.dma("pool", wab[:], winv[:, :, 4096:4112], writes=["wab"])
    abv = C.sb("abv", [128, NT, 16], F32)
    tA = C.sb("tA", [128, NT, 8], F32)
    tB = C.sb("tB", [128, NT, 8], F32)
    pab, pabk = psn(C, ALLB)
    for t in range(NT):
        for k in range(8):
            P.pe(lambda e, t=t, k=k: e.matmul(pab[:, t * 16:(t + 1) * 16], lhsT=hT[:, k, t * 128:(t + 1) * 128], rhs=wab[:, k, :],
                                              start=(k == 0), stop=(k == 7)), reads=[("hT", t), "wab"], writes=[pabk])
    P.act(lambda e: e.copy(out=abv[:].rearrange("p t c -> p (t c)"), in_=pab[:, 0:256]), reads=[pabk], writes=["abv"])
    P.dve(lambda e: e.tensor_tensor(out=tA[:], in0=abv[:, :, 0:8], in1=dtb[:].unsqueeze(1).to_broadcast([128, NT, 8]), op=ALU.add),
          reads=["abv", "dtb"], writes=["tA"])
    P.act(lambda e: e.activation(out=tB[:], in_=tA[:], func=AF.Abs), reads=["tA"], writes=["tB"])
    P.act(lambda e: e.activation(out=tB[:], in_=tB[:], func=AF.Exp, scale=-1.0), reads=["tB"], writes=["tB"])
    P.act(lambda e: e.activation(out=tB[:], in_=tB[:], func=AF.Ln, scale=1.0, bias=1.0), reads=["tB"], writes=["tB"])
    P.dve(lambda e: e.tensor_scalar(out=tA[:], in0=tA[:], scalar1=0.0, scalar2=None, op0=ALU.max), reads=["tA"], writes=["tA"])
    P.dve(lambda e: e.tensor_tensor(out=tA[:], in0=tA[:], in1=tB[:], op=ALU.add), reads=["tA", "tB"], writes=["tA"])
    P.act(lambda e: e.activation(out=alb[:], in_=alb[:], func=AF.Exp), reads=["alb"], writes=["alb"])
    P.dve(lambda e: e.scalar_tensor_tensor(out=col["g"][:], in0=tA[:], scalar=-1.0, in1=alb[:].unsqueeze(1).to_broadcast([128, NT, 8]),
                                           op0=ALU.mult, op1=ALU.mult), reads=["tA", "alb"], writes=["col_g"])
    P.act(lambda e: e.activation(out=col["beta"][:], in_=abv[:, :, 8:16], func=AF.Exp, scale=-1.0), reads=["abv"], writes=["col_beta"])
    P.act(lambda e: e.activation(out=col["beta"][:], in_=col["beta"][:], func=AF.Ln, scale=1.0, bias=1.0), reads=["col_beta"], writes=["col_beta"])
    P.act(lambda e: e.activation(out=col["beta"][:], in_=col["beta"][:], func=AF.Exp, scale=-1.0), reads=["col_beta"], writes=["col_beta"])
    pgc, pgck = psn(C, ALLB)
    prc, prck = psn(C, ALLB)
    ptt, pttk = psn(C, ALLB)
    for n in range(NT):
        P.pe(lambda e, n=n: e.matmul(pgc[:, n * 8:(n + 1) * 8], lhsT=tri32[:, 0, :], rhs=col["g"][:, n, :], start=True, stop=True),
             reads=["tri32", "col_g"], writes=[pgck])
        P.pe(lambda e, n=n: e.matmul(prc[:, n * 8:(n + 1) * 8], lhsT=tri32[:, 1, :], rhs=col["g"][:, n, :], start=True, stop=True),
             reads=["tri32", "col_g"], writes=[prck])
        P.pe(lambda e, n=n: e.matmul(ptt[:, n * 8:(n + 1) * 8], lhsT=ones32[:], rhs=col["g"][:, n, :], start=True, stop=True),
             reads=["ones32", "col_g"], writes=[pttk])
    fl = lambda t_: t_[:].rearrange("p t c -> p (t c)")
    P.act(lambda e: e.copy(out=fl(col["gc"]), in_=pgc[:, 0:128]), reads=[pgck], writes=["col_gc"])
    P.act(lambda e: e.activation(out=fl(col["egc"]), in_=pgc[:, 0:128], func=AF.Exp), reads=[pgck], writes=["col_egc"])
    P.act(lambda e: e.activation(out=fl(col["edec"]), in_=prc[:, 0:128], func=AF.Exp), reads=[prck], writes=["col_edec"])
    P.act(lambda e: e.activation(out=fl(col["gl"]), in_=ptt[:, 0:128], func=AF.Exp), reads=[pttk], writes=["col_gl"])
    P.dve(lambda e: e.scalar_tensor_tensor(out=col["cb"][:], in0=col["egc"][:], scalar=-1.0, in1=col["beta"][:], op0=ALU.mult, op1=ALU.mult),
          reads=["col_egc", "col_beta"], writes=["col_cb"])
    C.scope_end()

    W = HG * 128
    qT = C.sb("qT", [128, HG, T], BF16)
    kT = C.sb("kT", [128, HG, T], BF16)
    vb = C.sb("vb", [128, NT, HG, 128], BF16)
    sgA = C.sb("sgA", [128, NT, W], BF16)

    def bc_h(ap2):
        return ap2.unsqueeze(1).to_broadcast([128, HG, 128])

    def v3(ap):
        return ap.rearrange("p (h i) -> p h i", h=HG)

    for gI in range(8 // HG):
        h0 = gI * HG
        C.scope_begin()
        wgt = C.sb("wgt", [128, 8, W], BF16)
        P.dma("pool", wgt[:], winv[:, :, 3072 + h0 * 128:3072 + (h0 + HG) * 128], writes=["wgt"])
        raw = [C.sb("raw%d" % i, [128, T + 4], BF16) for i in range(2)]
        dcw = [C.sb("dcw%d" % i, [128, 4, 128], BF16) for i in range(2)]
        y32 = C.sb("y32", [128, T], F32)
        sqb = [C.sb("sqb%d" % i, [128, 512], BF16) for i in range(2)]
        rn = [C.sb("rn%d" % i, [128, 512], F32) for i in range(2)]
        vTb = [C.sb("vTb%d" % i, [128, 512], BF16) for i in range(2)]
        wsl = [C.sb("wsl%d" % i, [128, 8, 128], BF16) for i in range(2)]
        for i in range(2):
            P.dve(lambda e, i=i: e.memset(raw[i][:, 0:3], 0.0), writes=[("rawpad", i)])
        for t in range(NT):
            pg_, pgk_ = psn(C, ALLB)
            for k in range(8):
                P.pe(lambda e, pg_=pg_, k=k, t=t: e.matmul(pg_[:, 0:W], lhsT=hT[:, k, t * 128:(t + 1) * 128], rhs=wgt[:, k, :], start=(k == 0), stop=(k == 7)),
                     reads=[("hT", t), "wgt"], writes=[pgk_])
            P.act(lambda e, pg_=pg_, t=t: e.activation(out=sgA[:, t, :], in_=pg_[:, 0:W], func=AF.Silu), reads=[pgk_], writes=[("sgA", t)])
        nw = 0
        nblk = 0
        for hl in range(HG):
            h = h0 + hl
            for typ in (2, 0, 1):
                cidx = typ * 8 + h
                ib = nw % 2
                wb = wsl[ib]
                wk = ("wsl", ib)
                rw = raw[ib]
                dc = dcw[ib]
                nw += 1
                P.dma("pool", wb[:], winv[:, :, typ * 1024 + h * 128:typ * 1024 + (h + 1) * 128], writes=[wk])
                P.dve(lambda e, dc=dc, cidx=cidx: e.tensor_tensor(out=dc[:], in0=identb[:].unsqueeze(1).to_broadcast([128, 4, 128]),
                                                                 in1=cw[:, cidx, :].unsqueeze(2).to_broadcast([128, 4, 128]), op=ALU.mult),
                      reads=["identb", "cw"], writes=[("dcw", ib)])
                for tb in range(4):
                    sl = slice(tb * 512, (tb + 1) * 512)
                    pp, ppk = psn(C, ALLB)
                    for k in range(8):
                        P.pe(lambda e, pp=pp, wb=wb, k=k, sl=sl: e.matmul(pp[:], lhsT=wb[:, k, :], rhs=hT[:, k, sl], start=(k == 0), stop=(k == 7)),
                             reads=[wk] + [("hT", 4 * tb + i) for i in range(4)], writes=[ppk])
                    P.act(lambda e, pp=pp, tb=tb, rw=rw: e.copy(out=rw[:, 3 + tb * 512:3 + (tb + 1) * 512], in_=pp[:]), reads=[ppk], writes=[("raw", ib, tb)])
                    pc, pck = psn(C, ALLB)
                    rkeys = [("raw", ib, tb), ("rawpad", ib)] + ([("raw", ib, tb - 1)] if tb > 0 else [])
                    for j in range(4):
                        P.pe(lambda e, pc=pc, dc=dc, rw=rw, j=j, tb=tb: e.matmul(pc[:], lhsT=dc[:, j, :], rhs=rw[:, tb * 512 + j:tb * 512 + j + 512],
                                                                              start=(j == 0), stop=(j == 3)),
                             reads=rkeys + [("dcw", ib)], writes=[pck])
                    if typ == 2:
                        vt = vTb[nblk % 2]
                        vk = ("vTb", nblk % 2)
                        nblk += 1
                        P.act(lambda e, pc=pc, vt=vt: e.activation(out=vt[:], in_=pc[:], func=AF.Silu), reads=[pck], writes=[vk])
                        pt_, ptk_ = psn(C, ALLB)
                        ptb = pt_[:].bitcast(BF16)
                        for i in range(4):
                            P.pe(lambda e, ptb=ptb, i=i, vt=vt: e.transpose(ptb[:, i * 128:(i + 1) * 128], vt[:, i * 128:(i + 1) * 128], identb[:]),
                                 reads=[vk, "identb"], writes=[ptk_])
                        P.dve(lambda e, ptb=ptb, tb=tb, hl=hl, h=h: e.tensor_tensor(
                            out=vb[:, 4 * tb:4 * tb + 4, hl, :], in0=ptb[:, 0:512].rearrange("p (t d) -> p t d", t=4),
                            in1=col["beta"][:, 4 * tb:4 * tb + 4, h:h + 1].to_broadcast([128, 4, 128]), op=ALU.mult),
                            reads=[ptk_, "col_beta"], writes=[("vb", tb)])
                    else:
                        P.act(lambda e, pc=pc, sl=sl: e.activation(out=y32[:, sl], in_=pc[:], func=AF.Silu), reads=[pck], writes=[("y32", tb)])
                if typ != 2:
                    dst = qT if typ == 0 else kT
                    dkey = "qT" if typ == 0 else "kT"
                    sc = float(128 ** -0.5) if typ == 0 else 1.0
                    for tb in range(4):
                        sl = slice(tb * 512, (tb + 1) * 512)
                        sb_ = sqb[tb % 2]
                        sk_ = ("sqb", tb % 2)
                        rb_ = rn[tb % 2]
                        rk_ = ("rn", tb % 2)
                        P.pool(lambda e, sb_=sb_, sl=sl: e.tensor_tensor(out=sb_[:], in0=y32[:, sl], in1=y32[:, sl], op=ALU.mult), reads=[("y32", tb)], writes=[sk_])
                        pp, ppk = psn(C, ALLB)
                        P.pe(lambda e, pp=pp, sb_=sb_: e.matmul(pp[:], lhsT=onesb[:], rhs=sb_[:], start=True, stop=True), reads=["onesb", sk_], writes=[ppk])
                        P.act(lambda e, pp=pp, rb_=rb_: e.activation(out=rb_[:], in_=pp[:], func=AF.Ln, scale=1.0, bias=C.epsc[:, 0:1]), reads=[ppk, "epsc"], writes=[rk_])
                        P.act(lambda e, rb_=rb_: e.activation(out=rb_[:], in_=rb_[:], func=AF.Exp, scale=-0.5), reads=[rk_], writes=[rk_])
                        P.dve(lambda e, dst=dst, hl=hl, sl=sl, sc=sc, rb_=rb_: e.scalar_tensor_tensor(out=dst[:, hl, sl], in0=y32[:, sl], scalar=sc, in1=rb_[:],
                                                                                                op0=ALU.mult, op1=ALU.mult),
                              reads=[("y32", tb), rk_], writes=[(dkey, tb)])
        C.scope_end()

        C.scope_begin()
        wog = C.sb("wog", [128, HG, D], BF16)
        P.dma("pool", wog[:], wout_d.rearrange("(c p) n -> p c n", p=128)[:, h0:h0 + HG, :], writes=["wog"])
        P.pool(lambda e: e.tensor_tensor(out=wog[:], in0=wog[:], in1=gate_bc[:].unsqueeze(1).to_broadcast([128, HG, D]), op=ALU.mult),
               reads=["wog", "gate_bc"], writes=["wog"])
        S32 = C.sb("S32", [128, W], F32)
        Sbf = C.sb("Sbf", [128, W], BF16)
        P.dve(lambda e: e.memset(S32[:], 0.0), writes=["S32"])
        P.dve(lambda e: e.memset(Sbf[:], 0.0), writes=["Sbf"])
        f32t = lambda nm: C.sb(nm, [128, W], F32)
        bft = lambda nm: C.sb(nm, [128, W], BF16)
        NPRE = 1
        NIF = NPRE + 1
        TS = []
        for i in range(NPRE):
            d_ = {}
            for x in ("dgc", "d2", "dec", "egr", "bsl", "tL", "tU"):
                d_[x] = f32t(x)
            for x in ("L", "U", "attn", "Mm", "Tt0", "Tt1", "Tm0", "Tm1"):
                d_[x] = bft(x)
            TS.append(d_)
        TtF = [bft("TtF%d" % i) for i in range(NIF)]
        attnT = [bft("attnT%d" % i) for i in range(NIF)]
        qdT = [bft("qdT%d" % i) for i in range(NIF)]
        Rr, vnew, vdec, ktok, og, ogT = [bft(x) for x in ("Rr", "vnew", "vdec", "ktok", "og", "ogT")]
        tR, o32, osq = [f32t(x) for x in ("tR", "o32", "osq")]
        oss = C.sb("oss", [128, HG], F32)
        identb_h = identb[:].unsqueeze(1).to_broadcast([128, HG, 128])
        PB_PREP = [0, 1, 2, 3, 4]
        PB_REC = [5, 6]
        PB_OUT = [7]
        obank = {}

        def colb(nm, n):
            return col[nm][:, n, h0:h0 + HG].unsqueeze(2).to_broadcast([128, HG, 128])

        def tr_heads(S, dst_ap, src, skey, dkey, pool):
            pt_, ptk_ = psn(C, pool)
            ptb = pt_[:].bitcast(BF16)
            for hl in range(HG):
                hs = slice(hl * 128, (hl + 1) * 128)
                S.pe(lambda e, ptb=ptb, hs=hs: e.transpose(ptb[:, hs], src[:, hs], identb[:]), reads=[skey, "identb"], writes=[ptk_])
            S.act(lambda e, ptb=ptb: e.copy(out=dst_ap, in_=ptb[:, 0:W]), reads=[ptk_], writes=[dkey])

        def prep(n):
            S = Stream()
            csl = slice(n * 128, (n + 1) * 128)
            pb = n % NIF
            ts = n % NPRE
            D_ = TS[ts]
            dgc, d2, dec, egr, bsl, tL, tU = [D_[x] for x in ("dgc", "d2", "dec", "egr", "bsl", "tL", "tU")]
            L_, U_, attn, Mm = [D_[x] for x in ("L", "U", "attn", "Mm")]
            Tt = [D_["Tt0"], D_["Tt1"]]
            Tm = [D_["Tm0"], D_["Tm1"]]
            K_ = lambda nm: (nm, "ts", ts)
            S.dve(lambda e: e.tensor_tensor(out=v3(dgc[:]), in0=bc_h(ident32[:]), in1=colb("gc", n), op=ALU.mult), reads=["ident32", "col_gc"], writes=[K_("dgc")])
            pgr, pgrk = psn(C, PB_PREP)
            S.pe(lambda e: e.matmul(pgr[:, 0:W], lhsT=ones32[:], rhs=dgc[:], start=True, stop=True), reads=["ones32", K_("dgc")], writes=[pgrk])
            S.dve(lambda e: e.scalar_tensor_tensor(out=v3(d2[:]), in0=v3(pgr[:, 0:W]), scalar=-1.0, in1=colb("gc", n), op0=ALU.mult, op1=ALU.add),
                  reads=[pgrk, "col_gc"], writes=[K_("d2")])
            S.dve(lambda e: e.tensor_tensor(out=v3(d2[:]), in0=v3(d2[:]), in1=bc_h(mk32[:, 1, :]), op=ALU.add), reads=[K_("d2"), "mk32"], writes=[K_("d2")])
            S.act(lambda e: e.activation(out=dec[:], in_=d2[:], func=AF.Exp), reads=[K_("d2")], writes=[K_("dec")])
            S.act(lambda e: e.activation(out=egr[:], in_=pgr[:, 0:W], func=AF.Exp), reads=[pgrk], writes=[K_("egr")])
            S.dve(lambda e: e.tensor_tensor(out=v3(bsl[:]), in0=bc_h(mk32[:, 0, :]), in1=colb("beta", n), op=ALU.mult), reads=["mk32", "col_beta"], writes=[K_("bsl")])
            pkk, pkkk = psn(C, PB_PREP)
            pqk, pqkk = psn(C, PB_PREP)
            for hl in range(HG):
                S.pe(lambda e, hl=hl: e.matmul(pkk[:, hl * 128:(hl + 1) * 128], lhsT=kT[:, hl, csl], rhs=kT[:, hl, csl], start=True, stop=True),
                     reads=[("kT", n // 4)], writes=[pkkk])
            for hl in range(HG):
                S.pe(lambda e, hl=hl: e.matmul(pqk[:, hl * 128:(hl + 1) * 128], lhsT=qT[:, hl, csl], rhs=kT[:, hl, csl], start=True, stop=True),
                     reads=[("kT", n // 4), ("qT", n // 4)], writes=[pqkk])
            S.dve(lambda e: e.tensor_tensor(out=tL[:], in0=pkk[:, 0:W], in1=dec[:], op=ALU.mult), reads=[pkkk, K_("dec")], writes=[K_("tL")])
            S.dve(lambda e: e.tensor_tensor(out=L_[:], in0=tL[:], in1=bsl[:], op=ALU.mult), reads=[K_("tL"), K_("bsl")], writes=[K_("L")])
            S.dve(lambda e: e.tensor_tensor(out=attn[:], in0=pqk[:, 0:W], in1=dec[:], op=ALU.mult), reads=[pqkk, K_("dec")], writes=[K_("attn")])
            S.dve(lambda e: e.tensor_tensor(out=v3(qdT[pb][:]), in0=qT[:, :, csl], in1=v3(egr[:]), op=ALU.mult), reads=[("qT", n // 4), K_("egr")], writes=[("qdT", pb)])
            tr_heads(S, U_[:], L_, K_("L"), K_("U"), PB_PREP)
            tr_heads(S, attnT[pb][:], attn, K_("attn"), ("attnT", pb), PB_PREP)
            S.dve(lambda e: e.tensor_tensor(out=v3(tU[:]), in0=v3(U_[:]), in1=bc_h(lvm[:, 0, :]), op=ALU.mult), reads=[K_("U"), "lvm"], writes=[K_("tU")])
            S.dve(lambda e: e.tensor_tensor(out=v3(Tt[0][:]), in0=v3(tU[:]), in1=identb_h, op=ALU.add), reads=[K_("tU"), "identb"], writes=[("Tt", ts, 0)])
            tr_heads(S, Tm[0][:], Tt[0], ("Tt", ts, 0), ("Tm", ts, 0), PB_PREP)
            cur = 0
            for lv in range(1, 7):
                nxt = 1 - cur
                last = (lv == 6)
                tt_c, tm_c = Tt[cur], Tm[cur]
                pm, pmk = psn(C, PB_PREP)
                for hl in range(HG):
                    hs = slice(hl * 128, (hl + 1) * 128)
                    S.pe(lambda e, pm=pm, hs=hs, tt_c=tt_c: e.matmul(pm[:, hs], lhsT=L_[:, hs], rhs=tt_c[:, hs], start=True, stop=True),
                         reads=[K_("L"), ("Tt", ts, cur)], writes=[pmk])
                S.dve(lambda e, pm=pm, lv=lv: e.tensor_tensor(out=v3(Mm[:]), in0=v3(pm[:, 0:W]), in1=bc_h(lvm[:, 2 * lv, :]), op=ALU.mult),
                      reads=[pmk, "lvm"], writes=[K_("Mm")])
                pt2, pt2k = psn(C, PB_PREP)
                S.pe(lambda e, pt2=pt2, tt_c=tt_c: e.matmul(pt2[:, 0:W], lhsT=identb[:], rhs=tt_c[:], start=True, stop=False),
                     reads=["identb", ("Tt", ts, cur)], writes=[pt2k])
                for hl in range(HG):
                    hs = slice(hl * 128, (hl + 1) * 128)
                    S.pe(lambda e, pt2=pt2, hs=hs, tm_c=tm_c, hl=hl: e.matmul(pt2[:, hs], lhsT=tm_c[:, hs], rhs=Mm[:, hs], start=False, stop=(hl == HG - 1),
                                                                            skip_group_check=True),
                         reads=[("Tm", ts, cur), K_("Mm")], writes=[pt2k])
                if last:
                    S.act(lambda e, pt2=pt2: e.copy(out=TtF[pb][:], in_=pt2[:, 0:W]), reads=[pt2k], writes=[("TtF", pb)])
                else:
                    S.act(lambda e, pt2=pt2, nxt=nxt: e.copy(out=Tt[nxt][:], in_=pt2[:, 0:W]), reads=[pt2k], writes=[("Tt", ts, nxt)])
                    tr_heads(S, Tm[nxt][:], Tt[nxt], ("Tt", ts, nxt), ("Tm", ts, nxt), PB_PREP)
                cur = nxt
            return S

        def rec(n):
            S = Stream()
            csl = slice(n * 128, (n + 1) * 128)
            pb = n % NIF
            ttf = TtF[pb]
            pks, pksk = psn(C, PB_REC)
            for hl in range(HG):
                hs = slice(hl * 128, (hl + 1) * 128)
                S.pe(lambda e, hs=hs, hl=hl: e.matmul(pks[:, hs], lhsT=kT[:, hl, csl], rhs=Sbf[:, hs], start=True, stop=True),
                     reads=[("kT", n // 4), "Sbf"], writes=[pksk])
            for hl in range(HG):
                hs = slice(hl * 128, (hl + 1) * 128)
                S.dve(lambda e, hs=hs, hl=hl: e.scalar_tensor_tensor(out=Rr[:, hs], in0=pks[:, hs], scalar=col["cb"][:, n, h0 + hl:h0 + hl + 1], in1=vb[:, n, hl, :],
                                                                     op0=ALU.mult, op1=ALU.add), reads=[pksk, "col_cb", ("vb", n // 4)], writes=["Rr"])
            pvn, pvnk = psn(C, PB_REC)
            for hl in range(HG):
                hs = slice(hl * 128, (hl + 1) * 128)
                S.pe(lambda e, hs=hs: e.matmul(pvn[:, hs], lhsT=ttf[:, hs], rhs=Rr[:, hs], start=True, stop=True),
                     reads=[("TtF", pb), "Rr"], writes=[pvnk])
            S.act(lambda e: e.copy(out=vnew[:], in_=pvn[:, 0:W]), reads=[pvnk], writes=["vnew"])
            S.dve(lambda e: e.tensor_tensor(out=v3(vdec[:]), in0=v3(pvn[:, 0:W]), in1=colb("edec", n), op=ALU.mult), reads=[pvnk, "col_edec"], writes=["vdec"])
            tr_heads_k(S, n)
            pss, pssk = psn(C, PB_REC)
            for hl in range(HG):
                hs = slice(hl * 128, (hl + 1) * 128)
                S.pe(lambda e, hs=hs: e.matmul(pss[:, hs], lhsT=ktok[:, hs], rhs=vdec[:, hs], start=True, stop=True),
                     reads=["ktok", "vdec"], writes=[pssk])
            po_, pok_ = psn(C, PB_REC)
            for hl in range(HG):
                hs = slice(hl * 128, (hl + 1) * 128)
                S.pe(lambda e, hs=hs: e.matmul(po_[:, hs], lhsT=qdT[pb][:, hs], rhs=Sbf[:, hs], start=True, stop=False),
                     reads=[("qdT", pb), "Sbf"], writes=[pok_])
                S.pe(lambda e, hs=hs: e.matmul(po_[:, hs], lhsT=attnT[pb][:, hs], rhs=vnew[:, hs], start=False, stop=True),
                     reads=[("attnT", pb), "vnew"], writes=[pok_])
            obank[n] = (po_, pok_)
            for hl in range(HG):
                hs = slice(hl * 128, (hl + 1) * 128)
                S.dve(lambda e, hs=hs, hl=hl: e.scalar_tensor_tensor(out=S32[:, hs], in0=S32[:, hs], scalar=col["gl"][:, n, h0 + hl:h0 + hl + 1], in1=pss[:, hs],
                                                                     op0=ALU.mult, op1=ALU.add), reads=["S32", "col_gl", pssk], writes=["S32"])
            S.act(lambda e: e.copy(out=Sbf[:], in_=S32[:]), reads=["S32"], writes=["Sbf"])
            return S

        def tr_heads_k(S, n):
            csl = slice(n * 128, (n + 1) * 128)
            pkt, pktk = psn(C, PB_REC)
            pktb = pkt[:].bitcast(BF16)
            for hl in range(HG):
                S.pe(lambda e, hl=hl: e.transpose(pktb[:, hl * 128:(hl + 1) * 128], kT[:, hl, csl], identb[:]),
                     reads=[("kT", n // 4), "identb"], writes=[pktk])
            S.act(lambda e: e.copy(out=ktok[:], in_=pktb[:, 0:W]), reads=[pktk], writes=["ktok"])

        def outp(n):
            S = Stream()
            po_, pok_ = obank.pop(n)
            S.act(lambda e: e.copy(out=o32[:], in_=po_[:, 0:W]), reads=[pok_], writes=["o32"])
            S.act(lambda e: e.activation(out=osq[:], in_=o32[:], func=AF.Square), reads=["o32"], writes=["osq"])
            S.dve(lambda e: e.tensor_reduce(out=oss[:], in_=v3(osq[:]), axis=AX.X, op=ALU.add), reads=["osq"], writes=["oss"])
            S.act(lambda e: e.activation(out=oss[:], in_=oss[:], func=AF.Ln, scale=1.0 / 128, bias=C.epsc[:, 0:1]), reads=["oss", "epsc"], writes=["oss"])
            S.act(lambda e: e.activation(out=oss[:], in_=oss[:], func=AF.Exp, scale=-0.5), reads=["oss"], writes=["oss"])
            S.dve(lambda e: e.tensor_tensor(out=v3(o32[:]), in0=v3(o32[:]), in1=oss[:].unsqueeze(2).to_broadcast([128, HG, 128]), op=ALU.mult),
                  reads=["o32", "oss"], writes=["o32"])
            S.dve(lambda e: e.tensor_tensor(out=v3(o32[:]), in0=v3(o32[:]), in1=bc_h(gnb[:]), op=ALU.mult), reads=["o32", "gnb"], writes=["o32"])
            S.dve(lambda e: e.tensor_tensor(out=og[:], in0=o32[:], in1=sgA[:, n, :], op=ALU.mult), reads=["o32", ("sgA", n)], writes=["og"])
            tr_heads(S, ogT[:], og, "og", "ogT", PB_OUT)
            for half in range(2):
                py, pyk = psn(C, PB_OUT)
                for hl in range(HG):
                    hs = slice(hl * 128, (hl + 1) * 128)
                    S.pe(lambda e, py=py, hs=hs, hl=hl, half=half: e.matmul(py[:], lhsT=ogT[:, hs], rhs=wog[:, hl, half * 512:(half + 1) * 512],
                                                                          start=(hl == 0), stop=(hl == HG - 1)), reads=["ogT", "wog"], writes=[pyk])
                S.dve(lambda e, py=py, half=half: e.tensor_tensor(out=X[:, n, half * 512:(half + 1) * 512], in0=X[:, n, half * 512:(half + 1) * 512],
                                                                 in1=py[:], op=ALU.add), reads=[pyk, ("X", n)], writes=[("X", n)])
            return S

        pend = {}

        def prep_slices(m):
            ops = prep(m).ops
            k = (len(ops) + NPRE - 1) // NPRE
            out_ = []
            for i in range(NPRE):
                st = Stream()
                st.ops = ops[i * k:(i + 1) * k]
                out_.append(st)
            return out_

        for r in range(-NPRE, NT + 1):
            streams = []
            if 0 <= r < NT:
                streams.append(rec(r))
            if 1 <= r <= NT:
                streams.append(outp(r - 1))
            for m in range(r + 1, r + NPRE + 1):
                if 0 <= m < NT:
                    if m not in pend:
                        pend[m] = prep_slices(m)
                    streams.append(pend[m][r - m + NPRE])
            merge_streams(P, streams)
        C.scope_end()


def build_fused(parts=None):
    if parts is None:
        parts = []
        for layer in range(4):
            parts.append(("gdn" if layer % 2 == 0 else "mla", layer))
            parts.append(("ffn", layer))
    C = Ctx()
    hT = C.sb("hT", [128, 8, T], BF16)
    R = setup(C)
    for kind, layer in parts:
        C.scope_begin()
        if kind == "gdn":
            gdn_body(C, R, hT, layer)
        elif kind == "mla":
            mla_body(C, R, hT, layer)
        else:
            ffn_body(C, R, hT, layer, layer == 3)
        C.scope_end()
    store_x(C, R["X"])
    C.P.finish()
    C.P.emit()
    return C.nc


def make_maps(inp):
    consts = {"ident": _ident(), "invf": _invf64(), "tri": _tri(), "gmasks": _gmasks()}
    shared = {}
    for nm in INPUT_SHAPES:
        if nm in ("x", "c", "positions") or nm in consts:
            continue
        shared[nm] = np.ascontiguousarray(np.asarray(inp[nm]), dtype=np.float32)
    maps = []
    for b in range(8):
        m = dict(shared)
        m.update(consts)
        m["x"] = np.ascontiguousarray(inp["x"][b], dtype=np.float32)
        m["c"] = np.ascontiguousarray(inp["c"][b], dtype=np.float32)
        m["positions"] = np.ascontiguousarray(inp["positions"][b]).astype(np.int32)
        maps.append(m)
    return maps


_NC = {}


def kernel(**inputs):
    inp = {k: np.asarray(v) for k, v in inputs.items()}
    if "nc" not in _NC:
        _NC["nc"] = build_fused()
    r = run_bass_kernel_spmd(_NC["nc"], make_maps(inp), core_ids=list(range(8)))
    return np.ascontiguousarray(np.stack([r.results[b]["out"] for b in range(8)], axis=0), dtype=np.float32)
```

```python
from contextlib import ExitStack
import numpy as np
import concourse.bass as bass
import concourse.mybir as mybir
from concourse.bass_utils import run_bass_kernel_spmd

F32 = mybir.dt.float32
BF16 = mybir.dt.bfloat16
I32 = mybir.dt.int32
AF = mybir.ActivationFunctionType
ALU = mybir.AluOpType
AX = mybir.AxisListType

ENGS = ["pe", "act", "dve", "pool", "sp"]
NDMASEM = 8

D = 1024
T = 2048
NT = 16
DFF = 2816
NFC = 22
EPS = 1e-6


class Op:
    __slots__ = ("eng", "fn", "waits", "dwaits", "idx", "is_dma", "dsem", "dval", "tag", "calls")


class _Rec:
    def __init__(self):
        self.calls = []

    def __getattr__(self, name):
        def f(*a, **k):
            self.calls.append((name, a, k))
            return self
        return f


class Prog:
    def __init__(self, nc):
        self.nc = nc
        self.ops = {e: [] for e in ENGS}
        self.known = {e: {f: 0 for f in ENGS} for e in ENGS}
        self.kdma = {e: {} for e in ENGS}
        self.snaps = {e: [] for e in ENGS}
        self.last_w = {}
        self.readers = {}
        self.ndma = {e: 0 for e in ENGS}
        self.out_tokens = []
        self.milestones = {e: set() for e in ENGS}

    def _need(self, eng, tok, waits, dwaits):
        if tok is None:
            return
        if tok[0] == "e":
            _, f, idx = tok
            if f == eng and eng == "pe":
                return
            if self.known[eng][f] >= idx + 1:
                return
            waits.append((f, idx))
            self.milestones[f].add(idx)
            kn, kd = self.snaps[f][idx]
            for g in ENGS:
                if kn[g] > self.known[eng][g]:
                    self.known[eng][g] = kn[g]
            for k, v in kd.items():
                if self.kdma[eng].get(k, 0) < v:
                    self.kdma[eng][k] = v
            if self.known[eng][f] < idx + 1:
                self.known[eng][f] = idx + 1
        else:
            _, q, semi, val = tok
            key = (q, semi)
            if self.kdma[eng].get(key, 0) >= val:
                return
            dwaits.append((q, semi, val))
            self.kdma[eng][key] = val

    def op(self, eng, fn, reads=(), writes=(), dma=False, tag=None):
        rec = _Rec()
        fn(rec)
        return self.submit(eng, rec.calls, reads, writes, dma=dma, tag=tag)

    def submit(self, eng, calls, reads=(), writes=(), dma=False, tag=None):
        o = Op()
        o.eng = eng
        o.fn = True
        o.tag = tag
        o.is_dma = dma
        o.calls = calls
        assert len(o.calls) == 1
        waits, dwaits = [], []
        cand = []
        for k in reads:
            tok = self.last_w.get(k)
            if tok is not None:
                cand.append(tok)
            if isinstance(k, tuple) and k[0] == "ps":
                for rt in self.readers.get(k, ()):
                    if not (rt[0] == "e" and rt[1] == eng):
                        cand.append(rt)
        for k in writes:
            tok = self.last_w.get(k)
            if tok is not None:
                if not (tok[0] == "e" and tok[1] == eng):
                    cand.append(tok)
            for rt in self.readers.get(k, ()):
                if not (rt[0] == "e" and rt[1] == eng):
                    cand.append(rt)
        best = {}
        for tok in cand:
            kk = (tok[0], tok[1]) if tok[0] == "e" else (tok[0], tok[1], tok[2])
            if kk not in best or tok[-1] > best[kk][-1]:
                best[kk] = tok
        for tok in best.values():
            self._need(eng, tok, waits, dwaits)
        idx = len(self.ops[eng])
        o.idx = idx
        if dma:
            n = self.ndma[eng]
            self.ndma[eng] = n + 1
            semi = n % NDMASEM
            val = 16 * (n // NDMASEM + 1)
            if val > 16:
                self._need(eng, ("d", eng, semi, val - 16), waits, dwaits)
            o.dsem, o.dval = semi, val
            tok = ("d", eng, semi, val)
        else:
            tok = ("e", eng, idx)
        o.waits, o.dwaits = waits, dwaits
        self.ops[eng].append(o)
        self.snaps[eng].append((dict(self.known[eng]), dict(self.kdma[eng])))
        for k in reads:
            lst = self.readers.setdefault(k, [])
            if tok[0] == "e":
                lst[:] = [r for r in lst if not (r[0] == "e" and r[1] == tok[1])]
            lst.append(tok)
        for k in writes:
            self.last_w[k] = tok
            self.readers[k] = []
        return tok

    def pe(self, fn, reads=(), writes=()):
        return self.op("pe", fn, reads, writes)

    def act(self, fn, reads=(), writes=()):
        return self.op("act", fn, reads, writes)

    def dve(self, fn, reads=(), writes=()):
        return self.op("dve", fn, reads, writes)

    def pool(self, fn, reads=(), writes=()):
        return self.op("pool", fn, reads, writes)

    def dma(self, q, out, in_, reads=(), writes=(), is_out=False, **kw):
        tok = self.op(q, lambda e: e.dma_start(out=out, in_=in_, **kw), reads, writes, dma=True)
        if is_out:
            self.out_tokens.append(tok)
        return tok

    def barrier(self):
        toks = []
        for f in ENGS:
            for o in reversed(self.ops[f]):
                if (not o.is_dma) and o.fn is not None:
                    toks.append(("e", f, o.idx))
                    break
            n = self.ndma[f]
            for i in range(max(0, n - NDMASEM), n):
                toks.append(("d", f, i % NDMASEM, 16 * (i // NDMASEM + 1)))
        for e in ENGS:
            waits, dwaits = [], []
            for tok in toks:
                if tok[0] == "e" and tok[1] == e:
                    if e == "pe":
                        continue
                self._need(e, tok, waits, dwaits)
            if not waits and not dwaits:
                continue
            o = Op()
            o.eng = e; o.fn = None; o.waits = waits; o.dwaits = dwaits
            o.idx = len(self.ops[e]); o.is_dma = False; o.tag = "barrier"
            self.ops[e].append(o)
            self.snaps[e].append((dict(self.known[e]), dict(self.kdma[e])))

    def finish(self):
        waits, dwaits = [], []
        for tok in self.out_tokens:
            self._need("sp", tok, waits, dwaits)
        o = Op()
        o.eng = "sp"; o.fn = None; o.waits = waits; o.dwaits = dwaits
        o.idx = len(self.ops["sp"]); o.is_dma = False; o.tag = "finish"
        self.ops["sp"].append(o)
        self.snaps["sp"].append((dict(self.known["sp"]), dict(self.kdma["sp"])))

    def emit(self):
        nc = self.nc
        rank = {}
        for e in ENGS:
            ms = sorted(self.milestones[e])
            rank[e] = {idx: i + 1 for i, idx in enumerate(ms)}
        with ExitStack() as es:
            esem = {e: es.enter_context(nc.semaphore("pg_" + e)) for e in ENGS}
            dsem = {}
            for q in ENGS:
                for i in range(min(NDMASEM, self.ndma[q])):
                    dsem[(q, i)] = es.enter_context(nc.semaphore("dm_%s_%d" % (q, i)))
            block = es.enter_context(nc.Block())

            def run(eng_name):
                def body(eng):
                    for o in self.ops[eng_name]:
                        for (f, idx) in o.waits:
                            eng.wait_ge(esem[f], rank[f][idx])
                        for (q, semi, val) in o.dwaits:
                            eng.wait_ge(dsem[(q, semi)], val)
                        if o.fn is None:
                            continue
                        ins = None
                        for (nm, a, k) in o.calls:
                            ins = getattr(eng, nm)(*a, **k)
                        if o.is_dma:
                            ins.then_inc(dsem[(eng_name, o.dsem)], 16)
                        elif o.idx in rank[eng_name]:
                            ins.then_inc(esem[eng_name], 1)
                return body

            block.tensor(run("pe"))
            block.scalar(run("act"))
            block.vector(run("dve"))
            block.gpsimd(run("pool"))
            block.sync(run("sp"))


class Stream:
    def __init__(self):
        self.ops = []

    def _add(self, eng, fn, reads, writes):
        rec = _Rec()
        fn(rec)
        self.ops.append((eng, rec.calls, tuple(reads), tuple(writes)))

    def pe(self, fn, reads=(), writes=()):
        self._add("pe", fn, reads, writes)

    def act(self, fn, reads=(), writes=()):
        self._add("act", fn, reads, writes)

    def dve(self, fn, reads=(), writes=()):
        self._add("dve", fn, reads, writes)

    def pool(self, fn, reads=(), writes=()):
        self._add("pool", fn, reads, writes)


def merge_streams(P, streams):
    lists = [st.ops for st in streams if st is not None and st.ops]
    idx = [0] * len(lists)
    tot = [len(l) for l in lists]
    while True:
        live = [k for k in range(len(lists)) if idx[k] < tot[k]]
        if not live:
            break
        k = min(live, key=lambda q: idx[q] / tot[q])
        eng, calls, reads, writes = lists[k][idx[k]]
        P.submit(eng, calls, reads, writes)
        idx[k] += 1


class Ctx:
    def __init__(self):
        self.nc = bass.Bass("TRN2", target_bir_lowering=False)
        self.P = Prog(self.nc)
        self.es = ExitStack()
        self.banks = [self.es.enter_context(self.nc.psum_tensor("pb%d" % i, [128, 512], F32)) for i in range(8)]
        self.nbank = 0
        self.uid = 0
        self.stacks = [self.es]
        self.pcount = {}

    def sb(self, name, shape, dt):
        self.uid += 1
        return self.stacks[-1].enter_context(self.nc.sbuf_tensor("%s_%d" % (name, self.uid), shape, dt))

    def scope_begin(self):
        self.stacks.append(ExitStack())

    def scope_end(self):
        self.P.barrier()
        self.stacks.pop().close()

    def din(self, name, shape, dt=F32):
        return self.nc.dram_tensor(name, list(shape), dt, kind="ExternalInput").ap()

    def dout(self, name, shape, dt=F32):
        return self.nc.dram_tensor(name, list(shape), dt, kind="ExternalOutput").ap()

    def ps(self):
        i = self.nbank % 8
        self.nbank += 1
        return self.banks[i], ("ps", i)

    def key(self, base):
        self.uid += 1
        return (base, self.uid)


def load_cols(C, vec_d, n, name, ident32):
    P = C.P
    rows = C.sb(name + "_r", [n, 128], F32)
    cols = C.sb(name + "_c", [128, n], F32)
    P.dma("sp", rows[:], vec_d.rearrange("(c p) -> c p", p=128), writes=[name + "_r"])
    pb, pk = C.ps()
    P.pe(lambda e: e.transpose(pb[:, 0:n], rows[:], ident32[0:n, 0:n]), reads=[name + "_r", "ident32"], writes=[pk])
    P.dve(lambda e: e.tensor_copy(out=cols[:], in_=pb[:, 0:n]), reads=[pk], writes=[name + "_c"])
    return cols, name + "_c"


INPUT_SHAPES = {
    "x": ([T, D], F32), "c": ([D], F32), "positions": ([T], I32),
    "ada_w": ([4, D, 6 * D], F32), "ada_b": ([4, 6 * D], F32), "norm_mix_g": ([4, D], F32), "norm_ffn_g": ([4, D], F32),
    "gdn_w_in": ([2, D, 4112], F32), "gdn_conv_w": ([2, 3072, 4], F32), "gdn_a_log": ([2, 8], F32), "gdn_dt_bias": ([2, 8], F32),
    "gdn_norm_g": ([2, 128], F32), "gdn_w_out": ([2, D, D], F32),
    "mla_w_in": ([2, D, 704], F32), "mla_q_norm_g": ([2, 384], F32), "mla_kv_norm_g": ([2, 256], F32),
    "mla_w_uq": ([2, 384, 1536], F32), "mla_w_ukv": ([2, 256, 2048], F32), "mla_w_out": ([2, D, D], F32),
    "ffn_w_gate": ([4, D, DFF], F32), "ffn_w_up": ([4, D, DFF], F32), "ffn_w_down": ([4, DFF, D], F32),
    "final_norm_g": ([D], F32),
    "ident": ([128, 128], F32), "invf": ([128], F32), "tri": ([128, 128], F32), "gmasks": ([19, 128, 128], F32),
}


def setup(C):
    P = C.P
    C.I = {nm: C.din(nm, shp, dt) for nm, (shp, dt) in INPUT_SHAPES.items()}
    I = C.I
    ident32 = C.sb("ident32", [128, 128], F32)
    identb = C.sb("identb", [128, 128], BF16)
    ones32 = C.sb("ones32", [128, 128], F32)
    X = C.sb("X", [128, NT, D], F32)
    modc = C.sb("modc", [128, 24], F32)
    gs = C.sb("gs", [128, 8], F32)
    gate_bc = C.sb("gate_bc", [128, D], F32)
    ccol = C.sb("ccol", [128, 8], F32)
    ccolb = C.sb("ccolb", [128, 8], BF16)
    C.epsc = C.sb("epsc", [128, 1], F32)
    P.dve(lambda e: e.memset(C.epsc[:], EPS), writes=["epsc"])
    P.dma("sp", ident32[:], I["ident"], writes=["ident32"])
    P.dma("pool", identb[:], I["ident"], writes=["identb"])
    P.dve(lambda e: e.memset(ones32[:], 1.0), writes=["ones32"])
    xv = I["x"].rearrange("(t p) d -> p t d", p=128)
    for i in range(4):
        P.dma("sp", X[:, 4 * i:4 * i + 4, :], xv[:, 4 * i:4 * i + 4, :], writes=[("X", t) for t in range(4 * i, 4 * i + 4)])
    C.scope_begin()
    ccol_raw, ck = load_cols(C, I["c"], 8, "cc", ident32)
    P.act(lambda e: e.activation(out=ccol[:], in_=ccol_raw[:], func=AF.Silu), reads=[ck], writes=["ccol"])
    P.dve(lambda e: e.tensor_copy(out=ccolb[:], in_=ccol[:]), reads=["ccol"], writes=["ccolb"])
    C.scope_end()
    return dict(X=X, modc=modc, gs=gs, gate_bc=gate_bc, ident32=ident32, identb=identb, ones32=ones32, ccol=ccol, ccolb=ccolb)


def mod_compute(C, R, layer, groups, ng_d):
    P = C.P
    I = C.I
    modc, gs, gate_bc, ident32, ones32, ccol = R["modc"], R["gs"], R["gate_bc"], R["ident32"], R["ones32"], R["ccol"]
    adaw_d = I["ada_w"][layer]
    adab_d = I["ada_b"][layer]
    C.scope_begin()
    bcol, bk = load_cols(C, adab_d, 48, "ab", ident32)
    gcol, gk = load_cols(C, ng_d, 8, "ng", ident32)
    aws = [C.sb("aw%d" % i, [128, 8, 512], BF16) for i in range(2)]
    ccolb = R["ccolb"]
    pb, pk = C.ps()
    adv = adaw_d.rearrange("(k p) f -> p k f", p=128)
    na = 0
    for gi, g in enumerate(groups):
        for jj in range(2):
            aw = aws[na % 2]
            awk = ("aw", na % 2)
            na += 1
            P.dma("pool", aw[:], adv[:, :, g * D + jj * 512:g * D + (jj + 1) * 512], writes=[awk])
            for j4 in range(4):
                j = jj * 4 + j4
                for k in range(8):
                    P.pe(lambda e, j=j, j4=j4, k=k, gi=gi, aw=aw: e.matmul(pb[:, gi * 8 + j:gi * 8 + j + 1], lhsT=aw[:, k, j4 * 128:(j4 + 1) * 128],
                                                                         rhs=ccolb[:, k:k + 1], start=(k == 0), stop=(k == 7)),
                         reads=[awk, "ccolb"], writes=[pk])
    for gi, g in enumerate(groups):
        P.dve(lambda e, gi=gi, g=g: e.tensor_tensor(out=modc[:, gi * 8:gi * 8 + 8], in0=pb[:, gi * 8:gi * 8 + 8],
                                                    in1=bcol[:, g * 8:g * 8 + 8], op=ALU.add),
              reads=[pk, bk], writes=["modc"])
    P.dve(lambda e: e.scalar_tensor_tensor(out=gs[:], in0=modc[:, 8:16], scalar=1.0, in1=gcol[:], op0=ALU.add, op1=ALU.mult),
          reads=["modc", gk], writes=["gs"])
    dg = C.sb("dg", [128, 128], F32)
    for j in range(8):
        P.dve(lambda e, j=j: e.tensor_scalar(out=dg[:], in0=ident32[:], scalar1=modc[:, 16 + j:17 + j], scalar2=None, op0=ALU.mult),
              reads=["ident32", "modc"], writes=["dg"])
        pb2, pk2 = C.ps()
        P.pe(lambda e, pb2=pb2: e.matmul(pb2[:, 0:128], lhsT=ones32[:], rhs=dg[:], start=True, stop=True),
             reads=["ones32", "dg"], writes=[pk2])
        P.act(lambda e, j=j, pb2=pb2: e.copy(out=gate_bc[:, j * 128:(j + 1) * 128], in_=pb2[:, 0:128]), reads=[pk2], writes=["gate_bc"])
    C.scope_end()


def rstd_col(C, X, t, small, ki, junk):
    P = C.P
    kk = ("small", ki)
    P.act(lambda e: e.activation(out=junk[:], in_=X[:, t, :], func=AF.Square, accum_out=small[:, ki:ki + 1]),
          reads=[("X", t)], writes=["junk", kk])
    P.act(lambda e: e.activation(out=small[:, ki:ki + 1], in_=small[:, ki:ki + 1], func=AF.Ln, scale=1.0 / D, bias=C.epsc[:, 0:1]),
          reads=[kk, "epsc"], writes=[kk])
    P.act(lambda e: e.activation(out=small[:, ki:ki + 1], in_=small[:, ki:ki + 1], func=AF.Exp, scale=-0.5),
          reads=[kk], writes=[kk])
    return kk


def norm_to_hT(C, R, hT):
    P = C.P
    X, modc, gs, identb = R["X"], R["modc"], R["gs"], R["identb"]
    C.scope_begin()
    small = C.sb("nsmall", [128, NT], F32)
    junk = C.sb("junk", [128, D], BF16)
    tmps = [C.sb("tmpn%d" % i, [128, D], F32) for i in range(2)]
    xn = [C.sb("xn%d" % i, [128, D], BF16) for i in range(2)]
    for t in range(NT):
        tmp = tmps[t % 2]
        tk = ("tmpn", t % 2)
        kk = rstd_col(C, X, t, small, t, junk)
        xb = xn[t % 2]
        xk = ("xn", t % 2)
        P.dve(lambda e, xb=xb, t=t: e.tensor_scalar(out=xb[:], in0=X[:, t, :], scalar1=small[:, t:t + 1], scalar2=None, op0=ALU.mult),
              reads=[("X", t), kk], writes=[xk])
        pb, pk = C.ps()
        pbb = pb[:].bitcast(BF16)
        for c in range(8):
            P.pe(lambda e, c=c, xb=xb, pbb=pbb: e.transpose(pbb[:, c * 128:(c + 1) * 128], xb[:, c * 128:(c + 1) * 128], identb[:]),
                 reads=[xk, "identb"], writes=[pk])
        P.dve(lambda e, pbb=pbb, tmp=tmp: e.tensor_tensor(out=tmp[:].rearrange("p (c i) -> p c i", c=8), in0=pbb.rearrange("p (c i) -> p c i", c=8),
                                                          in1=gs[:].unsqueeze(2).to_broadcast([128, 8, 128]), op=ALU.mult),
              reads=[pk, "gs"], writes=[tk])
        P.pool(lambda e, t=t, tmp=tmp: e.tensor_tensor(out=hT[:, :, t * 128:(t + 1) * 128], in0=tmp[:].rearrange("p (c i) -> p c i", c=8),
                                                       in1=modc[:, 0:8].unsqueeze(2).to_broadcast([128, 8, 128]), op=ALU.add),
               reads=[tk, "modc"], writes=[("hT", t)])
    C.scope_end()


def store_x(C, X, name="out"):
    P = C.P
    o_d = C.dout(name, [T, D])
    ov = o_d.rearrange("(t p) d -> p t d", p=128)
    for i in range(4):
        P.dma("sp", ov[:, 4 * i:4 * i + 4, :], X[:, 4 * i:4 * i + 4, :], reads=[("X", t) for t in range(4 * i, 4 * i + 4)], is_out=True)


def ffn_body(C, R, hT, layer, final):
    P = C.P
    I = C.I
    ng_d = I["norm_ffn_g"][layer]
    wg_d = I["ffn_w_gate"][layer]
    wu_d = I["ffn_w_up"][layer]
    wd_d = I["ffn_w_down"][layer]
    mod_compute(C, R, layer, [3, 4, 5], ng_d)
    X, gate_bc = R["X"], R["gate_bc"]
    norm_to_hT(C, R, hT)
    G = 4
    wgu = [C.sb("wgu%d" % i, [128, 8, 2, G * 128], BF16) for i in range(2)]
    wdn = [C.sb("wdn%d" % i, [128, G, D], BF16) for i in range(2)]
    hid = [C.sb("hid%d" % i, [128, G, 512], BF16) for i in range(2)]
    sg = [C.sb("sg%d" % i, [128, 512], F32) for i in range(2)]
    wgv = wg_d.rearrange("(k p) f -> p k f", p=128)
    wuv = wu_d.rearrange("(k p) f -> p k f", p=128)
    wdv = wd_d.rearrange("(c p) n -> p c n", p=128)
    groups = []
    c0 = 0
    while c0 < NFC:
        g = min(G, NFC - c0)
        groups.append((c0, g))
        c0 += g
    nsg = 0

    def load_group(gi):
        c0, g = groups[gi]
        b = gi % 2
        kgu = ("wgu", b)
        kd = ("wdn", b)
        P.dma("pool", wgu[b][:, :, 0, 0:g * 128], wgv[:, :, c0 * 128:(c0 + g) * 128], writes=[kgu])
        P.dma("pool", wgu[b][:, :, 1, 0:g * 128], wuv[:, :, c0 * 128:(c0 + g) * 128], writes=[kgu])
        P.dma("pool", wdn[b][:, 0:g, :], wdv[:, c0:c0 + g, :], writes=[kd])
        P.pool(lambda e: e.tensor_tensor(out=wdn[b][:, 0:g, :], in0=wdn[b][:, 0:g, :],
                                         in1=gate_bc[:].unsqueeze(1).to_broadcast([128, g, D]), op=ALU.mult),
               reads=[kd, "gate_bc"], writes=[kd])

    def gate_up(gi, tb, hb):
        nonlocal nsg
        c0, g = groups[gi]
        b = gi % 2
        kgu = ("wgu", b)
        hk = ("hid", hb)
        for j in range(g):
            pg, pgk = C.ps()
            pu, puk = C.ps()
            for k in range(8):
                P.pe(lambda e, k=k: e.matmul(pg[:], lhsT=wgu[b][:, k, 0, j * 128:(j + 1) * 128], rhs=hT[:, k, tb * 512:(tb + 1) * 512],
                                            start=(k == 0), stop=(k == 7)),
                     reads=[kgu] + [("hT", 4 * tb + i) for i in range(4)], writes=[pgk])
            for k in range(8):
                P.pe(lambda e, k=k: e.matmul(pu[:], lhsT=wgu[b][:, k, 1, j * 128:(j + 1) * 128], rhs=hT[:, k, tb * 512:(tb + 1) * 512],
                                            start=(k == 0), stop=(k == 7)),
                     reads=[kgu] + [("hT", 4 * tb + i) for i in range(4)], writes=[puk])
            sb_ = nsg % 2
            nsg += 1
            sk = ("sg", sb_)
            P.act(lambda e: e.activation(out=sg[sb_][:], in_=pg[:], func=AF.Silu), reads=[pgk], writes=[sk])
            P.dve(lambda e: e.tensor_tensor(out=hid[hb][:, j, :], in0=sg[sb_][:], in1=pu[:], op=ALU.mult), reads=[sk, puk], writes=[hk])

    def down(gi, tb, hb):
        c0, g = groups[gi]
        b = gi % 2
        kd = ("wdn", b)
        hk = ("hid", hb)
        for tt in range(4):
            t = tb * 4 + tt
            for half in range(2):
                po, pok = C.ps()
                for j in range(g):
                    P.pe(lambda e, j=j: e.matmul(po[:], lhsT=hid[hb][:, j, tt * 128:(tt + 1) * 128], rhs=wdn[b][:, j, half * 512:(half + 1) * 512],
                                                start=(j == 0), stop=(j == g - 1)), reads=[hk, kd], writes=[pok])
                P.dve(lambda e: e.tensor_tensor(out=X[:, t, half * 512:(half + 1) * 512], in0=X[:, t, half * 512:(half + 1) * 512], in1=po[:], op=ALU.add),
                      reads=[pok, ("X", t)], writes=[("X", t)])

    items = [(gi, tb) for gi in range(len(groups)) for tb in range(4)]
    load_group(0)
    prev = None
    for ii, (gi, tb) in enumerate(items):
        hb = ii % 2
        gate_up(gi, tb, hb)
        if prev is not None:
            down(*prev)
        if tb == 0 and gi + 1 < len(groups):
            load_group(gi + 1)
        prev = (gi, tb, hb)
    down(*prev)
    if final:
        fg_d = I["final_norm_g"]
        fgb = C.sb("fgb", [128, D], F32)
        P.dma("sp", fgb[:], fg_d.unsqueeze(0).to_broadcast([128, D]), writes=["fgb"])
        small = C.sb("fsmall", [128, NT], F32)
        fjunk = C.sb("fjunk", [128, D], BF16)
        for t in range(NT):
            kk = rstd_col(C, X, t, small, t, fjunk)
            P.dve(lambda e, t=t: e.scalar_tensor_tensor(out=X[:, t, :], in0=X[:, t, :], scalar=small[:, t:t + 1], in1=fgb[:],
                                                        op0=ALU.mult, op1=ALU.mult),
                  reads=[("X", t), kk, "fgb"], writes=[("X", t)])


def _ident():
    return np.eye(128, dtype=np.float32)


QR = 384
KVR = 256
SCALE = float(192 ** -0.5)
TWO_PI = float(2 * np.pi)


def psn(C, pool):
    i = pool[C.pcount.get(tuple(pool), 0) % len(pool)]
    C.pcount[tuple(pool)] = C.pcount.get(tuple(pool), 0) + 1
    return C.banks[i], ("ps", i)


def mla_body(C, R, hT, layer):
    P = C.P
    I = C.I
    j_ = layer // 2
    ng_d = I["norm_mix_g"][layer]
    win_d = I["mla_w_in"][j_]
    qg_d = I["mla_q_norm_g"][j_]
    kvg_d = I["mla_kv_norm_g"][j_]
    wuq_d = I["mla_w_uq"][j_]
    wukv_d = I["mla_w_ukv"][j_]
    wout_d = I["mla_w_out"][j_]
    pos_d = I["positions"]
    invf_d = I["invf"]
    tri_d = I["tri"]
    mod_compute(C, R, layer, [0, 1, 2], ng_d)
    X, gate_bc, ident32, identb = R["X"], R["gate_bc"], R["ident32"], R["identb"]
    norm_to_hT(C, R, hT)
    ALLB = list(range(8))

    trib = C.sb("trib", [128, 128], BF16)
    P.dma("pool", trib[:], tri_d, writes=["trib"])
    wbuf = C.sb("wbuf", [128, 8 * D], BF16)
    w_in = wbuf[:, 0:8 * 704].rearrange("p (k f) -> p k f", k=8)
    w_out = wbuf[:].rearrange("p (k f) -> p k f", k=8)
    P.dma("pool", w_in, win_d.rearrange("(k p) f -> p k f", p=128), writes=["wbuf"])
    w_uq = C.sb("w_uq", [128, 3, 1536], BF16)
    P.dma("pool", w_uq[:], wuq_d.rearrange("(k p) f -> p k f", p=128), writes=["w_uq"])
    w_ukv = C.sb("w_ukv", [128, 2, 2048], BF16)
    P.dma("pool", w_ukv[:], wukv_d.rearrange("(k p) f -> p k f", p=128), writes=["w_ukv"])
    wk2A = C.sb("wk2A", [128, 8, 128], BF16)
    wk2B = C.sb("wk2B", [128, 8, 128], BF16)
    for half in range(2):
        o_ = half * 64
        P.act(lambda e, o_=o_: e.copy(out=wk2A[:, :, o_:o_ + 64], in_=w_in[:, :, 640:704]), reads=["wbuf"], writes=["wk2A"])
        P.act(lambda e, o_=o_: e.mul(out=wk2B[:, :, o_:o_ + 32], in_=w_in[:, :, 672:704], mul=-1.0), reads=["wbuf"], writes=["wk2B"])
        P.act(lambda e, o_=o_: e.copy(out=wk2B[:, :, o_ + 32:o_ + 64], in_=w_in[:, :, 640:672]), reads=["wbuf"], writes=["wk2B"])
    wq2 = C.sb("wq2", [128, 3, 8, 128], BF16)
    wq4 = w_uq[:].rearrange("p k (h f) -> p k h f", h=8)
    P.act(lambda e: e.copy(out=wq2[:, :, :, 0:64], in_=wq4[:, :, :, 128:192]), reads=["w_uq"], writes=["wq2"])
    P.act(lambda e: e.mul(out=wq2[:, :, :, 64:96], in_=wq4[:, :, :, 160:192], mul=-1.0), reads=["w_uq"], writes=["wq2"])
    P.act(lambda e: e.copy(out=wq2[:, :, :, 96:128], in_=wq4[:, :, :, 128:160]), reads=["w_uq"], writes=["wq2"])
    qgc, qgk = load_cols(C, qg_d, 3, "qg", ident32)
    kvgc, kvgk = load_cols(C, kvg_d, 2, "kvg", ident32)
    g5 = C.sb("g5", [128, 5], F32)
    P.dve(lambda e: e.tensor_copy(out=g5[:, 0:3], in_=qgc[:]), reads=[qgk], writes=["g5"])
    P.dve(lambda e: e.tensor_copy(out=g5[:, 3:5], in_=kvgc[:]), reads=[kvgk], writes=["g5"])

    cosT = C.sb("cosT", [128, T], BF16)
    sinT = C.sb("sinT", [128, T], BF16)
    cs2 = C.sb("cs2", [128, T], BF16)
    invf = C.sb("invf", [128, 1], F32)
    P.dma("sp", invf[:], invf_d.rearrange("(p o) -> p o", o=1), writes=["invf"])
    C.scope_begin()
    posi = C.sb("posi", [128, 512], I32)
    xs = C.sb("xs", [128, 512], F32)
    ri = C.sb("ri", [128, 512], I32)
    rf = C.sb("rf", [128, 512], F32)
    for tb in range(4):
        sl = slice(tb * 512, (tb + 1) * 512)
        P.dma("sp", posi[:], pos_d[sl].unsqueeze(0).to_broadcast([128, 512]), writes=["posi"])
        P.dve(lambda e: e.tensor_copy(out=xs[:], in_=posi[:]), reads=["posi"], writes=["xs"])
        P.dve(lambda e: e.tensor_scalar(out=xs[:], in0=xs[:], scalar1=invf[:, 0:1], scalar2=1.0 / TWO_PI, op0=ALU.mult, op1=ALU.mult),
              reads=["xs", "invf"], writes=["xs"])
        for which, tab in ((0, sinT), (1, cosT)):
            if which == 1:
                P.dve(lambda e: e.tensor_scalar(out=xs[:], in0=xs[:], scalar1=0.25, scalar2=None, op0=ALU.add), reads=["xs"], writes=["xs"])
            P.dve(lambda e: e.tensor_copy(out=ri[:], in_=xs[:]), reads=["xs"], writes=["ri"])
            P.dve(lambda e: e.tensor_copy(out=rf[:], in_=ri[:]), reads=["ri"], writes=["rf"])
            P.dve(lambda e: e.tensor_tensor(out=rf[:], in0=xs[:], in1=rf[:], op=ALU.subtract), reads=["xs", "rf"], writes=["rf"])
            P.act(lambda e, tab=tab, sl=sl: e.activation(out=tab[:, sl], in_=rf[:], func=AF.Sin, scale=TWO_PI * (1.0 - 2e-7)),
                  reads=["rf"], writes=[("cs", tb)])
        P.act(lambda e, sl=sl: e.copy(out=cs2[0:64, sl], in_=cosT[0:64, sl]), reads=[("cs", tb)], writes=[("cs2", tb)])
        P.act(lambda e, sl=sl: e.copy(out=cs2[64:128, sl], in_=sinT[64:128, sl]), reads=[("cs", tb)], writes=[("cs2", tb)])
    C.scope_end()

    cT = C.sb("cT", [128, 5, T], BF16)
    C.scope_begin()
    mjunk = C.sb("mjunk", [128, 640], BF16)
    clats = [C.sb("clat%d" % i, [128, 640], F32) for i in range(2)]
    cns = [C.sb("cn%d" % i, [128, 640], BF16) for i in range(2)]
    sms = [C.sb("msmall%d" % i, [128, 4], F32) for i in range(2)]
    for t in range(NT):
        clat, cn, sm = clats[t % 2], cns[t % 2], sms[t % 2]
        kcl, kcn, ksm = ("clat", t % 2), ("cn", t % 2), ("msm", t % 2)
        pa, pak = psn(C, ALLB)
        pb_, pbk = psn(C, ALLB)
        for k in range(8):
            P.pe(lambda e, k=k, t=t, pa=pa: e.matmul(pa[:], lhsT=hT[:, k, t * 128:(t + 1) * 128], rhs=w_in[:, k, 0:512],
                                                   start=(k == 0), stop=(k == 7)), reads=[("hT", t), "wbuf"], writes=[pak])
        for k in range(8):
            P.pe(lambda e, k=k, t=t, pb_=pb_: e.matmul(pb_[:, 0:128], lhsT=hT[:, k, t * 128:(t + 1) * 128], rhs=w_in[:, k, 512:640],
                                                     start=(k == 0), stop=(k == 7)), reads=[("hT", t), "wbuf"], writes=[pbk])
        P.act(lambda e, pa=pa, clat=clat: e.copy(out=clat[:, 0:512], in_=pa[:]), reads=[pak], writes=[kcl])
        P.act(lambda e, pb_=pb_, clat=clat: e.copy(out=clat[:, 512:640], in_=pb_[:, 0:128]), reads=[pbk], writes=[kcl])
        P.act(lambda e, clat=clat, sm=sm: e.activation(out=mjunk[:, 0:384], in_=clat[:, 0:384], func=AF.Square, accum_out=sm[:, 0:1]),
              reads=[kcl], writes=["mjunk", ksm])
        P.act(lambda e, clat=clat, sm=sm: e.activation(out=mjunk[:, 384:640], in_=clat[:, 384:640], func=AF.Square, accum_out=sm[:, 1:2]),
              reads=[kcl], writes=["mjunk", ksm])
        P.act(lambda e, sm=sm: e.activation(out=sm[:, 0:1], in_=sm[:, 0:1], func=AF.Ln, scale=1.0 / QR, bias=C.epsc[:, 0:1]), reads=[ksm, "epsc"], writes=[ksm])
        P.act(lambda e, sm=sm: e.activation(out=sm[:, 1:2], in_=sm[:, 1:2], func=AF.Ln, scale=1.0 / KVR, bias=C.epsc[:, 0:1]), reads=[ksm, "epsc"], writes=[ksm])
        P.act(lambda e, sm=sm: e.activation(out=sm[:, 0:2], in_=sm[:, 0:2], func=AF.Exp, scale=-0.5), reads=[ksm], writes=[ksm])
        P.dve(lambda e, clat=clat, cn=cn, sm=sm: e.tensor_scalar(out=cn[:, 0:384], in0=clat[:, 0:384], scalar1=sm[:, 0:1], scalar2=None, op0=ALU.mult),
              reads=[kcl, ksm], writes=[kcn])
        P.dve(lambda e, clat=clat, cn=cn, sm=sm: e.tensor_scalar(out=cn[:, 384:640], in0=clat[:, 384:640], scalar1=sm[:, 1:2], scalar2=None, op0=ALU.mult),
              reads=[kcl, ksm], writes=[kcn])
        pt, ptk = psn(C, ALLB)
        ptb = pt[:].bitcast(BF16)
        for c in range(5):
            P.pe(lambda e, c=c, ptb=ptb, cn=cn: e.transpose(ptb[:, c * 128:(c + 1) * 128], cn[:, c * 128:(c + 1) * 128], identb[:]),
                 reads=[kcn, "identb"], writes=[ptk])
        P.dve(lambda e, t=t, ptb=ptb: e.tensor_tensor(out=cT[:, :, t * 128:(t + 1) * 128],
                                                      in0=ptb[:, 0:640].rearrange("p (c i) -> p c i", c=5),
                                                      in1=g5[:].unsqueeze(2).to_broadcast([128, 5, 128]), op=ALU.mult),
              reads=[ptk, "g5"], writes=[("cT", t)])
    C.scope_end()
    krT = C.sb("krT", [128, T], BF16)
    C.scope_begin()
    t1 = C.sb("t1", [128, 512], F32)
    t2 = C.sb("t2", [128, 512], F32)
    for tb in range(4):
        sl = slice(tb * 512, (tb + 1) * 512)
        pA, pAk = psn(C, ALLB)
        pB, pBk = psn(C, ALLB)
        hk = [("hT", 4 * tb + i) for i in range(4)]
        for k in range(8):
            P.pe(lambda e, k=k, pA=pA, sl=sl: e.matmul(pA[:], lhsT=wk2A[:, k, :], rhs=hT[:, k, sl], start=(k == 0), stop=(k == 7)),
                 reads=hk + ["wk2A"], writes=[pAk])
        for k in range(8):
            P.pe(lambda e, k=k, pB=pB, sl=sl: e.matmul(pB[:], lhsT=wk2B[:, k, :], rhs=hT[:, k, sl], start=(k == 0), stop=(k == 7)),
                 reads=hk + ["wk2B"], writes=[pBk])
        P.dve(lambda e, pA=pA, sl=sl: e.tensor_tensor(out=t1[:], in0=pA[:], in1=cosT[:, sl], op=ALU.mult), reads=[pAk, ("cs", tb)], writes=["t1"])
        P.dve(lambda e, pB=pB, sl=sl: e.tensor_tensor(out=t2[:], in0=pB[:], in1=sinT[:, sl], op=ALU.mult), reads=[pBk, ("cs", tb)], writes=["t2"])
        P.dve(lambda e, sl=sl: e.tensor_tensor(out=krT[:, sl], in0=t1[:], in1=t2[:], op=ALU.add), reads=["t1", "t2"], writes=[("krT", tb)])
    C.scope_end()

    P.dma("pool", w_out, wout_d.rearrange("(k p) f -> p k f", p=128), writes=["wbuf"])
    P.pool(lambda e: e.tensor_tensor(out=w_out, in0=w_out, in1=gate_bc[:].unsqueeze(1).to_broadcast([128, 8, D]), op=ALU.mult),
           reads=["wbuf", "gate_bc"], writes=["wbuf"])

    oT = hT
    qn = C.sb("qn", [128, T], BF16)
    qr = C.sb("qr", [128, T], BF16)
    kn = C.sb("kn", [128, T], BF16)
    V = C.sb("V", [128, NT, 128], BF16)
    onesb = C.sb("onesb", [128, 128], BF16)
    P.dve(lambda e: e.memset(onesb[:], 1.0), writes=["onesb"])
    pT = [C.sb("pT%d" % i, [128, 512], BF16) for i in range(3)]
    rec = [C.sb("rec%d" % i, [128, 512], F32) for i in range(2)]
    OB = [0, 1]
    SMB = [2, 3]
    SB_ = [4, 5]
    MB = [6, 7]
    npt = 0
    for h in range(8):
        for tb in range(4):
            sl = slice(tb * 512, (tb + 1) * 512)
            ck = [("cT", 4 * tb + i) for i in range(4)]
            p1, p1k = psn(C, MB)
            for k in range(3):
                P.pe(lambda e, k=k, p1=p1, sl=sl, h=h: e.matmul(p1[:], lhsT=w_uq[:, k, h * 192:h * 192 + 128], rhs=cT[:, k, sl],
                                                              start=(k == 0), stop=(k == 2)), reads=ck + ["w_uq"], writes=[p1k])
            P.act(lambda e, p1=p1, sl=sl: e.copy(out=qn[:, sl], in_=p1[:]), reads=[p1k], writes=[("qn", tb)])
            p2, p2k = psn(C, MB)
            for k in range(2):
                P.pe(lambda e, k=k, p2=p2, sl=sl, h=h: e.matmul(p2[:], lhsT=w_ukv[:, k, h * 256:h * 256 + 128], rhs=cT[:, 3 + k, sl],
                                                              start=(k == 0), stop=(k == 1)), reads=ck + ["w_ukv"], writes=[p2k])
            P.act(lambda e, p2=p2, sl=sl: e.copy(out=kn[:, sl], in_=p2[:]), reads=[p2k], writes=[("kn", tb)])
            pA, pAk = psn(C, MB)
            for k in range(3):
                P.pe(lambda e, k=k, pA=pA, sl=sl, h=h: e.matmul(pA[:], lhsT=wq2[:, k, h, :], rhs=cT[:, k, sl],
                                                              start=(k == 0), stop=(k == 2)), reads=ck + ["wq2"], writes=[pAk])
            P.dve(lambda e, pA=pA, sl=sl: e.tensor_tensor(out=qr[:, sl], in0=pA[:], in1=cs2[:, sl], op=ALU.mult),
                  reads=[pAk, ("cs2", tb)], writes=[("qr", tb)])
            p3, p3k = psn(C, MB)
            for i in range(4):
                t = 4 * tb + i
                for k in range(2):
                    P.pe(lambda e, k=k, p3=p3, i=i, t=t, h=h: e.matmul(p3[:, i * 128:(i + 1) * 128], lhsT=cT[:, 3 + k, t * 128:(t + 1) * 128],
                                                                     rhs=w_ukv[:, k, h * 256 + 128:h * 256 + 256], start=(k == 0), stop=(k == 1)),
                         reads=[("cT", t), "w_ukv"], writes=[p3k])
            P.act(lambda e, p3=p3, tb=tb: e.copy(out=V[:, 4 * tb:4 * tb + 4, :], in_=p3[:].rearrange("p (i d) -> p i d", i=4)),
                  reads=[p3k], writes=[("V", tb)])
        items = []
        for qb in range(4):
            for kt in range(4 * qb + 4):
                items.append((qb, kt))
        accs = {}

        def emit_scores(qb, kt):
            nonlocal npt
            q0 = max(kt, 4 * qb)
            n = (4 * qb + 4 - q0) * 128
            qsl = slice(q0 * 128, (4 * qb + 4) * 128)
            ps_, psk = psn(C, SB_)
            P.pe(lambda e: e.matmul(ps_[:, 0:n], lhsT=kn[:, kt * 128:(kt + 1) * 128], rhs=qn[:, qsl], start=True, stop=False),
                 reads=[("kn", kt // 4), ("qn", qb)], writes=[psk])
            P.pe(lambda e: e.matmul(ps_[:, 0:n], lhsT=krT[:, kt * 128:(kt + 1) * 128], rhs=qr[:, qsl], start=False, stop=True),
                 reads=[("krT", kt // 4), ("qr", qb)], writes=[psk])
            pb_i = npt % 3
            npt += 1
            ptile = pT[pb_i]
            pk_ = ("pT", pb_i)
            P.act(lambda e: e.activation(out=ptile[:, 0:n], in_=ps_[:, 0:n], func=AF.Exp, scale=SCALE), reads=[psk], writes=[pk_])
            if kt >= 4 * qb:
                P.pool(lambda e: e.tensor_tensor(out=ptile[:, 0:128], in0=ptile[:, 0:128], in1=trib[:], op=ALU.mult),
                       reads=[pk_, "trib"], writes=[pk_])
            return ptile, pk_

        def emit_pv(qb, kt, ptile, pk_):
            if kt == 0:
                accs[qb] = (psn(C, OB), psn(C, SMB))
            (po, pok), (psm, psmk) = accs[qb]
            nkt = 4 * qb + 4
            q0 = max(kt, 4 * qb)
            n = (4 * qb + 4 - q0) * 128
            off = (q0 - 4 * qb) * 128
            P.pe(lambda e: e.matmul(po[:, off:off + n], lhsT=V[:, kt, :], rhs=ptile[:, 0:n], start=(kt == 0), stop=(kt == nkt - 1)),
                 reads=[pk_, ("V", kt // 4)], writes=[pok])
            P.pe(lambda e: e.matmul(psm[:, off:off + n], lhsT=onesb[:], rhs=ptile[:, 0:n], start=(kt == 0), stop=(kt == nkt - 1)),
                 reads=[pk_, "onesb"], writes=[psmk])
            if kt == nkt - 1:
                rc = rec[qb % 2]
                rck = ("rec", qb % 2)
                P.dve(lambda e: e.reciprocal(out=rc[:], in_=psm[:]), reads=[psmk], writes=[rck])
                P.dve(lambda e: e.tensor_tensor(out=oT[:, h, qb * 512:(qb + 1) * 512], in0=po[:], in1=rc[:], op=ALU.mult),
                      reads=[pok, rck], writes=[("hT", 4 * qb + i) for i in range(4)])

        cur = emit_scores(*items[0])
        for ii, (qb, kt) in enumerate(items):
            nxt = emit_scores(*items[ii + 1]) if ii + 1 < len(items) else None
            emit_pv(qb, kt, *cur)
            cur = nxt

    for t in range(NT):
        for half in range(2):
            po, pok = psn(C, ALLB)
            for h in range(8):
                P.pe(lambda e, po=po, h=h, t=t, half=half: e.matmul(po[:], lhsT=oT[:, h, t * 128:(t + 1) * 128],
                                                                  rhs=w_out[:, h, half * 512:(half + 1) * 512], start=(h == 0), stop=(h == 7)),
                     reads=[("hT", t), "wbuf"], writes=[pok])
            P.dve(lambda e, po=po, t=t, half=half: e.tensor_tensor(out=X[:, t, half * 512:(half + 1) * 512],
                                                                  in0=X[:, t, half * 512:(half + 1) * 512], in1=po[:], op=ALU.add),
                  reads=[pok, ("X", t)], writes=[("X", t)])


def _invf64():
    f = (10000.0 ** (-np.arange(0, 64, 2, dtype=np.float32) / np.float32(64))).astype(np.float32)
    return np.concatenate([f, f, f, f]).astype(np.float32)


def _tri():
    k = np.arange(128)[:, None]
    q = np.arange(128)[None, :]
    return (k <= q).astype(np.float32)


HG = 2
NMASK = 19


def _gmasks():
    idx = np.arange(128)
    p = idx[:, None]
    f = idx[None, :]
    m = []
    m.append((p <= f).astype(np.float32))
    m.append((p > f).astype(np.float32))
    m.append((f < p).astype(np.float32))
    m.append(np.where(f <= p, 0.0, -30000.0).astype(np.float32))
    m.append(np.zeros((128, 128), np.float32))
    b = 1
    while b < 128:
        blk = idx // b
        mU = ((blk[:, None] // 2) == (blk[None, :] // 2)) & ((blk[:, None] % 2) == 0) & ((blk[None, :] % 2) == 1)
        m.append(-mU.astype(np.float32))
        m.append(-mU.T.astype(np.float32))
        b *= 2
    return np.stack(m).astype(np.float32)


def gdn_body(C, R, hT, layer):
    P = C.P
    I = C.I
    j_ = layer // 2
    ng_d = I["norm_mix_g"][layer]
    win_d = I["gdn_w_in"][j_]
    conv_d = I["gdn_conv_w"][j_]
    alog_d = I["gdn_a_log"][j_]
    dtb_d = I["gdn_dt_bias"][j_]
    gng_d = I["gdn_norm_g"][j_]
    wout_d = I["gdn_w_out"][j_]
    gm_d = I["gmasks"]
    mod_compute(C, R, layer, [0, 1, 2], ng_d)
    X, gate_bc, ident32, identb, ones32 = R["X"], R["gate_bc"], R["ident32"], R["identb"], R["ones32"]
    norm_to_hT(C, R, hT)
    ALLB = list(range(8))
    winv = win_d.rearrange("(k p) f -> p k f", p=128)

    tri32 = C.sb("tri32", [128, 2, 128], F32)
    P.dma("sp", tri32[:], gm_d[0:2].rearrange("m p f -> p m f"), writes=["tri32"])
    mk32 = C.sb("mk32", [128, 2, 128], F32)
    P.dma("sp", mk32[:], gm_d[2:4].rearrange("m p f -> p m f"), writes=["mk32"])
    lvm = C.sb("lvm", [128, 14, 128], BF16)
    P.dma("pool", lvm[:], gm_d[5:19].rearrange("m p f -> p m f"), writes=["lvm"])
    onesb = C.sb("onesb", [128, 128], BF16)
    P.dve(lambda e: e.memset(onesb[:], 1.0), writes=["onesb"])
    cw = C.sb("cw", [128, 24, 4], F32)
    cvv = conv_d.rearrange("(c p) k -> p c k", p=128)
    for c_ in range(24):
        P.dma("sp", cw[:, c_, :], cvv[:, c_, :], writes=["cw"])
    gnb = C.sb("gnb", [128, 128], F32)
    P.dma("sp", gnb[:], gng_d.unsqueeze(0).to_broadcast([128, 128]), writes=["gnb"])
    alb = C.sb("alb", [128, 8], F32)
    dtb = C.sb("dtb", [128, 8], F32)
    P.dma("sp", alb[:], alog_d.unsqueeze(0).to_broadcast([128, 8]), writes=["alb"])
    P.dma("sp", dtb[:], dtb_d.unsqueeze(0).to_broadcast([128, 8]), writes=["dtb"])

    col = {nm: C.sb("col_" + nm, [128, NT, 8], F32) for nm in ("g", "beta", "gc", "egc", "cb", "edec", "gl")}
    C.scope_begin()
    wab = C.sb("wab", [128, 8, 16], BF16)
    P.dma("pool", wab[:], winv[:, :, 4096:4112], writes=["wab"])
    abv = C.sb("abv", [128, NT, 16], F32)
    tA = C.sb("tA", [128, NT, 8], F32)
    tB = C.sb("tB", [128, NT, 8], F32)
    pab, pabk = psn(C, ALLB)
    for t in range(NT):
        for k in range(8):
            P.pe(lambda e, t=t, k=k: e.matmul(pab[:, t * 16:(t + 1) * 16], lhsT=hT[:, k, t * 128:(t + 1) * 128], rhs=wab[:, k, :],
                                              start=(k == 0), stop=(k == 7)), reads=[("hT", t), "wab"], writes=[pabk])
    P.act(lambda e: e.copy(out=abv[:].rearrange("p t c -> p (t c)"), in_=pab[:, 0:256]), reads=[pabk], writes=["abv"])
    P.dve(lambda e: e.tensor_tensor(out=tA[:], in0=abv[:, :, 0:8], in1=dtb[:].unsqueeze(1).to_broadcast([128, NT, 8]), op=ALU.add),
          reads=["abv", "dtb"], writes=["tA"])
    P.act(lambda e: e.activation(out=tB[:], in_=tA[:], func=AF.Abs), reads=["tA"], writes=["tB"])
    P.act(lambda e: e.activation(out=tB[:], in_=tB[:], func=AF.Exp, scale=-1.0), reads=["tB"], writes=["tB"])
    P.act(lambda e: e.activation(out=tB[:], in_=tB[:], func=AF.Ln, scale=1.0, bias=1.0), reads=["tB"], writes=["tB"])
    P.dve(lambda e: e.tensor_scalar(out=tA[:], in0=tA[:], scalar1=0.0, scalar2=None, op0=ALU.max), reads=["tA"], writes=["tA"])
    P.dve(lambda e: e.tensor_tensor(out=tA[:], in0=tA[:], in1=tB[:], op=ALU.add), reads=["tA", "tB"], writes=["tA"])
    P.act(lambda e: e.activation(out=alb[:], in_=alb[:], func=AF.Exp), reads=["alb"], writes=["alb"])
    P.dve(lambda e: e.scalar_tensor_tensor(out=col["g"][:], in0=tA[:], scalar=-1.0, in1=alb[:].unsqueeze(1).to_broadcast([128, NT, 8]),
                                           op0=ALU.mult, op1=ALU.mult), reads=["tA", "alb"], writes=["col_g"])
    P.act(lambda e: e.activation(out=col["beta"][:], in_=abv[:, :, 8:16], func=AF.Exp, scale=-1.0), reads=["abv"], writes=["col_beta"])
    P.act(lambda e: e.activation(out=col["beta"][:], in_=col["beta"][:], func=AF.Ln, scale=1.0, bias=1.0), reads=["col_beta"], writes=["col_beta"])
    P.act(lambda e: e.activation(out=col["beta"][:], in_=col["beta"][:], func=AF.Exp, scale=-1.0), reads=["col_beta"], writes=["col_beta"])
    pgc, pgck = psn(C, ALLB)
    prc, prck = psn(C, ALLB)
    ptt, pttk = psn(C, ALLB)
    for n in range(NT):
        P.pe(lambda e, n=n: e.matmul(pgc[:, n * 8:(n + 1) * 8], lhsT=tri32[:, 0, :], rhs=col["g"][:, n, :], start=True, stop=True),
             reads=["tri32", "col_g"], writes=[pgck])
        P.pe(lambda e, n=n: e.matmul(prc[:, n * 8:(n + 1) * 8], lhsT=tri32[:, 1, :], rhs=col["g"][:, n, :], start=True, stop=True),
             reads=["tri32", "col_g"], writes=[prck])
        P.pe(lambda e, n=n: e.matmul(ptt[:, n * 8:(n + 1) * 8], lhsT=ones32[:], rhs=col["g"][:, n, :], start=True, stop=True),
             reads=["ones32", "col_g"], writes=[pttk])
    fl = lambda t_: t_[:].rearrange("p t c -> p (t c)")
    P.act(lambda e: e.copy(out=fl(col["gc"]), in_=pgc[:, 0:128]), reads=[pgck], writes=["col_gc"])
    P.act(lambda e: e.activation(out=fl(col["egc"]), in_=pgc[:, 0:128], func=AF.Exp), reads=[pgck], writes=["col_egc"])
    P.act(lambda e: e.activation(out=fl(col["edec"]), in_=prc[:, 0:128], func=AF.Exp), reads=[prck], writes=["col_edec"])
    P.act(lambda e: e.activation(out=fl(col["gl"]), in_=ptt[:, 0:128], func=AF.Exp), reads=[pttk], writes=["col_gl"])
    P.dve(lambda e: e.scalar_tensor_tensor(out=col["cb"][:], in0=col["egc"][:], scalar=-1.0, in1=col["beta"][:], op0=ALU.mult, op1=ALU.mult),
          reads=["col_egc", "col_beta"], writes=["col_cb"])
    C.scope_end()

    W = HG * 128
    qT = C.sb("qT", [128, HG, T], BF16)
    kT = C.sb("kT", [128, HG, T], BF16)
    vb = C.sb("vb", [128, NT, HG, 128], BF16)
    sgA = C.sb("sgA", [128, NT, W], BF16)

    def bc_h(ap2):
        return ap2.unsqueeze(1).to_broadcast([128, HG, 128])

    def v3(ap):
        return ap.rearrange("p (h i) -> p h i", h=HG)

    for gI in range(8 // HG):
        h0 = gI * HG
        C.scope_begin()
        wgt = C.sb("wgt", [128, 8, W], BF16)
        P.dma("pool", wgt[:], winv[:, :, 3072 + h0 * 128:3072 + (h0 + HG) * 128], writes=["wgt"])
        raw = [C.sb("raw%d" % i, [128, T + 4], BF16) for i in range(2)]
        dcw = [C.sb("dcw%d" % i, [128, 4, 128], BF16) for i in range(2)]
        y32 = C.sb("y32", [128, T], F32)
        sqb = [C.sb("sqb%d" % i, [128, 512], BF16) for i in range(2)]
        rn = [C.sb("rn%d" % i, [128, 512], F32) for i in range(2)]
        vTb = [C.sb("vTb%d" % i, [128, 512], BF16) for i in range(2)]
        wsl = [C.sb("wsl%d" % i, [128, 8, 128], BF16) for i in range(2)]
        for i in range(2):
            P.dve(lambda e, i=i: e.memset(raw[i][:, 0:3], 0.0), writes=[("rawpad", i)])
        for t in range(NT):
            pg_, pgk_ = psn(C, ALLB)
            for k in range(8):
                P.pe(lambda e, pg_=pg_, k=k, t=t: e.matmul(pg_[:, 0:W], lhsT=hT[:, k, t * 128:(t + 1) * 128], rhs=wgt[:, k, :], start=(k == 0), stop=(k == 7)),
                     reads=[("hT", t), "wgt"], writes=[pgk_])
            P.act(lambda e, pg_=pg_, t=t: e.activation(out=sgA[:, t, :], in_=pg_[:, 0:W], func=AF.Silu), reads=[pgk_], writes=[("sgA", t)])
        nw = 0
        nblk = 0
        for hl in range(HG):
            h = h0 + hl
            for typ in (2, 0, 1):
                cidx = typ * 8 + h
                ib = nw % 2
                wb = wsl[ib]
                wk = ("wsl", ib)
                rw = raw[ib]
                dc = dcw[ib]
                nw += 1
                P.dma("pool", wb[:], winv[:, :, typ * 1024 + h * 128:typ * 1024 + (h + 1) * 128], writes=[wk])
                P.dve(lambda e, dc=dc, cidx=cidx: e.tensor_tensor(out=dc[:], in0=identb[:].unsqueeze(1).to_broadcast([128, 4, 128]),
                                                                 in1=cw[:, cidx, :].unsqueeze(2).to_broadcast([128, 4, 128]), op=ALU.mult),
                      reads=["identb", "cw"], writes=[("dcw", ib)])
                for tb in range(4):
                    sl = slice(tb * 512, (tb + 1) * 512)
                    pp, ppk = psn(C, ALLB)
                    for k in range(8):
                        P.pe(lambda e, pp=pp, wb=wb, k=k, sl=sl: e.matmul(pp[:], lhsT=wb[:, k, :], rhs=hT[:, k, sl], start=(k == 0), stop=(k == 7)),
                             reads=[wk] + [("hT", 4 * tb + i) for i in range(4)], writes=[ppk])
                    P.act(lambda e, pp=pp, tb=tb, rw=rw: e.copy(out=rw[:, 3 + tb * 512:3 + (tb + 1) * 512], in_=pp[:]), reads=[ppk], writes=[("raw", ib, tb)])
                    pc, pck = psn(C, ALLB)
                    rkeys = [("raw", ib, tb), ("rawpad", ib)] + ([("raw", ib, tb - 1)] if tb > 0 else [])
                    for j in range(4):
                        P.pe(lambda e, pc=pc, dc=dc, rw=rw, j=j, tb=tb: e.matmul(pc[:], lhsT=dc[:, j, :], rhs=rw[:, tb * 512 + j:tb * 512 + j + 512],
                                                                              start=(j == 0), stop=(j == 3)),
                             reads=rkeys + [("dcw", ib)], writes=[pck])
                    if typ == 2:
                        vt = vTb[nblk % 2]
                        vk = ("vTb", nblk % 2)
                        nblk += 1
                        P.act(lambda e, pc=pc, vt=vt: e.activation(out=vt[:], in_=pc[:], func=AF.Silu), reads=[pck], writes=[vk])
                        pt_, ptk_ = psn(C, ALLB)
                        ptb = pt_[:].bitcast(BF16)
                        for i in range(4):
                            P.pe(lambda e, ptb=ptb, i=i, vt=vt: e.transpose(ptb[:, i * 128:(i + 1) * 128], vt[:, i * 128:(i + 1) * 128], identb[:]),
                                 reads=[vk, "identb"], writes=[ptk_])
                        P.dve(lambda e, ptb=ptb, tb=tb, hl=hl, h=h: e.tensor_tensor(
                            out=vb[:, 4 * tb:4 * tb + 4, hl, :], in0=ptb[:, 0:512].rearrange("p (t d) -> p t d", t=4),
                            in1=col["beta"][:, 4 * tb:4 * tb + 4, h:h + 1].to_broadcast([128, 4, 128]), op=ALU.mult),
                            reads=[ptk_, "col_beta"], writes=[("vb", tb)])
                    else:
                        P.act(lambda e, pc=pc, sl=sl: e.activation(out=y32[:, sl], in_=pc[:], func=AF.Silu), reads=[pck], writes=[("y32", tb)])
                if typ != 2:
                    dst = qT if typ == 0 else kT
                    dkey = "qT" if typ == 0 else "kT"
                    sc = float(128 ** -0.5) if typ == 0 else 1.0
                    for tb in range(4):
                        sl = slice(tb * 512, (tb + 1) * 512)
                        sb_ = sqb[tb % 2]
                        sk_ = ("sqb", tb % 2)
                        rb_ = rn[tb % 2]
                        rk_ = ("rn", tb % 2)
                        P.pool(lambda e, sb_=sb_, sl=sl: e.tensor_tensor(out=sb_[:], in0=y32[:, sl], in1=y32[:, sl], op=ALU.mult), reads=[("y32", tb)], writes=[sk_])
                        pp, ppk = psn(C, ALLB)
                        P.pe(lambda e, pp=pp, sb_=sb_: e.matmul(pp[:], lhsT=onesb[:], rhs=sb_[:], start=True, stop=True), reads=["onesb", sk_], writes=[ppk])
                        P.act(lambda e, pp=pp, rb_=rb_: e.activation(out=rb_[:], in_=pp[:], func=AF.Ln, scale=1.0, bias=C.epsc[:, 0:1]), reads=[ppk, "epsc"], writes=[rk_])
                        P.act(lambda e, rb_=rb_: e.activation(out=rb_[:], in_=rb_[:], func=AF.Exp, scale=-0.5), reads=[rk_], writes=[rk_])
                        P.dve(lambda e, dst=dst, hl=hl, sl=sl, sc=sc, rb_=rb_: e.scalar_tensor_tensor(out=dst[:, hl, sl], in0=y32[:, sl], scalar=sc, in1=rb_[:],
                                                                                                op0=ALU.mult, op1=ALU.mult),
                              reads=[("y32", tb), rk_], writes=[(dkey, tb)])
        C.scope_end()

        C.scope_begin()
        wog = C.sb("wog", [128, HG, D], BF16)
        P.dma("pool", wog[:], wout_d.rearrange("(c p) n -> p c n", p=128)[:, h0:h0 + HG, :], writes=["wog"])
        P.pool(lambda e: e.tensor_tensor(out=wog[:], in0=wog[:], in1=gate_bc[:].unsqueeze(1).to_broadcast([128, HG, D]), op=ALU.mult),
               reads=["wog", "gate_bc"], writes=["wog"])
        S32 = C.sb("S32", [128, W], F32)
        Sbf = C.sb("Sbf", [128, W], BF16)
        P.dve(lambda e: e.memset(S32[:], 0.0), writes=["S32"])
        P.dve(lambda e: e.memset(Sbf[:], 0.0), writes=["Sbf"])
        f32t = lambda nm: C.sb(nm, [128, W], F32)
        bft = lambda nm: C.sb(nm, [128, W], BF16)
        NPRE = 1
        NIF = NPRE + 1
        TS = []
        for i in range(NPRE):
            d_ = {}
            for x in ("dgc", "d2", "dec", "egr", "bsl", "tL", "tU"):
                d_[x] = f32t(x)
            for x in ("L", "U", "attn", "Mm", "Tt0", "Tt1", "Tm0", "Tm1"):
                d_[x] = bft(x)
            TS.append(d_)
        TtF = [bft("TtF%d" % i) for i in range(NIF)]
        attnT = [bft("attnT%d" % i) for i in range(NIF)]
        qdT = [bft("qdT%d" % i) for i in range(NIF)]
        Rr, vnew, vdec, ktok, og, ogT = [bft(x) for x in ("Rr", "vnew", "vdec", "ktok", "og", "ogT")]
        tR, o32, osq = [f32t(x) for x in ("tR", "o32", "osq")]
        oss = C.sb("oss", [128, HG], F32)
        identb_h = identb[:].unsqueeze(1).to_broadcast([128, HG, 128])
        PB_PREP = [0, 1, 2, 3, 4]
        PB_REC = [5, 6]
        PB_OUT = [7]
        obank = {}

        def colb(nm, n):
            return col[nm][:, n, h0:h0 + HG].unsqueeze(2).to_broadcast([128, HG, 128])

        def tr_heads(S, dst_ap, src, skey, dkey, pool):
            pt_, ptk_ = psn(C, pool)
            ptb = pt_[:].bitcast(BF16)
            for hl in range(HG):
                hs = slice(hl * 128, (hl + 1) * 128)
                S.pe(lambda e, ptb=ptb, hs=hs: e.transpose(ptb[:, hs], src[:, hs], identb[:]), reads=[skey, "identb"], writes=[ptk_])
            S.act(lambda e, ptb=ptb: e.copy(out=dst_ap, in_=ptb[:, 0:W]), reads=[ptk_], writes=[dkey])

        def prep(n):
            S = Stream()
            csl = slice(n * 128, (n + 1) * 128)
            pb = n % NIF
            ts = n % NPRE
            D_ = TS[ts]
            dgc, d2, dec, egr, bsl, tL, tU = [D_[x] for x in ("dgc", "d2", "dec", "egr", "bsl", "tL", "tU")]
            L_, U_, attn, Mm = [D_[x] for x in ("L", "U", "attn", "Mm")]
            Tt = [D_["Tt0"], D_["Tt1"]]
            Tm = [D_["Tm0"], D_["Tm1"]]
            K_ = lambda nm: (nm, "ts", ts)
            S.dve(lambda e: e.tensor_tensor(out=v3(dgc[:]), in0=bc_h(ident32[:]), in1=colb("gc", n), op=ALU.mult), reads=["ident32", "col_gc"], writes=[K_("dgc")])
            pgr, pgrk = psn(C, PB_PREP)
            S.pe(lambda e: e.matmul(pgr[:, 0:W], lhsT=ones32[:], rhs=dgc[:], start=True, stop=True), reads=["ones32", K_("dgc")], writes=[pgrk])
            S.dve(lambda e: e.scalar_tensor_tensor(out=v3(d2[:]), in0=v3(pgr[:, 0:W]), scalar=-1.0, in1=colb("gc", n), op0=ALU.mult, op1=ALU.add),
                  reads=[pgrk, "col_gc"], writes=[K_("d2")])
            S.dve(lambda e: e.tensor_tensor(out=v3(d2[:]), in0=v3(d2[:]), in1=bc_h(mk32[:, 1, :]), op=ALU.add), reads=[K_("d2"), "mk32"], writes=[K_("d2")])
            S.act(lambda e: e.activation(out=dec[:], in_=d2[:], func=AF.Exp), reads=[K_("d2")], writes=[K_("dec")])
            S.act(lambda e: e.activation(out=egr[:], in_=pgr[:, 0:W], func=AF.Exp), reads=[pgrk], writes=[K_("egr")])
            S.dve(lambda e: e.tensor_tensor(out=v3(bsl[:]), in0=bc_h(mk32[:, 0, :]), in1=colb("beta", n), op=ALU.mult), reads=["mk32", "col_beta"], writes=[K_("bsl")])
            pkk, pkkk = psn(C, PB_PREP)
            pqk, pqkk = psn(C, PB_PREP)
            for hl in range(HG):
                S.pe(lambda e, hl=hl: e.matmul(pkk[:, hl * 128:(hl + 1) * 128], lhsT=kT[:, hl, csl], rhs=kT[:, hl, csl], start=True, stop=True),
                     reads=[("kT", n // 4)], writes=[pkkk])
            for hl in range(HG):
                S.pe(lambda e, hl=hl: e.matmul(pqk[:, hl * 128:(hl + 1) * 128], lhsT=qT[:, hl, csl], rhs=kT[:, hl, csl], start=True, stop=True),
                     reads=[("kT", n // 4), ("qT", n // 4)], writes=[pqkk])
            S.dve(lambda e: e.tensor_tensor(out=tL[:], in0=pkk[:, 0:W], in1=dec[:], op=ALU.mult), reads=[pkkk, K_("dec")], writes=[K_("tL")])
            S.dve(lambda e: e.tensor_tensor(out=L_[:], in0=tL[:], in1=bsl[:], op=ALU.mult), reads=[K_("tL"), K_("bsl")], writes=[K_("L")])
            S.dve(lambda e: e.tensor_tensor(out=attn[:], in0=pqk[:, 0:W], in1=dec[:], op=ALU.mult), reads=[pqkk, K_("dec")], writes=[K_("attn")])
            S.dve(lambda e: e.tensor_tensor(out=v3(qdT[pb][:]), in0=qT[:, :, csl], in1=v3(egr[:]), op=ALU.mult), reads=[("qT", n // 4), K_("egr")], writes=[("qdT", pb)])
            tr_heads(S, U_[:], L_, K_("L"), K_("U"), PB_PREP)
            tr_heads(S, attnT[pb][:], attn, K_("attn"), ("attnT", pb), PB_PREP)
            S.dve(lambda e: e.tensor_tensor(out=v3(tU[:]), in0=v3(U_[:]), in1=bc_h(lvm[:, 0, :]), op=ALU.mult), reads=[K_("U"), "lvm"], writes=[K_("tU")])
            S.dve(lambda e: e.tensor_tensor(out=v3(Tt[0][:]), in0=v3(tU[:]), in1=identb_h, op=ALU.add), reads=[K_("tU"), "identb"], writes=[("Tt", ts, 0)])
            tr_heads(S, Tm[0][:], Tt[0], ("Tt", ts, 0), ("Tm", ts, 0), PB_PREP)
            cur = 0
            for lv in range(1, 7):
                nxt = 1 - cur
                last = (lv == 6)
                tt_c, tm_c = Tt[cur], Tm[cur]
                pm, pmk = psn(C, PB_PREP)
                for hl in range(HG):
                    hs = slice(hl * 128, (hl + 1) * 128)
                    S.pe(lambda e, pm=pm, hs=hs, tt_c=tt_c: e.matmul(pm[:, hs], lhsT=L_[:, hs], rhs=tt_c[:, hs], start=True, stop=True),
                         reads=[K_("L"), ("Tt", ts, cur)], writes=[pmk])
                S.dve(lambda e, pm=pm, lv=lv: e.tensor_tensor(out=v3(Mm[:]), in0=v3(pm[:, 0:W]), in1=bc_h(lvm[:, 2 * lv, :]), op=ALU.mult),
                      reads=[pmk, "lvm"], writes=[K_("Mm")])
                pt2, pt2k = psn(C, PB_PREP)
                S.pe(lambda e, pt2=pt2, tt_c=tt_c: e.matmul(pt2[:, 0:W], lhsT=identb[:], rhs=tt_c[:], start=True, stop=False),
                     reads=["identb", ("Tt", ts, cur)], writes=[pt2k])
                for hl in range(HG):
                    hs = slice(hl * 128, (hl + 1) * 128)
                    S.pe(lambda e, pt2=pt2, hs=hs, tm_c=tm_c, hl=hl: e.matmul(pt2[:, hs], lhsT=tm_c[:, hs], rhs=Mm[:, hs], start=False, stop=(hl == HG - 1),
                                                                            skip_group_check=True),
                         reads=[("Tm", ts, cur), K_("Mm")], writes=[pt2k])
                if last:
                    S.act(lambda e, pt2=pt2: e.copy(out=TtF[pb][:], in_=pt2[:, 0:W]), reads=[pt2k], writes=[("TtF", pb)])
                else:
                    S.act(lambda e, pt2=pt2, nxt=nxt: e.copy(out=Tt[nxt][:], in_=pt2[:, 0:W]), reads=[pt2k], writes=[("Tt", ts, nxt)])
                    tr_heads(S, Tm[nxt][:], Tt[nxt], ("Tt", ts, nxt), ("Tm", ts, nxt), PB_PREP)
                cur = nxt
            return S

        def rec(n):
            S = Stream()
            csl = slice(n * 128, (n + 1) * 128)
            pb = n % NIF
            ttf = TtF[pb]
            pks, pksk = psn(C, PB_REC)
            for hl in range(HG):
                hs = slice(hl * 128, (hl + 1) * 128)
                S.pe(lambda e, hs=hs, hl=hl: e.matmul(pks[:, hs], lhsT=kT[:, hl, csl], rhs=Sbf[:, hs], start=True, stop=True),
                     reads=[("kT", n // 4), "Sbf"], writes=[pksk])
            for hl in range(HG):
                hs = slice(hl * 128, (hl + 1) * 128)
                S.dve(lambda e, hs=hs, hl=hl: e.scalar_tensor_tensor(out=Rr[:, hs], in0=pks[:, hs], scalar=col["cb"][:, n, h0 + hl:h0 + hl + 1], in1=vb[:, n, hl, :],
                                                                     op0=ALU.mult, op1=ALU.add), reads=[pksk, "col_cb", ("vb", n // 4)], writes=["Rr"])
            pvn, pvnk = psn(C, PB_REC)
            for hl in range(HG):
                hs = slice(hl * 128, (hl + 1) * 128)
                S.pe(lambda e, hs=hs: e.matmul(pvn[:, hs], lhsT=ttf[:, hs], rhs=Rr[:, hs], start=True, stop=True),
                     reads=[("TtF", pb), "Rr"], writes=[pvnk])
            S.act(lambda e: e.copy(out=vnew[:], in_=pvn[:, 0:W]), reads=[pvnk], writes=["vnew"])
            S.dve(lambda e: e.tensor_tensor(out=v3(vdec[:]), in0=v3(pvn[:, 0:W]), in1=colb("edec", n), op=ALU.mult), reads=[pvnk, "col_edec"], writes=["vdec"])
            tr_heads_k(S, n)
            pss, pssk = psn(C, PB_REC)
            for hl in range(HG):
                hs = slice(hl * 128, (hl + 1) * 128)
                S.pe(lambda e, hs=hs: e.matmul(pss[:, hs], lhsT=ktok[:, hs], rhs=vdec[:, hs], start=True, stop=True),
                     reads=["ktok", "vdec"], writes=[pssk])
            po_, pok_ = psn(C, PB_REC)
            for hl in range(HG):
                hs = slice(hl * 128, (hl + 1) * 128)
                S.pe(lambda e, hs=hs: e.matmul(po_[:, hs], lhsT=qdT[pb][:, hs], rhs=Sbf[:, hs], start=True, stop=False),
                     reads=[("qdT", pb), "Sbf"], writes=[pok_])
                S.pe(lambda e, hs=hs: e.matmul(po_[:, hs], lhsT=attnT[pb][:, hs], rhs=vnew[:, hs], start=False, stop=True),
                     reads=[("attnT", pb), "vnew"], writes=[pok_])
            obank[n] = (po_, pok_)
            for hl in range(HG):
                hs = slice(hl * 128, (hl + 1) * 128)
                S.dve(lambda e, hs=hs, hl=hl: e.scalar_tensor_tensor(out=S32[:, hs], in0=S32[:, hs], scalar=col["gl"][:, n, h0 + hl:h0 + hl + 1], in1=pss[:, hs],
                                                                     op0=ALU.mult, op1=ALU.add), reads=["S32", "col_gl", pssk], writes=["S32"])
            S.act(lambda e: e.copy(out=Sbf[:], in_=S32[:]), reads=["S32"], writes=["Sbf"])
            return S

        def tr_heads_k(S, n):
            csl = slice(n * 128, (n + 1) * 128)
            pkt, pktk = psn(C, PB_REC)
            pktb = pkt[:].bitcast(BF16)
            for hl in range(HG):
                S.pe(lambda e, hl=hl: e.transpose(pktb[:, hl * 128:(hl + 1) * 128], kT[:, hl, csl], identb[:]),
                     reads=[("kT", n // 4), "identb"], writes=[pktk])
            S.act(lambda e: e.copy(out=ktok[:], in_=pktb[:, 0:W]), reads=[pktk], writes=["ktok"])

        def outp(n):
            S = Stream()
            po_, pok_ = obank.pop(n)
            S.act(lambda e: e.copy(out=o32[:], in_=po_[:, 0:W]), reads=[pok_], writes=["o32"])
            S.act(lambda e: e.activation(out=osq[:], in_=o32[:], func=AF.Square), reads=["o32"], writes=["osq"])
            S.dve(lambda e: e.tensor_reduce(out=oss[:], in_=v3(osq[:]), axis=AX.X, op=ALU.add), reads=["osq"], writes=["oss"])
            S.act(lambda e: e.activation(out=oss[:], in_=oss[:], func=AF.Ln, scale=1.0 / 128, bias=C.epsc[:, 0:1]), reads=["oss", "epsc"], writes=["oss"])
            S.act(lambda e: e.activation(out=oss[:], in_=oss[:], func=AF.Exp, scale=-0.5), reads=["oss"], writes=["oss"])
            S.dve(lambda e: e.tensor_tensor(out=v3(o32[:]), in0=v3(o32[:]), in1=oss[:].unsqueeze(2).to_broadcast([128, HG, 128]), op=ALU.mult),
                  reads=["o32", "oss"], writes=["o32"])
            S.dve(lambda e: e.tensor_tensor(out=v3(o32[:]), in0=v3(o32[:]), in1=bc_h(gnb[:]), op=ALU.mult), reads=["o32", "gnb"], writes=["o32"])
            S.dve(lambda e: e.tensor_tensor(out=og[:], in0=o32[:], in1=sgA[:, n, :], op=ALU.mult), reads=["o32", ("sgA", n)], writes=["og"])
            tr_heads(S, ogT[:], og, "og", "ogT", PB_OUT)
            for half in range(2):
                py, pyk = psn(C, PB_OUT)
                for hl in range(HG):
                    hs = slice(hl * 128, (hl + 1) * 128)
                    S.pe(lambda e, py=py, hs=hs, hl=hl, half=half: e.matmul(py[:], lhsT=ogT[:, hs], rhs=wog[:, hl, half * 512:(half + 1) * 512],
                                                                          start=(hl == 0), stop=(hl == HG - 1)), reads=["ogT", "wog"], writes=[pyk])
                S.dve(lambda e, py=py, half=half: e.tensor_tensor(out=X[:, n, half * 512:(half + 1) * 512], in0=X[:, n, half * 512:(half + 1) * 512],
                                                                 in1=py[:], op=ALU.add), reads=[pyk, ("X", n)], writes=[("X", n)])
            return S

        pend = {}

        def prep_slices(m):
            ops = prep(m).ops
            k = (len(ops) + NPRE - 1) // NPRE
            out_ = []
            for i in range(NPRE):
                st = Stream()
                st.ops = ops[i * k:(i + 1) * k]
                out_.append(st)
            return out_

        for r in range(-NPRE, NT + 1):
            streams = []
            if 0 <= r < NT:
                streams.append(rec(r))
            if 1 <= r <= NT:
                streams.append(outp(r - 1))
            for m in range(r + 1, r + NPRE + 1):
                if 0 <= m < NT:
                    if m not in pend:
                        pend[m] = prep_slices(m)
                    streams.append(pend[m][r - m + NPRE])
            merge_streams(P, streams)
        C.scope_end()


def build_fused(parts=None):
    if parts is None:
        parts = []
        for layer in range(4):
            parts.append(("gdn" if layer % 2 == 0 else "mla", layer))
            parts.append(("ffn", layer))
    C = Ctx()
    hT = C.sb("hT", [128, 8, T], BF16)
    R = setup(C)
    for kind, layer in parts:
        C.scope_begin()
        if kind == "gdn":
            gdn_body(C, R, hT, layer)
        elif kind == "mla":
            mla_body(C, R, hT, layer)
        else:
            ffn_body(C, R, hT, layer, layer == 3)
        C.scope_end()
    store_x(C, R["X"])
    C.P.finish()
    C.P.emit()
    return C.nc


def make_maps(inp):
    consts = {"ident": _ident(), "invf": _invf64(), "tri": _tri(), "gmasks": _gmasks()}
    shared = {}
    for nm in INPUT_SHAPES:
        if nm in ("x", "c", "positions") or nm in consts:
            continue
        shared[nm] = np.ascontiguousarray(np.asarray(inp[nm]), dtype=np.float32)
    maps = []
    for b in range(8):
        m = dict(shared)
        m.update(consts)
        m["x"] = np.ascontiguousarray(inp["x"][b], dtype=np.float32)
        m["c"] = np.ascontiguousarray(inp["c"][b], dtype=np.float32)
        m["positions"] = np.ascontiguousarray(inp["positions"][b]).astype(np.int32)
        maps.append(m)
    return maps


_NC = {}


def kernel(**inputs):
    inp = {k: np.asarray(v) for k, v in inputs.items()}
    if "nc" not in _NC:
        _NC["nc"] = build_fused()
    r = run_bass_kernel_spmd(_NC["nc"], make_maps(inp), core_ids=list(range(8)))
    return np.ascontiguousarray(np.stack([r.results[b]["out"] for b in range(8)], axis=0), dtype=np.float32)
```

```python
from contextlib import ExitStack
import numpy as np
import concourse.bass as bass
import concourse.mybir as mybir
from concourse.bass_utils import run_bass_kernel_spmd

F32 = mybir.dt.float32
BF16 = mybir.dt.bfloat16
I32 = mybir.dt.int32
AF = mybir.ActivationFunctionType
ALU = mybir.AluOpType
AX = mybir.AxisListType

ENGS = ["pe", "act", "dve", "pool", "sp"]
NDMASEM = 8

D = 1024
T = 2048
NT = 16
DFF = 2816
NFC = 22
EPS = 1e-6


class Op:
    __slots__ = ("eng", "fn", "waits", "dwaits", "idx", "is_dma", "dsem", "dval", "tag", "calls")


class _Rec:
    def __init__(self):
        self.calls = []

    def __getattr__(self, name):
        def f(*a, **k):
            self.calls.append((name, a, k))
            return self
        return f


class Prog:
    def __init__(self, nc):
        self.nc = nc
        self.ops = {e: [] for e in ENGS}
        self.known = {e: {f: 0 for f in ENGS} for e in ENGS}
        self.kdma = {e: {} for e in ENGS}
        self.snaps = {e: [] for e in ENGS}
        self.last_w = {}
        self.readers = {}
        self.ndma = {e: 0 for e in ENGS}
        self.out_tokens = []
        self.milestones = {e: set() for e in ENGS}

    def _need(self, eng, tok, waits, dwaits):
        if tok is None:
            return
        if tok[0] == "e":
            _, f, idx = tok
            if f == eng and eng == "pe":
                return
            if self.known[eng][f] >= idx + 1:
                return
            waits.append((f, idx))
            self.milestones[f].add(idx)
            kn, kd = self.snaps[f][idx]
            for g in ENGS:
                if kn[g] > self.known[eng][g]:
                    self.known[eng][g] = kn[g]
            for k, v in kd.items():
                if self.kdma[eng].get(k, 0) < v:
                    self.kdma[eng][k] = v
            if self.known[eng][f] < idx + 1:
                self.known[eng][f] = idx + 1
        else:
            _, q, semi, val = tok
            key = (q, semi)
            if self.kdma[eng].get(key, 0) >= val:
                return
            dwaits.append((q, semi, val))
            self.kdma[eng][key] = val

    def op(self, eng, fn, reads=(), writes=(), dma=False, tag=None):
        rec = _Rec()
        fn(rec)
        return self.submit(eng, rec.calls, reads, writes, dma=dma, tag=tag)

    def submit(self, eng, calls, reads=(), writes=(), dma=False, tag=None):
        o = Op()
        o.eng = eng
        o.fn = True
        o.tag = tag
        o.is_dma = dma
        o.calls = calls
        assert len(o.calls) == 1
        waits, dwaits = [], []
        cand = []
        for k in reads:
            tok = self.last_w.get(k)
            if tok is not None:
                cand.append(tok)
            if isinstance(k, tuple) and k[0] == "ps":
                for rt in self.readers.get(k, ()):
                    if not (rt[0] == "e" and rt[1] == eng):
                        cand.append(rt)
        for k in writes:
            tok = self.last_w.get(k)
            if tok is not None:
                if not (tok[0] == "e" and tok[1] == eng):
                    cand.append(tok)
            for rt in self.readers.get(k, ()):
                if not (rt[0] == "e" and rt[1] == eng):
                    cand.append(rt)
        best = {}
        for tok in cand:
            kk = (tok[0], tok[1]) if tok[0] == "e" else (tok[0], tok[1], tok[2])
            if kk not in best or tok[-1] > best[kk][-1]:
                best[kk] = tok
        for tok in best.values():
            self._need(eng, tok, waits, dwaits)
        idx = len(self.ops[eng])
        o.idx = idx
        if dma:
            n = self.ndma[eng]
            self.ndma[eng] = n + 1
            semi = n % NDMASEM
            val = 16 * (n // NDMASEM + 1)
            if val > 16:
                self._need(eng, ("d", eng, semi, val - 16), waits, dwaits)
            o.dsem, o.dval = semi, val
            tok = ("d", eng, semi, val)
        else:
            tok = ("e", eng, idx)
        o.waits, o.dwaits = waits, dwaits
        self.ops[eng].append(o)
        self.snaps[eng].append((dict(self.known[eng]), dict(self.kdma[eng])))
        for k in reads:
            lst = self.readers.setdefault(k, [])
            if tok[0] == "e":
                lst[:] = [r for r in lst if not (r[0] == "e" and r[1] == tok[1])]
            lst.append(tok)
        for k in writes:
            self.last_w[k] = tok
            self.readers[k] = []
        return tok

    def pe(self, fn, reads=(), writes=()):
        return self.op("pe", fn, reads, writes)

    def act(self, fn, reads=(), writes=()):
        return self.op("act", fn, reads, writes)

    def dve(self, fn, reads=(), writes=()):
        return self.op("dve", fn, reads, writes)

    def pool(self, fn, reads=(), writes=()):
        return self.op("pool", fn, reads, writes)

    def dma(self, q, out, in_, reads=(), writes=(), is_out=False, **kw):
        tok = self.op(q, lambda e: e.dma_start(out=out, in_=in_, **kw), reads, writes, dma=True)
        if is_out:
            self.out_tokens.append(tok)
        return tok

    def barrier(self):
        toks = []
        for f in ENGS:
            for o in reversed(self.ops[f]):
                if (not o.is_dma) and o.fn is not None:
                    toks.append(("e", f, o.idx))
                    break
            n = self.ndma[f]
            for i in range(max(0, n - NDMASEM), n):
                toks.append(("d", f, i % NDMASEM, 16 * (i // NDMASEM + 1)))
        for e in ENGS:
            waits, dwaits = [], []
            for tok in toks:
                if tok[0] == "e" and tok[1] == e:
                    if e == "pe":
                        continue
                self._need(e, tok, waits, dwaits)
            if not waits and not dwaits:
                continue
            o = Op()
            o.eng = e; o.fn = None; o.waits = waits; o.dwaits = dwaits
            o.idx = len(self.ops[e]); o.is_dma = False; o.tag = "barrier"
            self.ops[e].append(o)
            self.snaps[e].append((dict(self.known[e]), dict(self.kdma[e])))

    def finish(self):
        waits, dwaits = [], []
        for tok in self.out_tokens:
            self._need("sp", tok, waits, dwaits)
        o = Op()
        o.eng = "sp"; o.fn = None; o.waits = waits; o.dwaits = dwaits
        o.idx = len(self.ops["sp"]); o.is_dma = False; o.tag = "finish"
        self.ops["sp"].append(o)
        self.snaps["sp"].append((dict(self.known["sp"]), dict(self.kdma["sp"])))

    def emit(self):
        nc = self.nc
        rank = {}
        for e in ENGS:
            ms = sorted(self.milestones[e])
            rank[e] = {idx: i + 1 for i, idx in enumerate(ms)}
        with ExitStack() as es:
            esem = {e: es.enter_context(nc.semaphore("pg_" + e)) for e in ENGS}
            dsem = {}
            for q in ENGS:
                for i in range(min(NDMASEM, self.ndma[q])):
                    dsem[(q, i)] = es.enter_context(nc.semaphore("dm_%s_%d" % (q, i)))
            block = es.enter_context(nc.Block())

            def run(eng_name):
                def body(eng):
                    for o in self.ops[eng_name]:
                        for (f, idx) in o.waits:
                            eng.wait_ge(esem[f], rank[f][idx])
                        for (q, semi, val) in o.dwaits:
                            eng.wait_ge(dsem[(q, semi)], val)
                        if o.fn is None:
                            continue
                        ins = None
                        for (nm, a, k) in o.calls:
                            ins = getattr(eng, nm)(*a, **k)
                        if o.is_dma:
                            ins.then_inc(dsem[(eng_name, o.dsem)], 16)
                        elif o.idx in rank[eng_name]:
                            ins.then_inc(esem[eng_name], 1)
                return body

            block.tensor(run("pe"))
            block.scalar(run("act"))
            block.vector(run("dve"))
            block.gpsimd(run("pool"))
            block.sync(run("sp"))


class Stream:
    def __init__(self):
        self.ops = []

    def _add(self, eng, fn, reads, writes):
        rec = _Rec()
        fn(rec)
        self.ops.append((eng, rec.calls, tuple(reads), tuple(writes)))

    def pe(self, fn, reads=(), writes=()):
        self._add("pe", fn, reads, writes)

    def act(self, fn, reads=(), writes=()):
        self._add("act", fn, reads, writes)

    def dve(self, fn, reads=(), writes=()):
        self._add("dve", fn, reads, writes)

    def pool(self, fn, reads=(), writes=()):
        self._add("pool", fn, reads, writes)


def merge_streams(P, streams):
    lists = [st.ops for st in streams if st is not None and st.ops]
    idx = [0] * len(lists)
    tot = [len(l) for l in lists]
    while True:
        live = [k for k in range(len(lists)) if idx[k] < tot[k]]
        if not live:
            break
        k = min(live, key=lambda q: idx[q] / tot[q])
        eng, calls, reads, writes = lists[k][idx[k]]
        P.submit(eng, calls, reads, writes)
        idx[k] += 1


class Ctx:
    def __init__(self):
        self.nc = bass.Bass("TRN2", target_bir_lowering=False)
        self.P = Prog(self.nc)
        self.es = ExitStack()
        self.banks = [self.es.enter_context(self.nc.psum_tensor("pb%d" % i, [128, 512], F32)) for i in range(8)]
        self.nbank = 0
        self.uid = 0
        self.stacks = [self.es]
        self.pcount = {}

    def sb(self, name, shape, dt):
        self.uid += 1
        return self.stacks[-1].enter_context(self.nc.sbuf_tensor("%s_%d" % (name, self.uid), shape, dt))

    def scope_begin(self):
        self.stacks.append(ExitStack())

    def scope_end(self):
        self.P.barrier()
        self.stacks.pop().close()

    def din(self, name, shape, dt=F32):
        return self.nc.dram_tensor(name, list(shape), dt, kind="ExternalInput").ap()

    def dout(self, name, shape, dt=F32):
        return self.nc.dram_tensor(name, list(shape), dt, kind="ExternalOutput").ap()

    def ps(self):
        i = self.nbank % 8
        self.nbank += 1
        return self.banks[i], ("ps", i)

    def key(self, base):
        self.uid += 1
        return (base, self.uid)


def load_cols(C, vec_d, n, name, ident32):
    P = C.P
    rows = C.sb(name + "_r", [n, 128], F32)
    cols = C.sb(name + "_c", [128, n], F32)
    P.dma("sp", rows[:], vec_d.rearrange("(c p) -> c p", p=128), writes=[name + "_r"])
    pb, pk = C.ps()
    P.pe(lambda e: e.transpose(pb[:, 0:n], rows[:], ident32[0:n, 0:n]), reads=[name + "_r", "ident32"], writes=[pk])
    P.dve(lambda e: e.tensor_copy(out=cols[:], in_=pb[:, 0:n]), reads=[pk], writes=[name + "_c"])
    return cols, name + "_c"


INPUT_SHAPES = {
    "x": ([T, D], F32), "c": ([D], F32), "positions": ([T], I32),
    "ada_w": ([4, D, 6 * D], F32), "ada_b": ([4, 6 * D], F32), "norm_mix_g": ([4, D], F32), "norm_ffn_g": ([4, D], F32),
    "gdn_w_in": ([2, D, 4112], F32), "gdn_conv_w": ([2, 3072, 4], F32), "gdn_a_log": ([2, 8], F32), "gdn_dt_bias": ([2, 8], F32),
    "gdn_norm_g": ([2, 128], F32), "gdn_w_out": ([2, D, D], F32),
    "mla_w_in": ([2, D, 704], F32), "mla_q_norm_g": ([2, 384], F32), "mla_kv_norm_g": ([2, 256], F32),
    "mla_w_uq": ([2, 384, 1536], F32), "mla_w_ukv": ([2, 256, 2048], F32), "mla_w_out": ([2, D, D], F32),
    "ffn_w_gate": ([4, D, DFF], F32), "ffn_w_up": ([4, D, DFF], F32), "ffn_w_down": ([4, DFF, D], F32),
    "final_norm_g": ([D], F32),
    "ident": ([128, 128], F32), "invf": ([128], F32), "tri": ([128, 128], F32), "gmasks": ([19, 128, 128], F32),
}


def setup(C):
    P = C.P
    C.I = {nm: C.din(nm, shp, dt) for nm, (shp, dt) in INPUT_SHAPES.items()}
    I = C.I
    ident32 = C.sb("ident32", [128, 128], F32)
    identb = C.sb("identb", [128, 128], BF16)
    ones32 = C.sb("ones32", [128, 128], F32)
    X = C.sb("X", [128, NT, D], F32)
    modc = C.sb("modc", [128, 24], F32)
    gs = C.sb("gs", [128, 8], F32)
    gate_bc = C.sb("gate_bc", [128, D], F32)
    ccol = C.sb("ccol", [128, 8], F32)
    ccolb = C.sb("ccolb", [128, 8], BF16)
    C.epsc = C.sb("epsc", [128, 1], F32)
    P.dve(lambda e: e.memset(C.epsc[:], EPS), writes=["epsc"])
    P.dma("sp", ident32[:], I["ident"], writes=["ident32"])
    P.dma("pool", identb[:], I["ident"], writes=["identb"])
    P.dve(lambda e: e.memset(ones32[:], 1.0), writes=["ones32"])
    xv = I["x"].rearrange("(t p) d -> p t d", p=128)
    for i in range(4):
        P.dma("sp", X[:, 4 * i:4 * i + 4, :], xv[:, 4 * i:4 * i + 4, :], writes=[("X", t) for t in range(4 * i, 4 * i + 4)])
    C.scope_begin()
    ccol_raw, ck = load_cols(C, I["c"], 8, "cc", ident32)
    P.act(lambda e: e.activation(out=ccol[:], in_=ccol_raw[:], func=AF.Silu), reads=[ck], writes=["ccol"])
    P.dve(lambda e: e.tensor_copy(out=ccolb[:], in_=ccol[:]), reads=["ccol"], writes=["ccolb"])
    C.scope_end()
    return dict(X=X, modc=modc, gs=gs, gate_bc=gate_bc, ident32=ident32, identb=identb, ones32=ones32, ccol=ccol, ccolb=ccolb)


def mod_compute(C, R, layer, groups, ng_d):
    P = C.P
    I = C.I
    modc, gs, gate_bc, ident32, ones32, ccol = R["modc"], R["gs"], R["gate_bc"], R["ident32"], R["ones32"], R["ccol"]
    adaw_d = I["ada_w"][layer]
    adab_d = I["ada_b"][layer]
    C.scope_begin()
    bcol, bk = load_cols(C, adab_d, 48, "ab", ident32)
    gcol, gk = load_cols(C, ng_d, 8, "ng", ident32)
    aws = [C.sb("aw%d" % i, [128, 8, 512], BF16) for i in range(2)]
    ccolb = R["ccolb"]
    pb, pk = C.ps()
    adv = adaw_d.rearrange("(k p) f -> p k f", p=128)
    na = 0
    for gi, g in enumerate(groups):
        for jj in range(2):
            aw = aws[na % 2]
            awk = ("aw", na % 2)
            na += 1
            P.dma("pool", aw[:], adv[:, :, g * D + jj * 512:g * D + (jj + 1) * 512], writes=[awk])
            for j4 in range(4):
                j = jj * 4 + j4
                for k in range(8):
                    P.pe(lambda e, j=j, j4=j4, k=k, gi=gi, aw=aw: e.matmul(pb[:, gi * 8 + j:gi * 8 + j + 1], lhsT=aw[:, k, j4 * 128:(j4 + 1) * 128],
                                                                         rhs=ccolb[:, k:k + 1], start=(k == 0), stop=(k == 7)),
                         reads=[awk, "ccolb"], writes=[pk])
    for gi, g in enumerate(groups):
        P.dve(lambda e, gi=gi, g=g: e.tensor_tensor(out=modc[:, gi * 8:gi * 8 + 8], in0=pb[:, gi * 8:gi * 8 + 8],
                                                    in1=bcol[:, g * 8:g * 8 + 8], op=ALU.add),
              reads=[pk, bk], writes=["modc"])
    P.dve(lambda e: e.scalar_tensor_tensor(out=gs[:], in0=modc[:, 8:16], scalar=1.0, in1=gcol[:], op0=ALU.add, op1=ALU.mult),
          reads=["modc", gk], writes=["gs"])
    dg = C.sb("dg", [128, 128], F32)
    for j in range(8):
        P.dve(lambda e, j=j: e.tensor_scalar(out=dg[:], in0=ident32[:], scalar1=modc[:, 16 + j:17 + j], scalar2=None, op0=ALU.mult),
              reads=["ident32", "modc"], writes=["dg"])
        pb2, pk2 = C.ps()
        P.pe(lambda e, pb2=pb2: e.matmul(pb2[:, 0:128], lhsT=ones32[:], rhs=dg[:], start=True, stop=True),
             reads=["ones32", "dg"], writes=[pk2])
        P.act(lambda e, j=j, pb2=pb2: e.copy(out=gate_bc[:, j * 128:(j + 1) * 128], in_=pb2[:, 0:128]), reads=[pk2], writes=["gate_bc"])
    C.scope_end()


def rstd_col(C, X, t, small, ki, junk):
    P = C.P
    kk = ("small", ki)
    P.act(lambda e: e.activation(out=junk[:], in_=X[:, t, :], func=AF.Square, accum_out=small[:, ki:ki + 1]),
          reads=[("X", t)], writes=["junk", kk])
    P.act(lambda e: e.activation(out=small[:, ki:ki + 1], in_=small[:, ki:ki + 1], func=AF.Ln, scale=1.0 / D, bias=C.epsc[:, 0:1]),
          reads=[kk, "epsc"], writes=[kk])
    P.act(lambda e: e.activation(out=small[:, ki:ki + 1], in_=small[:, ki:ki + 1], func=AF.Exp, scale=-0.5),
          reads=[kk], writes=[kk])
    return kk


def norm_to_hT(C, R, hT):
    P = C.P
    X, modc, gs, identb = R["X"], R["modc"], R["gs"], R["identb"]
    C.scope_begin()
    small = C.sb("nsmall", [128, NT], F32)
    junk = C.sb("junk", [128, D], BF16)
    tmps = [C.sb("tmpn%d" % i, [128, D], F32) for i in range(2)]
    xn = [C.sb("xn%d" % i, [128, D], BF16) for i in range(2)]
    for t in range(NT):
        tmp = tmps[t % 2]
        tk = ("tmpn", t % 2)
        kk = rstd_col(C, X, t, small, t, junk)
        xb = xn[t % 2]
        xk = ("xn", t % 2)
        P.dve(lambda e, xb=xb, t=t: e.tensor_scalar(out=xb[:], in0=X[:, t, :], scalar1=small[:, t:t + 1], scalar2=None, op0=ALU.mult),
              reads=[("X", t), kk], writes=[xk])
        pb, pk = C.ps()
        pbb = pb[:].bitcast(BF16)
        for c in range(8):
            P.pe(lambda e, c=c, xb=xb, pbb=pbb: e.transpose(pbb[:, c * 128:(c + 1) * 128], xb[:, c * 128:(c + 1) * 128], identb[:]),
                 reads=[xk, "identb"], writes=[pk])
        P.dve(lambda e, pbb=pbb, tmp=tmp: e.tensor_tensor(out=tmp[:].rearrange("p (c i) -> p c i", c=8), in0=pbb.rearrange("p (c i) -> p c i", c=8),
                                                          in1=gs[:].unsqueeze(2).to_broadcast([128, 8, 128]), op=ALU.mult),
              reads=[pk, "gs"], writes=[tk])
        P.pool(lambda e, t=t, tmp=tmp: e.tensor_tensor(out=hT[:, :, t * 128:(t + 1) * 128], in0=tmp[:].rearrange("p (c i) -> p c i", c=8),
                                                       in1=modc[:, 0:8].unsqueeze(2).to_broadcast([128, 8, 128]), op=ALU.add),
               reads=[tk, "modc"], writes=[("hT", t)])
    C.scope_end()


def store_x(C, X, name="out"):
    P = C.P
    o_d = C.dout(name, [T, D])
    ov = o_d.rearrange("(t p) d -> p t d", p=128)
    for i in range(4):
        P.dma("sp", ov[:, 4 * i:4 * i + 4, :], X[:, 4 * i:4 * i + 4, :], reads=[("X", t) for t in range(4 * i, 4 * i + 4)], is_out=True)


def ffn_body(C, R, hT, layer, final):
    P = C.P
    I = C.I
    ng_d = I["norm_ffn_g"][layer]
    wg_d = I["ffn_w_gate"][layer]
    wu_d = I["ffn_w_up"][layer]
    wd_d = I["ffn_w_down"][layer]
    mod_compute(C, R, layer, [3, 4, 5], ng_d)
    X, gate_bc = R["X"], R["gate_bc"]
    norm_to_hT(C, R, hT)
    G = 4
    wgu = [C.sb("wgu%d" % i, [128, 8, 2, G * 128], BF16) for i in range(2)]
    wdn = [C.sb("wdn%d" % i, [128, G, D], BF16) for i in range(2)]
    hid = [C.sb("hid%d" % i, [128, G, 512], BF16) for i in range(2)]
    sg = [C.sb("sg%d" % i, [128, 512], F32) for i in range(2)]
    wgv = wg_d.rearrange("(k p) f -> p k f", p=128)
    wuv = wu_d.rearrange("(k p) f -> p k f", p=128)
    wdv = wd_d.rearrange("(c p) n -> p c n", p=128)
    groups = []
    c0 = 0
    while c0 < NFC:
        g = min(G, NFC - c0)
        groups.append((c0, g))
        c0 += g
    nsg = 0

    def load_group(gi):
        c0, g = groups[gi]
        b = gi % 2
        kgu = ("wgu", b)
        kd = ("wdn", b)
        P.dma("pool", wgu[b][:, :, 0, 0:g * 128], wgv[:, :, c0 * 128:(c0 + g) * 128], writes=[kgu])
        P.dma("pool", wgu[b][:, :, 1, 0:g * 128], wuv[:, :, c0 * 128:(c0 + g) * 128], writes=[kgu])
        P.dma("pool", wdn[b][:, 0:g, :], wdv[:, c0:c0 + g, :], writes=[kd])
        P.pool(lambda e: e.tensor_tensor(out=wdn[b][:, 0:g, :], in0=wdn[b][:, 0:g, :],
                                         in1=gate_bc[:].unsqueeze(1).to_broadcast([128, g, D]), op=ALU.mult),
               reads=[kd, "gate_bc"], writes=[kd])

    def gate_up(gi, tb, hb):
        nonlocal nsg
        c0, g = groups[gi]
        b = gi % 2
        kgu = ("wgu", b)
        hk = ("hid", hb)
        for j in range(g):
            pg, pgk = C.ps()
            pu, puk = C.ps()
            for k in range(8):
                P.pe(lambda e, k=k: e.matmul(pg[:], lhsT=wgu[b][:, k, 0, j * 128:(j + 1) * 128], rhs=hT[:, k, tb * 512:(tb + 1) * 512],
                                            start=(k == 0), stop=(k == 7)),
                     reads=[kgu] + [("hT", 4 * tb + i) for i in range(4)], writes=[pgk])
            for k in range(8):
                P.pe(lambda e, k=k: e.matmul(pu[:], lhsT=wgu[b][:, k, 1, j * 128:(j + 1) * 128], rhs=hT[:, k, tb * 512:(tb + 1) * 512],
                                            start=(k == 0), stop=(k == 7)),
                     reads=[kgu] + [("hT", 4 * tb + i) for i in range(4)], writes=[puk])
            sb_ = nsg % 2
            nsg += 1
            sk = ("sg", sb_)
            P.act(lambda e: e.activation(out=sg[sb_][:], in_=pg[:], func=AF.Silu), reads=[pgk], writes=[sk])
            P.dve(lambda e: e.tensor_tensor(out=hid[hb][:, j, :], in0=sg[sb_][:], in1=pu[:], op=ALU.mult), reads=[sk, puk], writes=[hk])

    def down(gi, tb, hb):
        c0, g = groups[gi]
        b = gi % 2
        kd = ("wdn", b)
        hk = ("hid", hb)
        for tt in range(4):
            t = tb * 4 + tt
            for half in range(2):
                po, pok = C.ps()
                for j in range(g):
                    P.pe(lambda e, j=j: e.matmul(po[:], lhsT=hid[hb][:, j, tt * 128:(tt + 1) * 128], rhs=wdn[b][:, j, half * 512:(half + 1) * 512],
                                                start=(j == 0), stop=(j == g - 1)), reads=[hk, kd], writes=[pok])
                P.dve(lambda e: e.tensor_tensor(out=X[:, t, half * 512:(half + 1) * 512], in0=X[:, t, half * 512:(half + 1) * 512], in1=po[:], op=ALU.add),
                      reads=[pok, ("X", t)], writes=[("X", t)])

    items = [(gi, tb) for gi in range(len(groups)) for tb in range(4)]
    load_group(0)
    prev = None
    for ii, (gi, tb) in enumerate(items):
        hb = ii % 2
        gate_up(gi, tb, hb)
        if prev is not None:
            down(*prev)
        if tb == 0 and gi + 1 < len(groups):
            load_group(gi + 1)
        prev = (gi, tb, hb)
    down(*prev)
    if final:
        fg_d = I["final_norm_g"]
        fgb = C.sb("fgb", [128, D], F32)
        P.dma("sp", fgb[:], fg_d.unsqueeze(0).to_broadcast([128, D]), writes=["fgb"])
        small = C.sb("fsmall", [128, NT], F32)
        fjunk = C.sb("fjunk", [128, D], BF16)
        for t in range(NT):
            kk = rstd_col(C, X, t, small, t, fjunk)
            P.dve(lambda e, t=t: e.scalar_tensor_tensor(out=X[:, t, :], in0=X[:, t, :], scalar=small[:, t:t + 1], in1=fgb[:],
                                                        op0=ALU.mult, op1=ALU.mult),
                  reads=[("X", t), kk, "fgb"], writes=[("X", t)])


def _ident():
    return np.eye(128, dtype=np.float32)


QR = 384
KVR = 256
SCALE = float(192 ** -0.5)
TWO_PI = float(2 * np.pi)


def psn(C, pool):
    i = pool[C.pcount.get(tuple(pool), 0) % len(pool)]
    C.pcount[tuple(pool)] = C.pcount.get(tuple(pool), 0) + 1
    return C.banks[i], ("ps", i)


def mla_body(C, R, hT, layer):
    P = C.P
    I = C.I
    j_ = layer // 2
    ng_d = I["norm_mix_g"][layer]
    win_d = I["mla_w_in"][j_]
    qg_d = I["mla_q_norm_g"][j_]
    kvg_d = I["mla_kv_norm_g"][j_]
    wuq_d = I["mla_w_uq"][j_]
    wukv_d = I["mla_w_ukv"][j_]
    wout_d = I["mla_w_out"][j_]
    pos_d = I["positions"]
    invf_d = I["invf"]
    tri_d = I["tri"]
    mod_compute(C, R, layer, [0, 1, 2], ng_d)
    X, gate_bc, ident32, identb = R["X"], R["gate_bc"], R["ident32"], R["identb"]
    norm_to_hT(C, R, hT)
    ALLB = list(range(8))

    trib = C.sb("trib", [128, 128], BF16)
    P.dma("pool", trib[:], tri_d, writes=["trib"])
    wbuf = C.sb("wbuf", [128, 8 * D], BF16)
    w_in = wbuf[:, 0:8 * 704].rearrange("p (k f) -> p k f", k=8)
    w_out = wbuf[:].rearrange("p (k f) -> p k f", k=8)
    P.dma("pool", w_in, win_d.rearrange("(k p) f -> p k f", p=128), writes=["wbuf"])
    w_uq = C.sb("w_uq", [128, 3, 1536], BF16)
    P.dma("pool", w_uq[:], wuq_d.rearrange("(k p) f -> p k f", p=128), writes=["w_uq"])
    w_ukv = C.sb("w_ukv", [128, 2, 2048], BF16)
    P.dma("pool", w_ukv[:], wukv_d.rearrange("(k p) f -> p k f", p=128), writes=["w_ukv"])
    wk2A = C.sb("wk2A", [128, 8, 128], BF16)
    wk2B = C.sb("wk2B", [128, 8, 128], BF16)
    for half in range(2):
        o_ = half * 64
        P.act(lambda e, o_=o_: e.copy(out=wk2A[:, :, o_:o_ + 64], in_=w_in[:, :, 640:704]), reads=["wbuf"], writes=["wk2A"])
        P.act(lambda e, o_=o_: e.mul(out=wk2B[:, :, o_:o_ + 32], in_=w_in[:, :, 672:704], mul=-1.0), reads=["wbuf"], writes=["wk2B"])
        P.act(lambda e, o_=o_: e.copy(out=wk2B[:, :, o_ + 32:o_ + 64], in_=w_in[:, :, 640:672]), reads=["wbuf"], writes=["wk2B"])
    wq2 = C.sb("wq2", [128, 3, 8, 128], BF16)
    wq4 = w_uq[:].rearrange("p k (h f) -> p k h f", h=8)
    P.act(lambda e: e.copy(out=wq2[:, :, :, 0:64], in_=wq4[:, :, :, 128:192]), reads=["w_uq"], writes=["wq2"])
    P.act(lambda e: e.mul(out=wq2[:, :, :, 64:96], in_=wq4[:, :, :, 160:192], mul=-1.0), reads=["w_uq"], writes=["wq2"])
    P.act(lambda e: e.copy(out=wq2[:, :, :, 96:128], in_=wq4[:, :, :, 128:160]), reads=["w_uq"], writes=["wq2"])
    qgc, qgk = load_cols(C, qg_d, 3, "qg", ident32)
    kvgc, kvgk = load_cols(C, kvg_d, 2, "kvg", ident32)
    g5 = C.sb("g5", [128, 5], F32)
    P.dve(lambda e: e.tensor_copy(out=g5[:, 0:3], in_=qgc[:]), reads=[qgk], writes=["g5"])
    P.dve(lambda e: e.tensor_copy(out=g5[:, 3:5], in_=kvgc[:]), reads=[kvgk], writes=["g5"])

    cosT = C.sb("cosT", [128, T], BF16)
    sinT = C.sb("sinT", [128, T], BF16)
    cs2 = C.sb("cs2", [128, T], BF16)
    invf = C.sb("invf", [128, 1], F32)
    P.dma("sp", invf[:], invf_d.rearrange("(p o) -> p o", o=1), writes=["invf"])
    C.scope_begin()
    posi = C.sb("posi", [128, 512], I32)
    xs = C.sb("xs", [128, 512], F32)
    ri = C.sb("ri", [128, 512], I32)
    rf = C.sb("rf", [128, 512], F32)
    for tb in range(4):
        sl = slice(tb * 512, (tb + 1) * 512)
        P.dma("sp", posi[:], pos_d[sl].unsqueeze(0).to_broadcast([128, 512]), writes=["posi"])
        P.dve(lambda e: e.tensor_copy(out=xs[:], in_=posi[:]), reads=["posi"], writes=["xs"])
        P.dve(lambda e: e.tensor_scalar(out=xs[:], in0=xs[:], scalar1=invf[:, 0:1], scalar2=1.0 / TWO_PI, op0=ALU.mult, op1=ALU.mult),
              reads=["xs", "invf"], writes=["xs"])
        for which, tab in ((0, sinT), (1, cosT)):
            if which == 1:
                P.dve(lambda e: e.tensor_scalar(out=xs[:], in0=xs[:], scalar1=0.25, scalar2=None, op0=ALU.add), reads=["xs"], writes=["xs"])
            P.dve(lambda e: e.tensor_copy(out=ri[:], in_=xs[:]), reads=["xs"], writes=["ri"])
            P.dve(lambda e: e.tensor_copy(out=rf[:], in_=ri[:]), reads=["ri"], writes=["rf"])
            P.dve(lambda e: e.tensor_tensor(out=rf[:], in0=xs[:], in1=rf[:], op=ALU.subtract), reads=["xs", "rf"], writes=["rf"])
            P.act(lambda e, tab=tab, sl=sl: e.activation(out=tab[:, sl], in_=rf[:], func=AF.Sin, scale=TWO_PI * (1.0 - 2e-7)),
                  reads=["rf"], writes=[("cs", tb)])
        P.act(lambda e, sl=sl: e.copy(out=cs2[0:64, sl], in_=cosT[0:64, sl]), reads=[("cs", tb)], writes=[("cs2", tb)])
        P.act(lambda e, sl=sl: e.copy(out=cs2[64:128, sl], in_=sinT[64:128, sl]), reads=[("cs", tb)], writes=[("cs2", tb)])
    C.scope_end()

    cT = C.sb("cT", [128, 5, T], BF16)
    C.scope_begin()
    mjunk = C.sb("mjunk", [128, 640], BF16)
    clats = [C.sb("clat%d" % i, [128, 640], F32) for i in range(2)]
    cns = [C.sb("cn%d" % i, [128, 640], BF16) for i in range(2)]
    sms = [C.sb("msmall%d" % i, [128, 4], F32) for i in range(2)]
    for t in range(NT):
        clat, cn, sm = clats[t % 2], cns[t % 2], sms[t % 2]
        kcl, kcn, ksm = ("clat", t % 2), ("cn", t % 2), ("msm", t % 2)
        pa, pak = psn(C, ALLB)
        pb_, pbk = psn(C, ALLB)
        for k in range(8):
            P.pe(lambda e, k=k, t=t, pa=pa: e.matmul(pa[:], lhsT=hT[:, k, t * 128:(t + 1) * 128], rhs=w_in[:, k, 0:512],
                                                   start=(k == 0), stop=(k == 7)), reads=[("hT", t), "wbuf"], writes=[pak])
        for k in range(8):
            P.pe(lambda e, k=k, t=t, pb_=pb_: e.matmul(pb_[:, 0:128], lhsT=hT[:, k, t * 128:(t + 1) * 128], rhs=w_in[:, k, 512:640],
                                                     start=(k == 0), stop=(k == 7)), reads=[("hT", t), "wbuf"], writes=[pbk])
        P.act(lambda e, pa=pa, clat=clat: e.copy(out=clat[:, 0:512], in_=pa[:]), reads=[pak], writes=[kcl])
        P.act(lambda e, pb_=pb_, clat=clat: e.copy(out=clat[:, 512:640], in_=pb_[:, 0:128]), reads=[pbk], writes=[kcl])
        P.act(lambda e, clat=clat, sm=sm: e.activation(out=mjunk[:, 0:384], in_=clat[:, 0:384], func=AF.Square, accum_out=sm[:, 0:1]),
              reads=[kcl], writes=["mjunk", ksm])
        P.act(lambda e, clat=clat, sm=sm: e.activation(out=mjunk[:, 384:640], in_=clat[:, 384:640], func=AF.Square, accum_out=sm[:, 1:2]),
              reads=[kcl], writes=["mjunk", ksm])
        P.act(lambda e, sm=sm: e.activation(out=sm[:, 0:1], in_=sm[:, 0:1], func=AF.Ln, scale=1.0 / QR, bias=C.epsc[:, 0:1]), reads=[ksm, "epsc"], writes=[ksm])
        P.act(lambda e, sm=sm: e.activation(out=sm[:, 1:2], in_=sm[:, 1:2], func=AF.Ln, scale=1.0 / KVR, bias=C.epsc[:, 0:1]), reads=[ksm, "epsc"], writes=[ksm])
        P.act(lambda e, sm=sm: e.activation(out=sm[:, 0:2], in_=sm[:, 0:2], func=AF.Exp, scale=-0.5), reads=[ksm], writes=[ksm])
        P.dve(lambda e, clat=clat, cn=cn, sm=sm: e.tensor_scalar(out=cn[:, 0:384], in0=clat[:, 0:384], scalar1=sm[:, 0:1], scalar2=None, op0=ALU.mult),
              reads=[kcl, ksm], writes=[kcn])
        P.dve(lambda e, clat=clat, cn=cn, sm=sm: e.tensor_scalar(out=cn[:, 384:640], in0=clat[:, 384:640], scalar1=sm[:, 1:2], scalar2=None, op0=ALU.mult),
              reads=[kcl, ksm], writes=[kcn])
        pt, ptk = psn(C, ALLB)
        ptb = pt[:].bitcast(BF16)
        for c in range(5):
            P.pe(lambda e, c=c, ptb=ptb, cn=cn: e.transpose(ptb[:, c * 128:(c + 1) * 128], cn[:, c * 128:(c + 1) * 128], identb[:]),
                 reads=[kcn, "identb"], writes=[ptk])
        P.dve(lambda e, t=t, ptb=ptb: e.tensor_tensor(out=cT[:, :, t * 128:(t + 1) * 128],
                                                      in0=ptb[:, 0:640].rearrange("p (c i) -> p c i", c=5),
                                                      in1=g5[:].unsqueeze(2).to_broadcast([128, 5, 128]), op=ALU.mult),
              reads=[ptk, "g5"], writes=[("cT", t)])
    C.scope_end()
    krT = C.sb("krT", [128, T], BF16)
    C.scope_begin()
    t1 = C.sb("t1", [128, 512], F32)
    t2 = C.sb("t2", [128, 512], F32)
    for tb in range(4):
        sl = slice(tb * 512, (tb + 1) * 512)
        pA, pAk = psn(C, ALLB)
        pB, pBk = psn(C, ALLB)
        hk = [("hT", 4 * tb + i) for i in range(4)]
        for k in range(8):
            P.pe(lambda e, k=k, pA=pA, sl=sl: e.matmul(pA[:], lhsT=wk2A[:, k, :], rhs=hT[:, k, sl], start=(k == 0), stop=(k == 7)),
                 reads=hk + ["wk2A"], writes=[pAk])
        for k in range(8):
            P.pe(lambda e, k=k, pB=pB, sl=sl: e.matmul(pB[:], lhsT=wk2B[:, k, :], rhs=hT[:, k, sl], start=(k == 0), stop=(k == 7)),
                 reads=hk + ["wk2B"], writes=[pBk])
        P.dve(lambda e, pA=pA, sl=sl: e.tensor_tensor(out=t1[:], in0=pA[:], in1=cosT[:, sl], op=ALU.mult), reads=[pAk, ("cs", tb)], writes=["t1"])
        P.dve(lambda e, pB=pB, sl=sl: e.tensor_tensor(out=t2[:], in0=pB[:], in1=sinT[:, sl], op=ALU.mult), reads=[pBk, ("cs", tb)], writes=["t2"])
        P.dve(lambda e, sl=sl: e.tensor_tensor(out=krT[:, sl], in0=t1[:], in1=t2[:], op=ALU.add), reads=["t1", "t2"], writes=[("krT", tb)])
    C.scope_end()

    P.dma("pool", w_out, wout_d.rearrange("(k p) f -> p k f", p=128), writes=["wbuf"])
    P.pool(lambda e: e.tensor_tensor(out=w_out, in0=w_out, in1=gate_bc[:].unsqueeze(1).to_broadcast([128, 8, D]), op=ALU.mult),
           reads=["wbuf", "gate_bc"], writes=["wbuf"])

    oT = hT
    qn = C.sb("qn", [128, T], BF16)
    qr = C.sb("qr", [128, T], BF16)
    kn = C.sb("kn", [128, T], BF16)
    V = C.sb("V", [128, NT, 128], BF16)
    onesb = C.sb("onesb", [128, 128], BF16)
    P.dve(lambda e: e.memset(onesb[:], 1.0), writes=["onesb"])
    pT = [C.sb("pT%d" % i, [128, 512], BF16) for i in range(3)]
    rec = [C.sb("rec%d" % i, [128, 512], F32) for i in range(2)]
    OB = [0, 1]
    SMB = [2, 3]
    SB_ = [4, 5]
    MB = [6, 7]
    PJB = [6, 7, 4, 5, 2, 3]
    npt = 0
    for h in range(8):
        for tb in range(4):
            sl = slice(tb * 512, (tb + 1) * 512)
            ck = [("cT", 4 * tb + i) for i in range(4)]
            p1, p1k = psn(C, PJB)
            for k in range(3):
                P.pe(lambda e, k=k, p1=p1, sl=sl, h=h: e.matmul(p1[:], lhsT=w_uq[:, k, h * 192:h * 192 + 128], rhs=cT[:, k, sl],
                                                              start=(k == 0), stop=(k == 2)), reads=ck + ["w_uq"], writes=[p1k])
            P.act(lambda e, p1=p1, sl=sl: e.copy(out=qn[:, sl], in_=p1[:]), reads=[p1k], writes=[("qn", tb)])
            p2, p2k = psn(C, PJB)
            for k in range(2):
                P.pe(lambda e, k=k, p2=p2, sl=sl, h=h: e.matmul(p2[:], lhsT=w_ukv[:, k, h * 256:h * 256 + 128], rhs=cT[:, 3 + k, sl],
                                                              start=(k == 0), stop=(k == 1)), reads=ck + ["w_ukv"], writes=[p2k])
            P.act(lambda e, p2=p2, sl=sl: e.copy(out=kn[:, sl], in_=p2[:]), reads=[p2k], writes=[("kn", tb)])
            pA, pAk = psn(C, PJB)
            for k in range(3):
                P.pe(lambda e, k=k, pA=pA, sl=sl, h=h: e.matmul(pA[:], lhsT=wq2[:, k, h, :], rhs=cT[:, k, sl],
                                                              start=(k == 0), stop=(k == 2)), reads=ck + ["wq2"], writes=[pAk])
            P.dve(lambda e, pA=pA, sl=sl: e.tensor_tensor(out=qr[:, sl], in0=pA[:], in1=cs2[:, sl], op=ALU.mult),
                  reads=[pAk, ("cs2", tb)], writes=[("qr", tb)])
            p3, p3k = psn(C, PJB)
            for i in range(4):
                t = 4 * tb + i
                for k in range(2):
                    P.pe(lambda e, k=k, p3=p3, i=i, t=t, h=h: e.matmul(p3[:, i * 128:(i + 1) * 128], lhsT=cT[:, 3 + k, t * 128:(t + 1) * 128],
                                                                     rhs=w_ukv[:, k, h * 256 + 128:h * 256 + 256], start=(k == 0), stop=(k == 1)),
                         reads=[("cT", t), "w_ukv"], writes=[p3k])
            P.act(lambda e, p3=p3, tb=tb: e.copy(out=V[:, 4 * tb:4 * tb + 4, :], in_=p3[:].rearrange("p (i d) -> p i d", i=4)),
                  reads=[p3k], writes=[("V", tb)])
        items = []
        for qb in range(4):
            for kt in range(4 * qb + 4):
                items.append((qb, kt))
        accs = {}

        def emit_scores(qb, kt):
            nonlocal npt
            q0 = max(kt, 4 * qb)
            n = (4 * qb + 4 - q0) * 128
            qsl = slice(q0 * 128, (4 * qb + 4) * 128)
            ps_, psk = psn(C, SB_)
            P.pe(lambda e: e.matmul(ps_[:, 0:n], lhsT=kn[:, kt * 128:(kt + 1) * 128], rhs=qn[:, qsl], start=True, stop=False),
                 reads=[("kn", kt // 4), ("qn", qb)], writes=[psk])
            P.pe(lambda e: e.matmul(ps_[:, 0:n], lhsT=krT[:, kt * 128:(kt + 1) * 128], rhs=qr[:, qsl], start=False, stop=True),
                 reads=[("krT", kt // 4), ("qr", qb)], writes=[psk])
            pb_i = npt % 3
            npt += 1
            ptile = pT[pb_i]
            pk_ = ("pT", pb_i)
            P.act(lambda e: e.activation(out=ptile[:, 0:n], in_=ps_[:, 0:n], func=AF.Exp, scale=SCALE), reads=[psk], writes=[pk_])
            if kt >= 4 * qb:
                P.pool(lambda e: e.tensor_tensor(out=ptile[:, 0:128], in0=ptile[:, 0:128], in1=trib[:], op=ALU.mult),
                       reads=[pk_, "trib"], writes=[pk_])
            return ptile, pk_

        def emit_pv(qb, kt, ptile, pk_):
            if kt == 0:
                accs[qb] = (psn(C, OB), psn(C, SMB))
            (po, pok), (psm, psmk) = accs[qb]
            nkt = 4 * qb + 4
            q0 = max(kt, 4 * qb)
            n = (4 * qb + 4 - q0) * 128
            off = (q0 - 4 * qb) * 128
            P.pe(lambda e: e.matmul(po[:, off:off + n], lhsT=V[:, kt, :], rhs=ptile[:, 0:n], start=(kt == 0), stop=(kt == nkt - 1)),
                 reads=[pk_, ("V", kt // 4)], writes=[pok])
            P.pe(lambda e: e.matmul(psm[:, off:off + n], lhsT=onesb[:], rhs=ptile[:, 0:n], start=(kt == 0), stop=(kt == nkt - 1)),
                 reads=[pk_, "onesb"], writes=[psmk])
            if kt == nkt - 1:
                rc = rec[qb % 2]
                rck = ("rec", qb % 2)
                P.dve(lambda e: e.reciprocal(out=rc[:], in_=psm[:]), reads=[psmk], writes=[rck])
                P.dve(lambda e: e.tensor_tensor(out=oT[:, h, qb * 512:(qb + 1) * 512], in0=po[:], in1=rc[:], op=ALU.mult),
                      reads=[pok, rck], writes=[("hT", 4 * qb + i) for i in range(4)])

        cur = emit_scores(*items[0])
        for ii, (qb, kt) in enumerate(items):
            nxt = emit_scores(*items[ii + 1]) if ii + 1 < len(items) else None
            emit_pv(qb, kt, *cur)
            cur = nxt

    for t in range(NT):
        for half in range(2):
            po, pok = psn(C, ALLB)
            for h in range(8):
                P.pe(lambda e, po=po, h=h, t=t, half=half: e.matmul(po[:], lhsT=oT[:, h, t * 128:(t + 1) * 128],
                                                                  rhs=w_out[:, h, half * 512:(half + 1) * 512], start=(h == 0), stop=(h == 7)),
                     reads=[("hT", t), "wbuf"], writes=[pok])
            P.dve(lambda e, po=po, t=t, half=half: e.tensor_tensor(out=X[:, t, half * 512:(half + 1) * 512],
                                                                  in0=X[:, t, half * 512:(half + 1) * 512], in1=po[:], op=ALU.add),
                  reads=[pok, ("X", t)], writes=[("X", t)])


def _invf64():
    f = (10000.0 ** (-np.arange(0, 64, 2, dtype=np.float32) / np.float32(64))).astype(np.float32)
    return np.concatenate([f, f, f, f]).astype(np.float32)


def _tri():
    k = np.arange(128)[:, None]
    q = np.arange(128)[None, :]
    return (k <= q).astype(np.float32)


HG = 2
NMASK = 19


def _gmasks():
    idx = np.arange(128)
    p = idx[:, None]
    f = idx[None, :]
    m = []
    m.append((p <= f).astype(np.float32))
    m.append((p > f).astype(np.float32))
    m.append((f < p).astype(np.float32))
    m.append(np.where(f <= p, 0.0, -30000.0).astype(np.float32))
    m.append(np.zeros((128, 128), np.float32))
    b = 1
    while b < 128:
        blk = idx // b
        mU = ((blk[:, None] // 2) == (blk[None, :] // 2)) & ((blk[:, None] % 2) == 0) & ((blk[None, :] % 2) == 1)
        m.append(-mU.astype(np.float32))
        m.append(-mU.T.astype(np.float32))
        b *= 2
    return np.stack(m).astype(np.float32)


def gdn_body(C, R, hT, layer):
    P = C.P
    I = C.I
    j_ = layer // 2
    ng_d = I["norm_mix_g"][layer]
    win_d = I["gdn_w_in"][j_]
    conv_d = I["gdn_conv_w"][j_]
    alog_d = I["gdn_a_log"][j_]
    dtb_d = I["gdn_dt_bias"][j_]
    gng_d = I["gdn_norm_g"][j_]
    wout_d = I["gdn_w_out"][j_]
    gm_d = I["gmasks"]
    mod_compute(C, R, layer, [0, 1, 2], ng_d)
    X, gate_bc, ident32, identb, ones32 = R["X"], R["gate_bc"], R["ident32"], R["identb"], R["ones32"]
    norm_to_hT(C, R, hT)
    ALLB = list(range(8))
    winv = win_d.rearrange("(k p) f -> p k f", p=128)

    tri32 = C.sb("tri32", [128, 2, 128], F32)
    P.dma("sp", tri32[:], gm_d[0:2].rearrange("m p f -> p m f"), writes=["tri32"])
    mk32 = C.sb("mk32", [128, 2, 128], F32)
    P.dma("sp", mk32[:], gm_d[2:4].rearrange("m p f -> p m f"), writes=["mk32"])
    lvm = C.sb("lvm", [128, 14, 128], BF16)
    P.dma("pool", lvm[:], gm_d[5:19].rearrange("m p f -> p m f"), writes=["lvm"])
    onesb = C.sb("onesb", [128, 128], BF16)
    P.dve(lambda e: e.memset(onesb[:], 1.0), writes=["onesb"])
    cw = C.sb("cw", [128, 24, 4], F32)
    cvv = conv_d.rearrange("(c p) k -> p c k", p=128)
    for c_ in range(24):
        P.dma("sp", cw[:, c_, :], cvv[:, c_, :], writes=["cw"])
    gnb = C.sb("gnb", [128, 128], F32)
    P.dma("sp", gnb[:], gng_d.unsqueeze(0).to_broadcast([128, 128]), writes=["gnb"])
    alb = C.sb("alb", [128, 8], F32)
    dtb = C.sb("dtb", [128, 8], F32)
    P.dma("sp", alb[:], alog_d.unsqueeze(0).to_broadcast([128, 8]), writes=["alb"])
    P.dma("sp", dtb[:], dtb_d.unsqueeze(0).to_broadcast([128, 8]), writes=["dtb"])

    col = {nm: C.sb("col_" + nm, [128, NT, 8], F32) for nm in ("g", "beta", "gc", "egc", "cb", "edec", "gl")}
    C.scope_begin()
    wab = C.sb("wab", [128, 8, 16], BF16)
    P.dma("pool", wab[:], winv[:, :, 4096:4112], writes=["wab"])
    abv = C.sb("abv", [128, NT, 16], F32)
    tA = C.sb("tA", [128, NT, 8], F32)
    tB = C.sb("tB", [128, NT, 8], F32)
    pab, pabk = psn(C, ALLB)
    for t in range(NT):
        for k in range(8):
            P.pe(lambda e, t=t, k=k: e.matmul(pab[:, t * 16:(t + 1) * 16], lhsT=hT[:, k, t * 128:(t + 1) * 128], rhs=wab[:, k, :],
                                              start=(k == 0), stop=(k == 7)), reads=[("hT", t), "wab"], writes=[pabk])
    P.act(lambda e: e.copy(out=abv[:].rearrange("p t c -> p (t c)"), in_=pab[:, 0:256]), reads=[pabk], writes=["abv"])
    P.dve(lambda e: e.tensor_tensor(out=tA[:], in0=abv[:, :, 0:8], in1=dtb[:].unsqueeze(1).to_broadcast([128, NT, 8]), op=ALU.add),
          reads=["abv", "dtb"], writes=["tA"])
    P.act(lambda e: e.activation(out=tB[:], in_=tA[:], func=AF.Abs), reads=["tA"], writes=["tB"])
    P.act(lambda e: e.activation(out=tB[:], in_=tB[:], func=AF.Exp, scale=-1.0), reads=["tB"], writes=["tB"])
    P.act(lambda e: e.activation(out=tB[:], in_=tB[:], func=AF.Ln, scale=1.0, bias=1.0), reads=["tB"], writes=["tB"])
    P.dve(lambda e: e.tensor_scalar(out=tA[:], in0=tA[:], scalar1=0.0, scalar2=None, op0=ALU.max), reads=["tA"], writes=["tA"])
    P.dve(lambda e: e.tensor_tensor(out=tA[:], in0=tA[:], in1=tB[:], op=ALU.add), reads=["tA", "tB"], writes=["tA"])
    P.act(lambda e: e.activation(out=alb[:], in_=alb[:], func=AF.Exp), reads=["alb"], writes=["alb"])
    P.dve(lambda e: e.scalar_tensor_tensor(out=col["g"][:], in0=tA[:], scalar=-1.0, in1=alb[:].unsqueeze(1).to_broadcast([128, NT, 8]),
                                           op0=ALU.mult, op1=ALU.mult), reads=["tA", "alb"], writes=["col_g"])
    P.act(lambda e: e.activation(out=col["beta"][:], in_=abv[:, :, 8:16], func=AF.Exp, scale=-1.0), reads=["abv"], writes=["col_beta"])
    P.act(lambda e: e.activation(out=col["beta"][:], in_=col["beta"][:], func=AF.Ln, scale=1.0, bias=1.0), reads=["col_beta"], writes=["col_beta"])
    P.act(lambda e: e.activation(out=col["beta"][:], in_=col["beta"][:], func=AF.Exp, scale=-1.0), reads=["col_beta"], writes=["col_beta"])
    pgc, pgck = psn(C, ALLB)
    prc, prck = psn(C, ALLB)
    ptt, pttk = psn(C, ALLB)
    for n in range(NT):
        P.pe(lambda e, n=n: e.matmul(pgc[:, n * 8:(n + 1) * 8], lhsT=tri32[:, 0, :], rhs=col["g"][:, n, :], start=True, stop=True),
             reads=["tri32", "col_g"], writes=[pgck])
        P.pe(lambda e, n=n: e.matmul(prc[:, n * 8:(n + 1) * 8], lhsT=tri32[:, 1, :], rhs=col["g"][:, n, :], start=True, stop=True),
             reads=["tri32", "col_g"], writes=[prck])
        P.pe(lambda e, n=n: e.matmul(ptt[:, n * 8:(n + 1) * 8], lhsT=ones32[:], rhs=col["g"][:, n, :], start=True, stop=True),
             reads=["ones32", "col_g"], writes=[pttk])
    fl = lambda t_: t_[:].rearrange("p t c -> p (t c)")
    P.act(lambda e: e.copy(out=fl(col["gc"]), in_=pgc[:, 0:128]), reads=[pgck], writes=["col_gc"])
    P.act(lambda e: e.activation(out=fl(col["egc"]), in_=pgc[:, 0:128], func=AF.Exp), reads=[pgck], writes=["col_egc"])
    P.act(lambda e: e.activation(out=fl(col["edec"]), in_=prc[:, 0:128], func=AF.Exp), reads=[prck], writes=["col_edec"])
    P.act(lambda e: e.activation(out=fl(col["gl"]), in_=ptt[:, 0:128], func=AF.Exp), reads=[pttk], writes=["col_gl"])
    P.dve(lambda e: e.scalar_tensor_tensor(out=col["cb"][:], in0=col["egc"][:], scalar=-1.0, in1=col["beta"][:], op0=ALU.mult, op1=ALU.mult),
          reads=["col_egc", "col_beta"], writes=["col_cb"])
    C.scope_end()

    W = HG * 128
    qT = C.sb("qT", [128, HG, T], BF16)
    kT = C.sb("kT", [128, HG, T], BF16)
    vb = C.sb("vb", [128, NT, HG, 128], BF16)
    sgA = C.sb("sgA", [128, NT, W], BF16)

    def bc_h(ap2):
        return ap2.unsqueeze(1).to_broadcast([128, HG, 128])

    def v3(ap):
        return ap.rearrange("p (h i) -> p h i", h=HG)

    for gI in range(8 // HG):
        h0 = gI * HG
        C.scope_begin()
        wgt = C.sb("wgt", [128, 8, W], BF16)
        P.dma("pool", wgt[:], winv[:, :, 3072 + h0 * 128:3072 + (h0 + HG) * 128], writes=["wgt"])
        raw = [C.sb("raw%d" % i, [128, T + 4], BF16) for i in range(2)]
        dcw = [C.sb("dcw%d" % i, [128, 4, 128], BF16) for i in range(2)]
        y32 = C.sb("y32", [128, T], F32)
        sqb = [C.sb("sqb%d" % i, [128, 512], BF16) for i in range(2)]
        rn = [C.sb("rn%d" % i, [128, 512], F32) for i in range(2)]
        vTb = [C.sb("vTb%d" % i, [128, 512], BF16) for i in range(2)]
        wsl = [C.sb("wsl%d" % i, [128, 8, 128], BF16) for i in range(2)]
        for i in range(2):
            P.dve(lambda e, i=i: e.memset(raw[i][:, 0:3], 0.0), writes=[("rawpad", i)])
        for t in range(NT):
            pg_, pgk_ = psn(C, ALLB)
            for k in range(8):
                P.pe(lambda e, pg_=pg_, k=k, t=t: e.matmul(pg_[:, 0:W], lhsT=hT[:, k, t * 128:(t + 1) * 128], rhs=wgt[:, k, :], start=(k == 0), stop=(k == 7)),
                     reads=[("hT", t), "wgt"], writes=[pgk_])
            P.act(lambda e, pg_=pg_, t=t: e.activation(out=sgA[:, t, :], in_=pg_[:, 0:W], func=AF.Silu), reads=[pgk_], writes=[("sgA", t)])
        nw = 0
        nblk = 0
        for hl in range(HG):
            h = h0 + hl
            for typ in (2, 0, 1):
                cidx = typ * 8 + h
                ib = nw % 2
                wb = wsl[ib]
                wk = ("wsl", ib)
                rw = raw[ib]
                dc = dcw[ib]
                nw += 1
                P.dma("pool", wb[:], winv[:, :, typ * 1024 + h * 128:typ * 1024 + (h + 1) * 128], writes=[wk])
                P.dve(lambda e, dc=dc, cidx=cidx: e.tensor_tensor(out=dc[:], in0=identb[:].unsqueeze(1).to_broadcast([128, 4, 128]),
                                                                 in1=cw[:, cidx, :].unsqueeze(2).to_broadcast([128, 4, 128]), op=ALU.mult),
                      reads=["identb", "cw"], writes=[("dcw", ib)])
                def proj_blk(tb, wb=wb, wk=wk, rw=rw, ib=ib):
                    sl = slice(tb * 512, (tb + 1) * 512)
                    pp, ppk = psn(C, ALLB)
                    for k in range(8):
                        P.pe(lambda e, k=k: e.matmul(pp[:], lhsT=wb[:, k, :], rhs=hT[:, k, sl], start=(k == 0), stop=(k == 7)),
                             reads=[wk] + [("hT", 4 * tb + i) for i in range(4)], writes=[ppk])
                    P.act(lambda e: e.copy(out=rw[:, 3 + tb * 512:3 + (tb + 1) * 512], in_=pp[:]), reads=[ppk], writes=[("raw", ib, tb)])

                def conv_blk(tb, rw=rw, dc=dc, ib=ib, typ=typ, hl=hl, h=h):
                    nonlocal nblk
                    sl = slice(tb * 512, (tb + 1) * 512)
                    pc, pck = psn(C, ALLB)
                    rkeys = [("raw", ib, tb), ("rawpad", ib)] + ([("raw", ib, tb - 1)] if tb > 0 else [])
                    for j in range(4):
                        P.pe(lambda e, j=j: e.matmul(pc[:], lhsT=dc[:, j, :], rhs=rw[:, tb * 512 + j:tb * 512 + j + 512], start=(j == 0), stop=(j == 3)),
                             reads=rkeys + [("dcw", ib)], writes=[pck])
                    if typ == 2:
                        vt = vTb[nblk % 2]
                        vk = ("vTb", nblk % 2)
                        nblk += 1
                        P.act(lambda e: e.activation(out=vt[:], in_=pc[:], func=AF.Silu), reads=[pck], writes=[vk])
                        pt_, ptk_ = psn(C, ALLB)
                        ptb = pt_[:].bitcast(BF16)
                        for i in range(4):
                            P.pe(lambda e, i=i: e.transpose(ptb[:, i * 128:(i + 1) * 128], vt[:, i * 128:(i + 1) * 128], identb[:]),
                                 reads=[vk, "identb"], writes=[ptk_])
                        P.dve(lambda e: e.tensor_tensor(
                            out=vb[:, 4 * tb:4 * tb + 4, hl, :], in0=ptb[:, 0:512].rearrange("p (t d) -> p t d", t=4),
                            in1=col["beta"][:, 4 * tb:4 * tb + 4, h:h + 1].to_broadcast([128, 4, 128]), op=ALU.mult),
                            reads=[ptk_, "col_beta"], writes=[("vb", tb)])
                    else:
                        P.act(lambda e: e.activation(out=y32[:, sl], in_=pc[:], func=AF.Silu), reads=[pck], writes=[("y32", tb)])

                for tb in range(4):
                    proj_blk(tb)
                    if tb >= 1:
                        conv_blk(tb - 1)
                conv_blk(3)
                if typ != 2:
                    dst = qT if typ == 0 else kT
                    dkey = "qT" if typ == 0 else "kT"
                    sc = float(128 ** -0.5) if typ == 0 else 1.0
                    for tb in range(4):
                        sl = slice(tb * 512, (tb + 1) * 512)
                        sb_ = sqb[tb % 2]
                        sk_ = ("sqb", tb % 2)
                        rb_ = rn[tb % 2]
                        rk_ = ("rn", tb % 2)
                        P.pool(lambda e, sb_=sb_, sl=sl: e.tensor_tensor(out=sb_[:], in0=y32[:, sl], in1=y32[:, sl], op=ALU.mult), reads=[("y32", tb)], writes=[sk_])
                        pp, ppk = psn(C, ALLB)
                        P.pe(lambda e, pp=pp, sb_=sb_: e.matmul(pp[:], lhsT=onesb[:], rhs=sb_[:], start=True, stop=True), reads=["onesb", sk_], writes=[ppk])
                        P.act(lambda e, pp=pp, rb_=rb_: e.activation(out=rb_[:], in_=pp[:], func=AF.Ln, scale=1.0, bias=C.epsc[:, 0:1]), reads=[ppk, "epsc"], writes=[rk_])
                        P.act(lambda e, rb_=rb_: e.activation(out=rb_[:], in_=rb_[:], func=AF.Exp, scale=-0.5), reads=[rk_], writes=[rk_])
                        P.dve(lambda e, dst=dst, hl=hl, sl=sl, sc=sc, rb_=rb_: e.scalar_tensor_tensor(out=dst[:, hl, sl], in0=y32[:, sl], scalar=sc, in1=rb_[:],
                                                                                                op0=ALU.mult, op1=ALU.mult),
                              reads=[("y32", tb), rk_], writes=[(dkey, tb)])
        C.scope_end()

        C.scope_begin()
        wog = C.sb("wog", [128, HG, D], BF16)
        P.dma("pool", wog[:], wout_d.rearrange("(c p) n -> p c n", p=128)[:, h0:h0 + HG, :], writes=["wog"])
        P.pool(lambda e: e.tensor_tensor(out=wog[:], in0=wog[:], in1=gate_bc[:].unsqueeze(1).to_broadcast([128, HG, D]), op=ALU.mult),
               reads=["wog", "gate_bc"], writes=["wog"])
        S32 = C.sb("S32", [128, W], F32)
        Sbf = C.sb("Sbf", [128, W], BF16)
        P.dve(lambda e: e.memset(S32[:], 0.0), writes=["S32"])
        P.dve(lambda e: e.memset(Sbf[:], 0.0), writes=["Sbf"])
        f32t = lambda nm: C.sb(nm, [128, W], F32)
        bft = lambda nm: C.sb(nm, [128, W], BF16)
        NPRE = 1
        NIF = NPRE + 1
        TS = []
        for i in range(NPRE):
            d_ = {}
            for x in ("dgc", "d2", "dec", "egr", "bsl", "tL", "tU"):
                d_[x] = f32t(x)
            for x in ("L", "U", "attn", "Mm", "Tt0", "Tt1", "Tm0", "Tm1"):
                d_[x] = bft(x)
            TS.append(d_)
        TtF = [bft("TtF%d" % i) for i in range(NIF)]
        attnT = [bft("attnT%d" % i) for i in range(NIF)]
        qdT = [bft("qdT%d" % i) for i in range(NIF)]
        Rr, vnew, vdec, ktok, og, ogT = [bft(x) for x in ("Rr", "vnew", "vdec", "ktok", "og", "ogT")]
        tR, o32, osq = [f32t(x) for x in ("tR", "o32", "osq")]
        oss = C.sb("oss", [128, HG], F32)
        identb_h = identb[:].unsqueeze(1).to_broadcast([128, HG, 128])
        PB_PREP = [0, 1, 2, 3, 4]
        PB_REC = [5, 6]
        PB_OUT = [7]
        obank = {}

        def colb(nm, n):
            return col[nm][:, n, h0:h0 + HG].unsqueeze(2).to_broadcast([128, HG, 128])

        def tr_heads(S, dst_ap, src, skey, dkey, pool):
            pt_, ptk_ = psn(C, pool)
            ptb = pt_[:].bitcast(BF16)
            for hl in range(HG):
                hs = slice(hl * 128, (hl + 1) * 128)
                S.pe(lambda e, ptb=ptb, hs=hs: e.transpose(ptb[:, hs], src[:, hs], identb[:]), reads=[skey, "identb"], writes=[ptk_])
            S.act(lambda e, ptb=ptb: e.copy(out=dst_ap, in_=ptb[:, 0:W]), reads=[ptk_], writes=[dkey])

        def prep(n):
            S = Stream()
            csl = slice(n * 128, (n + 1) * 128)
            pb = n % NIF
            ts = n % NPRE
            D_ = TS[ts]
            dgc, d2, dec, egr, bsl, tL, tU = [D_[x] for x in ("dgc", "d2", "dec", "egr", "bsl", "tL", "tU")]
            L_, U_, attn, Mm = [D_[x] for x in ("L", "U", "attn", "Mm")]
            Tt = [D_["Tt0"], D_["Tt1"]]
            Tm = [D_["Tm0"], D_["Tm1"]]
            K_ = lambda nm: (nm, "ts", ts)
            S.dve(lambda e: e.tensor_tensor(out=v3(dgc[:]), in0=bc_h(ident32[:]), in1=colb("gc", n), op=ALU.mult), reads=["ident32", "col_gc"], writes=[K_("dgc")])
            pgr, pgrk = psn(C, PB_PREP)
            S.pe(lambda e: e.matmul(pgr[:, 0:W], lhsT=ones32[:], rhs=dgc[:], start=True, stop=True), reads=["ones32", K_("dgc")], writes=[pgrk])
            S.dve(lambda e: e.scalar_tensor_tensor(out=v3(d2[:]), in0=v3(pgr[:, 0:W]), scalar=-1.0, in1=colb("gc", n), op0=ALU.mult, op1=ALU.add),
                  reads=[pgrk, "col_gc"], writes=[K_("d2")])
            S.dve(lambda e: e.tensor_tensor(out=v3(d2[:]), in0=v3(d2[:]), in1=bc_h(mk32[:, 1, :]), op=ALU.add), reads=[K_("d2"), "mk32"], writes=[K_("d2")])
            S.act(lambda e: e.activation(out=dec[:], in_=d2[:], func=AF.Exp), reads=[K_("d2")], writes=[K_("dec")])
            S.act(lambda e: e.activation(out=egr[:], in_=pgr[:, 0:W], func=AF.Exp), reads=[pgrk], writes=[K_("egr")])
            S.dve(lambda e: e.tensor_tensor(out=v3(bsl[:]), in0=bc_h(mk32[:, 0, :]), in1=colb("beta", n), op=ALU.mult), reads=["mk32", "col_beta"], writes=[K_("bsl")])
            pkk, pkkk = psn(C, PB_PREP)
            pqk, pqkk = psn(C, PB_PREP)
            for hl in range(HG):
                S.pe(lambda e, hl=hl: e.matmul(pkk[:, hl * 128:(hl + 1) * 128], lhsT=kT[:, hl, csl], rhs=kT[:, hl, csl], start=True, stop=True),
                     reads=[("kT", n // 4)], writes=[pkkk])
            for hl in range(HG):
                S.pe(lambda e, hl=hl: e.matmul(pqk[:, hl * 128:(hl + 1) * 128], lhsT=qT[:, hl, csl], rhs=kT[:, hl, csl], start=True, stop=True),
                     reads=[("kT", n // 4), ("qT", n // 4)], writes=[pqkk])
            S.dve(lambda e: e.tensor_tensor(out=tL[:], in0=pkk[:, 0:W], in1=dec[:], op=ALU.mult), reads=[pkkk, K_("dec")], writes=[K_("tL")])
            S.dve(lambda e: e.tensor_tensor(out=L_[:], in0=tL[:], in1=bsl[:], op=ALU.mult), reads=[K_("tL"), K_("bsl")], writes=[K_("L")])
            S.dve(lambda e: e.tensor_tensor(out=attn[:], in0=pqk[:, 0:W], in1=dec[:], op=ALU.mult), reads=[pqkk, K_("dec")], writes=[K_("attn")])
            S.dve(lambda e: e.tensor_tensor(out=v3(qdT[pb][:]), in0=qT[:, :, csl], in1=v3(egr[:]), op=ALU.mult), reads=[("qT", n // 4), K_("egr")], writes=[("qdT", pb)])
            tr_heads(S, U_[:], L_, K_("L"), K_("U"), PB_PREP)
            tr_heads(S, attnT[pb][:], attn, K_("attn"), ("attnT", pb), PB_PREP)
            S.dve(lambda e: e.tensor_tensor(out=v3(tU[:]), in0=v3(U_[:]), in1=bc_h(lvm[:, 0, :]), op=ALU.mult), reads=[K_("U"), "lvm"], writes=[K_("tU")])
            S.dve(lambda e: e.tensor_tensor(out=v3(Tt[0][:]), in0=v3(tU[:]), in1=identb_h, op=ALU.add), reads=[K_("tU"), "identb"], writes=[("Tt", ts, 0)])
            tr_heads(S, Tm[0][:], Tt[0], ("Tt", ts, 0), ("Tm", ts, 0), PB_PREP)
            cur = 0
            for lv in range(1, 7):
                nxt = 1 - cur
                last = (lv == 6)
                tt_c, tm_c = Tt[cur], Tm[cur]
                pm, pmk = psn(C, PB_PREP)
                for hl in range(HG):
                    hs = slice(hl * 128, (hl + 1) * 128)
                    S.pe(lambda e, pm=pm, hs=hs, tt_c=tt_c: e.matmul(pm[:, hs], lhsT=L_[:, hs], rhs=tt_c[:, hs], start=True, stop=True),
                         reads=[K_("L"), ("Tt", ts, cur)], writes=[pmk])
                S.dve(lambda e, pm=pm, lv=lv: e.tensor_tensor(out=v3(Mm[:]), in0=v3(pm[:, 0:W]), in1=bc_h(lvm[:, 2 * lv, :]), op=ALU.mult),
                      reads=[pmk, "lvm"], writes=[K_("Mm")])
                pt2, pt2k = psn(C, PB_PREP)
                S.pe(lambda e, pt2=pt2, tt_c=tt_c: e.matmul(pt2[:, 0:W], lhsT=identb[:], rhs=tt_c[:], start=True, stop=False),
                     reads=["identb", ("Tt", ts, cur)], writes=[pt2k])
                for hl in range(HG):
                    hs = slice(hl * 128, (hl + 1) * 128)
                    S.pe(lambda e, pt2=pt2, hs=hs, tm_c=tm_c, hl=hl: e.matmul(pt2[:, hs], lhsT=tm_c[:, hs], rhs=Mm[:, hs], start=False, stop=(hl == HG - 1),
                                                                            skip_group_check=True),
                         reads=[("Tm", ts, cur), K_("Mm")], writes=[pt2k])
                if last:
                    S.act(lambda e, pt2=pt2: e.copy(out=TtF[pb][:], in_=pt2[:, 0:W]), reads=[pt2k], writes=[("TtF", pb)])
                else:
                    S.act(lambda e, pt2=pt2, nxt=nxt: e.copy(out=Tt[nxt][:], in_=pt2[:, 0:W]), reads=[pt2k], writes=[("Tt", ts, nxt)])
                    tr_heads(S, Tm[nxt][:], Tt[nxt], ("Tt", ts, nxt), ("Tm", ts, nxt), PB_PREP)
                cur = nxt
            return S

        def rec(n):
            S = Stream()
            csl = slice(n * 128, (n + 1) * 128)
            pb = n % NIF
            ttf = TtF[pb]
            pks, pksk = psn(C, PB_REC)
            for hl in range(HG):
                hs = slice(hl * 128, (hl + 1) * 128)
                S.pe(lambda e, hs=hs, hl=hl: e.matmul(pks[:, hs], lhsT=kT[:, hl, csl], rhs=Sbf[:, hs], start=True, stop=True),
                     reads=[("kT", n // 4), "Sbf"], writes=[pksk])
            for hl in range(HG):
                hs = slice(hl * 128, (hl + 1) * 128)
                S.dve(lambda e, hs=hs, hl=hl: e.scalar_tensor_tensor(out=Rr[:, hs], in0=pks[:, hs], scalar=col["cb"][:, n, h0 + hl:h0 + hl + 1], in1=vb[:, n, hl, :],
                                                                     op0=ALU.mult, op1=ALU.add), reads=[pksk, "col_cb", ("vb", n // 4)], writes=["Rr"])
            pvn, pvnk = psn(C, PB_REC)
            for hl in range(HG):
                hs = slice(hl * 128, (hl + 1) * 128)
                S.pe(lambda e, hs=hs: e.matmul(pvn[:, hs], lhsT=ttf[:, hs], rhs=Rr[:, hs], start=True, stop=True),
                     reads=[("TtF", pb), "Rr"], writes=[pvnk])
            S.act(lambda e: e.copy(out=vnew[:], in_=pvn[:, 0:W]), reads=[pvnk], writes=["vnew"])
            S.dve(lambda e: e.tensor_tensor(out=v3(vdec[:]), in0=v3(pvn[:, 0:W]), in1=colb("edec", n), op=ALU.mult), reads=[pvnk, "col_edec"], writes=["vdec"])
            tr_heads_k(S, n)
            pss, pssk = psn(C, PB_REC)
            for hl in range(HG):
                hs = slice(hl * 128, (hl + 1) * 128)
                S.pe(lambda e, hs=hs: e.matmul(pss[:, hs], lhsT=ktok[:, hs], rhs=vdec[:, hs], start=True, stop=True),
                     reads=["ktok", "vdec"], writes=[pssk])
            po_, pok_ = psn(C, PB_REC)
            for hl in range(HG):
                hs = slice(hl * 128, (hl + 1) * 128)
                S.pe(lambda e, hs=hs: e.matmul(po_[:, hs], lhsT=qdT[pb][:, hs], rhs=Sbf[:, hs], start=True, stop=False),
                     reads=[("qdT", pb), "Sbf"], writes=[pok_])
                S.pe(lambda e, hs=hs: e.matmul(po_[:, hs], lhsT=attnT[pb][:, hs], rhs=vnew[:, hs], start=False, stop=True),
                     reads=[("attnT", pb), "vnew"], writes=[pok_])
            obank[n] = (po_, pok_)
            for hl in range(HG):
                hs = slice(hl * 128, (hl + 1) * 128)
                S.dve(lambda e, hs=hs, hl=hl: e.scalar_tensor_tensor(out=S32[:, hs], in0=S32[:, hs], scalar=col["gl"][:, n, h0 + hl:h0 + hl + 1], in1=pss[:, hs],
                                                                     op0=ALU.mult, op1=ALU.add), reads=["S32", "col_gl", pssk], writes=["S32"])
            S.act(lambda e: e.copy(out=Sbf[:], in_=S32[:]), reads=["S32"], writes=["Sbf"])
            return S

        def tr_heads_k(S, n):
            csl = slice(n * 128, (n + 1) * 128)
            pkt, pktk = psn(C, PB_REC)
            pktb = pkt[:].bitcast(BF16)
            for hl in range(HG):
                S.pe(lambda e, hl=hl: e.transpose(pktb[:, hl * 128:(hl + 1) * 128], kT[:, hl, csl], identb[:]),
                     reads=[("kT", n // 4), "identb"], writes=[pktk])
            S.act(lambda e: e.copy(out=ktok[:], in_=pktb[:, 0:W]), reads=[pktk], writes=["ktok"])

        def outp(n):
            S = Stream()
            po_, pok_ = obank.pop(n)
            S.act(lambda e: e.copy(out=o32[:], in_=po_[:, 0:W]), reads=[pok_], writes=["o32"])
            S.act(lambda e: e.activation(out=osq[:], in_=o32[:], func=AF.Square), reads=["o32"], writes=["osq"])
            S.dve(lambda e: e.tensor_reduce(out=oss[:], in_=v3(osq[:]), axis=AX.X, op=ALU.add), reads=["osq"], writes=["oss"])
            S.act(lambda e: e.activation(out=oss[:], in_=oss[:], func=AF.Ln, scale=1.0 / 128, bias=C.epsc[:, 0:1]), reads=["oss", "epsc"], writes=["oss"])
            S.act(lambda e: e.activation(out=oss[:], in_=oss[:], func=AF.Exp, scale=-0.5), reads=["oss"], writes=["oss"])
            S.dve(lambda e: e.tensor_tensor(out=v3(o32[:]), in0=v3(o32[:]), in1=oss[:].unsqueeze(2).to_broadcast([128, HG, 128]), op=ALU.mult),
                  reads=["o32", "oss"], writes=["o32"])
            S.dve(lambda e: e.tensor_tensor(out=v3(o32[:]), in0=v3(o32[:]), in1=bc_h(gnb[:]), op=ALU.mult), reads=["o32", "gnb"], writes=["o32"])
            S.dve(lambda e: e.tensor_tensor(out=og[:], in0=o32[:], in1=sgA[:, n, :], op=ALU.mult), reads=["o32", ("sgA", n)], writes=["og"])
            tr_heads(S, ogT[:], og, "og", "ogT", PB_OUT)
            for half in range(2):
                py, pyk = psn(C, PB_OUT)
                for hl in range(HG):
                    hs = slice(hl * 128, (hl + 1) * 128)
                    S.pe(lambda e, py=py, hs=hs, hl=hl, half=half: e.matmul(py[:], lhsT=ogT[:, hs], rhs=wog[:, hl, half * 512:(half + 1) * 512],
                                                                          start=(hl == 0), stop=(hl == HG - 1)), reads=["ogT", "wog"], writes=[pyk])
                S.dve(lambda e, py=py, half=half: e.tensor_tensor(out=X[:, n, half * 512:(half + 1) * 512], in0=X[:, n, half * 512:(half + 1) * 512],
                                                                 in1=py[:], op=ALU.add), reads=[pyk, ("X", n)], writes=[("X", n)])
            return S

        pend = {}

        def prep_slices(m):
            ops = prep(m).ops
            k = (len(ops) + NPRE - 1) // NPRE
            out_ = []
            for i in range(NPRE):
                st = Stream()
                st.ops = ops[i * k:(i + 1) * k]
                out_.append(st)
            return out_

        for r in range(-NPRE, NT + 1):
            streams = []
            if 0 <= r < NT:
                streams.append(rec(r))
            if 1 <= r <= NT:
                streams.append(outp(r - 1))
            for m in range(r + 1, r + NPRE + 1):
                if 0 <= m < NT:
                    if m not in pend:
                        pend[m] = prep_slices(m)
                    streams.append(pend[m][r - m + NPRE])
            merge_streams(P, streams)
        C.scope_end()


def build_fused(parts=None):
    if parts is None:
        parts = []
        for layer in range(4):
            parts.append(("gdn" if layer % 2 == 0 else "mla", layer))
            parts.append(("ffn", layer))
    C = Ctx()
    hT = C.sb("hT", [128, 8, T], BF16)
    R = setup(C)
    for kind, layer in parts:
        C.scope_begin()
        if kind == "gdn":
            gdn_body(C, R, hT, layer)
        elif kind == "mla":
            mla_body(C, R, hT, layer)
        else:
            ffn_body(C, R, hT, layer, layer == 3)
        C.scope_end()
    store_x(C, R["X"])
    C.P.finish()
    C.P.emit()
    return C.nc


def make_maps(inp):
    consts = {"ident": _ident(), "invf": _invf64(), "tri": _tri(), "gmasks": _gmasks()}
    shared = {}
    for nm in INPUT_SHAPES:
        if nm in ("x", "c", "positions") or nm in consts:
            continue
        shared[nm] = np.ascontiguousarray(np.asarray(inp[nm]), dtype=np.float32)
    maps = []
    for b in range(8):
        m = dict(shared)
        m.update(consts)
        m["x"] = np.ascontiguousarray(inp["x"][b], dtype=np.float32)
        m["c"] = np.ascontiguousarray(inp["c"][b], dtype=np.float32)
        m["positions"] = np.ascontiguousarray(inp["positions"][b]).astype(np.int32)
        maps.append(m)
    return maps


_NC = {}


def kernel(**inputs):
    inp = {k: np.asarray(v) for k, v in inputs.items()}
    if "nc" not in _NC:
        _NC["nc"] = build_fused()
    r = run_bass_kernel_spmd(_NC["nc"], make_maps(inp), core_ids=list(range(8)))
    return np.ascontiguousarray(np.stack([r.results[b]["out"] for b in range(8)], axis=0), dtype=np.float32)
```

```python
from contextlib import ExitStack
import numpy as np
import concourse.bass as bass
import concourse.mybir as mybir
from concourse.bass_utils import run_bass_kernel_spmd

F32 = mybir.dt.float32
BF16 = mybir.dt.bfloat16
I32 = mybir.dt.int32
AF = mybir.ActivationFunctionType
ALU = mybir.AluOpType
AX = mybir.AxisListType

ENGS = ["pe", "act", "dve", "pool", "sp"]
NDMASEM = 8

D = 1024
T = 2048
NT = 16
DFF = 2816
NFC = 22
EPS = 1e-6


class Op:
    __slots__ = ("eng", "fn", "waits", "dwaits", "idx", "is_dma", "dsem", "dval", "tag", "calls")


class _Rec:
    def __init__(self):
        self.calls = []

    def __getattr__(self, name):
        def f(*a, **k):
            self.calls.append((name, a, k))
            return self
        return f


class Prog:
    def __init__(self, nc):
        self.nc = nc
        self.ops = {e: [] for e in ENGS}
        self.known = {e: {f: 0 for f in ENGS} for e in ENGS}
        self.kdma = {e: {} for e in ENGS}
        self.snaps = {e: [] for e in ENGS}
        self.last_w = {}
        self.readers = {}
        self.ndma = {e: 0 for e in ENGS}
        self.out_tokens = []
        self.milestones = {e: set() for e in ENGS}

    def _need(self, eng, tok, waits, dwaits):
        if tok is None:
            return
        if tok[0] == "e":
            _, f, idx = tok
            if f == eng and eng == "pe":
                return
            if self.known[eng][f] >= idx + 1:
                return
            waits.append((f, idx))
            self.milestones[f].add(idx)
            kn, kd = self.snaps[f][idx]
            for g in ENGS:
                if kn[g] > self.known[eng][g]:
                    self.known[eng][g] = kn[g]
            for k, v in kd.items():
                if self.kdma[eng].get(k, 0) < v:
                    self.kdma[eng][k] = v
            if self.known[eng][f] < idx + 1:
                self.known[eng][f] = idx + 1
        else:
            _, q, semi, val = tok
            key = (q, semi)
            if self.kdma[eng].get(key, 0) >= val:
                return
            dwaits.append((q, semi, val))
            self.kdma[eng][key] = val

    def op(self, eng, fn, reads=(), writes=(), dma=False, tag=None):
        rec = _Rec()
        fn(rec)
        return self.submit(eng, rec.calls, reads, writes, dma=dma, tag=tag)

    def submit(self, eng, calls, reads=(), writes=(), dma=False, tag=None):
        o = Op()
        o.eng = eng
        o.fn = True
        o.tag = tag
        o.is_dma = dma
        o.calls = calls
        assert len(o.calls) == 1
        waits, dwaits = [], []
        cand = []
        for k in reads:
            tok = self.last_w.get(k)
            if tok is not None:
                cand.append(tok)
            if isinstance(k, tuple) and k[0] == "ps":
                for rt in self.readers.get(k, ()):
                    if not (rt[0] == "e" and rt[1] == eng):
                        cand.append(rt)
        for k in writes:
            tok = self.last_w.get(k)
            if tok is not None:
                if not (tok[0] == "e" and tok[1] == eng):
                    cand.append(tok)
            for rt in self.readers.get(k, ()):
                if not (rt[0] == "e" and rt[1] == eng):
                    cand.append(rt)
        best = {}
        for tok in cand:
            kk = (tok[0], tok[1]) if tok[0] == "e" else (tok[0], tok[1], tok[2])
            if kk not in best or tok[-1] > best[kk][-1]:
                best[kk] = tok
        for tok in best.values():
            self._need(eng, tok, waits, dwaits)
        idx = len(self.ops[eng])
        o.idx = idx
        if dma:
            n = self.ndma[eng]
            self.ndma[eng] = n + 1
            semi = n % NDMASEM
            val = 16 * (n // NDMASEM + 1)
            if val > 16:
                self._need(eng, ("d", eng, semi, val - 16), waits, dwaits)
            o.dsem, o.dval = semi, val
            tok = ("d", eng, semi, val)
        else:
            tok = ("e", eng, idx)
        o.waits, o.dwaits = waits, dwaits
        self.ops[eng].append(o)
        self.snaps[eng].append((dict(self.known[eng]), dict(self.kdma[eng])))
        for k in reads:
            lst = self.readers.setdefault(k, [])
            if tok[0] == "e":
                lst[:] = [r for r in lst if not (r[0] == "e" and r[1] == tok[1])]
            lst.append(tok)
        for k in writes:
            self.last_w[k] = tok
            self.readers[k] = []
        return tok

    def pe(self, fn, reads=(), writes=()):
        return self.op("pe", fn, reads, writes)

    def act(self, fn, reads=(), writes=()):
        return self.op("act", fn, reads, writes)

    def dve(self, fn, reads=(), writes=()):
        return self.op("dve", fn, reads, writes)

    def pool(self, fn, reads=(), writes=()):
        return self.op("pool", fn, reads, writes)

    def dma(self, q, out, in_, reads=(), writes=(), is_out=False, **kw):
        tok = self.op(q, lambda e: e.dma_start(out=out, in_=in_, **kw), reads, writes, dma=True)
        if is_out:
            self.out_tokens.append(tok)
        return tok

    def barrier(self):
        toks = []
        for f in ENGS:
            for o in reversed(self.ops[f]):
                if (not o.is_dma) and o.fn is not None:
                    toks.append(("e", f, o.idx))
                    break
            n = self.ndma[f]
            for i in range(max(0, n - NDMASEM), n):
                toks.append(("d", f, i % NDMASEM, 16 * (i // NDMASEM + 1)))
        for e in ENGS:
            waits, dwaits = [], []
            for tok in toks:
                if tok[0] == "e" and tok[1] == e:
                    if e == "pe":
                        continue
                self._need(e, tok, waits, dwaits)
            if not waits and not dwaits:
                continue
            o = Op()
            o.eng = e; o.fn = None; o.waits = waits; o.dwaits = dwaits
            o.idx = len(self.ops[e]); o.is_dma = False; o.tag = "barrier"
            self.ops[e].append(o)
            self.snaps[e].append((dict(self.known[e]), dict(self.kdma[e])))

    def finish(self):
        waits, dwaits = [], []
        for tok in self.out_tokens:
            self._need("sp", tok, waits, dwaits)
        o = Op()
        o.eng = "sp"; o.fn = None; o.waits = waits; o.dwaits = dwaits
        o.idx = len(self.ops["sp"]); o.is_dma = False; o.tag = "finish"
        self.ops["sp"].append(o)
        self.snaps["sp"].append((dict(self.known["sp"]), dict(self.kdma["sp"])))

    def emit(self):
        nc = self.nc
        rank = {}
        for e in ENGS:
            ms = sorted(self.milestones[e])
            rank[e] = {idx: i + 1 for i, idx in enumerate(ms)}
        with ExitStack() as es:
            esem = {e: es.enter_context(nc.semaphore("pg_" + e)) for e in ENGS}
            dsem = {}
            for q in ENGS:
                for i in range(min(NDMASEM, self.ndma[q])):
                    dsem[(q, i)] = es.enter_context(nc.semaphore("dm_%s_%d" % (q, i)))
            block = es.enter_context(nc.Block())

            def run(eng_name):
                def body(eng):
                    for o in self.ops[eng_name]:
                        for (f, idx) in o.waits:
                            eng.wait_ge(esem[f], rank[f][idx])
                        for (q, semi, val) in o.dwaits:
                            eng.wait_ge(dsem[(q, semi)], val)
                        if o.fn is None:
                            continue
                        ins = None
                        for (nm, a, k) in o.calls:
                            ins = getattr(eng, nm)(*a, **k)
                        if o.is_dma:
                            ins.then_inc(dsem[(eng_name, o.dsem)], 16)
                        elif o.idx in rank[eng_name]:
                            ins.then_inc(esem[eng_name], 1)
                return body

            block.tensor(run("pe"))
            block.scalar(run("act"))
            block.vector(run("dve"))
            block.gpsimd(run("pool"))
            block.sync(run("sp"))


class Stream:
    def __init__(self):
        self.ops = []

    def _add(self, eng, fn, reads, writes):
        rec = _Rec()
        fn(rec)
        self.ops.append((eng, rec.calls, tuple(reads), tuple(writes)))

    def pe(self, fn, reads=(), writes=()):
        self._add("pe", fn, reads, writes)

    def act(self, fn, reads=(), writes=()):
        self._add("act", fn, reads, writes)

    def dve(self, fn, reads=(), writes=()):
        self._add("dve", fn, reads, writes)

    def pool(self, fn, reads=(), writes=()):
        self._add("pool", fn, reads, writes)


def merge_streams(P, streams):
    lists = [st.ops for st in streams if st is not None and st.ops]
    idx = [0] * len(lists)
    tot = [len(l) for l in lists]
    while True:
        live = [k for k in range(len(lists)) if idx[k] < tot[k]]
        if not live:
            break
        k = min(live, key=lambda q: (idx[q] / tot[q]) * (0.6 if q == 0 else 1.0))
        eng, calls, reads, writes = lists[k][idx[k]]
        P.submit(eng, calls, reads, writes)
        idx[k] += 1


class Ctx:
    def __init__(self):
        self.nc = bass.Bass("TRN2", target_bir_lowering=False)
        self.P = Prog(self.nc)
        self.es = ExitStack()
        self.banks = [self.es.enter_context(self.nc.psum_tensor("pb%d" % i, [128, 512], F32)) for i in range(8)]
        self.nbank = 0
        self.uid = 0
        self.stacks = [self.es]
        self.pcount = {}

    def sb(self, name, shape, dt):
        self.uid += 1
        return self.stacks[-1].enter_context(self.nc.sbuf_tensor("%s_%d" % (name, self.uid), shape, dt))

    def scope_begin(self):
        self.stacks.append(ExitStack())

    def scope_end(self):
        self.P.barrier()
        self.stacks.pop().close()

    def din(self, name, shape, dt=F32):
        return self.nc.dram_tensor(name, list(shape), dt, kind="ExternalInput").ap()

    def dout(self, name, shape, dt=F32):
        return self.nc.dram_tensor(name, list(shape), dt, kind="ExternalOutput").ap()

    def ps(self):
        i = self.nbank % 8
        self.nbank += 1
        return self.banks[i], ("ps", i)

    def key(self, base):
        self.uid += 1
        return (base, self.uid)


def load_cols(C, vec_d, n, name, ident32):
    P = C.P
    rows = C.sb(name + "_r", [n, 128], F32)
    cols = C.sb(name + "_c", [128, n], F32)
    P.dma("sp", rows[:], vec_d.rearrange("(c p) -> c p", p=128), writes=[name + "_r"])
    pb, pk = C.ps()
    P.pe(lambda e: e.transpose(pb[:, 0:n], rows[:], ident32[0:n, 0:n]), reads=[name + "_r", "ident32"], writes=[pk])
    P.dve(lambda e: e.tensor_copy(out=cols[:], in_=pb[:, 0:n]), reads=[pk], writes=[name + "_c"])
    return cols, name + "_c"


INPUT_SHAPES = {
    "x": ([T, D], F32), "c": ([D], F32), "positions": ([T], I32),
    "ada_w": ([4, D, 6 * D], F32), "ada_b": ([4, 6 * D], F32), "norm_mix_g": ([4, D], F32), "norm_ffn_g": ([4, D], F32),
    "gdn_w_in": ([2, D, 4112], F32), "gdn_conv_w": ([2, 3072, 4], F32), "gdn_a_log": ([2, 8], F32), "gdn_dt_bias": ([2, 8], F32),
    "gdn_norm_g": ([2, 128], F32), "gdn_w_out": ([2, D, D], F32),
    "mla_w_in": ([2, D, 704], F32), "mla_q_norm_g": ([2, 384], F32), "mla_kv_norm_g": ([2, 256], F32),
    "mla_w_uq": ([2, 384, 1536], F32), "mla_w_ukv": ([2, 256, 2048], F32), "mla_w_out": ([2, D, D], F32),
    "ffn_w_gate": ([4, D, DFF], F32), "ffn_w_up": ([4, D, DFF], F32), "ffn_w_down": ([4, DFF, D], F32),
    "final_norm_g": ([D], F32),
    "ident": ([128, 128], F32), "invf": ([128], F32), "tri": ([128, 128], F32), "gmasks": ([19, 128, 128], F32),
}


def setup(C):
    P = C.P
    C.I = {nm: C.din(nm, shp, dt) for nm, (shp, dt) in INPUT_SHAPES.items()}
    I = C.I
    ident32 = C.sb("ident32", [128, 128], F32)
    identb = C.sb("identb", [128, 128], BF16)
    ones32 = C.sb("ones32", [128, 128], F32)
    X = C.sb("X", [128, NT, D], F32)
    modc = C.sb("modc", [128, 24], F32)
    gs = C.sb("gs", [128, 8], F32)
    gate_bc = C.sb("gate_bc", [128, D], F32)
    ccol = C.sb("ccol", [128, 8], F32)
    ccolb = C.sb("ccolb", [128, 8], BF16)
    C.epsc = C.sb("epsc", [128, 1], F32)
    P.dve(lambda e: e.memset(C.epsc[:], EPS), writes=["epsc"])
    P.dma("sp", ident32[:], I["ident"], writes=["ident32"])
    P.dma("pool", identb[:], I["ident"], writes=["identb"])
    P.dve(lambda e: e.memset(ones32[:], 1.0), writes=["ones32"])
    xv = I["x"].rearrange("(t p) d -> p t d", p=128)
    for i in range(4):
        P.dma("sp", X[:, 4 * i:4 * i + 4, :], xv[:, 4 * i:4 * i + 4, :], writes=[("X", t) for t in range(4 * i, 4 * i + 4)])
    C.scope_begin()
    ccol_raw, ck = load_cols(C, I["c"], 8, "cc", ident32)
    P.act(lambda e: e.activation(out=ccol[:], in_=ccol_raw[:], func=AF.Silu), reads=[ck], writes=["ccol"])
    P.dve(lambda e: e.tensor_copy(out=ccolb[:], in_=ccol[:]), reads=["ccol"], writes=["ccolb"])
    C.scope_end()
    return dict(X=X, modc=modc, gs=gs, gate_bc=gate_bc, ident32=ident32, identb=identb, ones32=ones32, ccol=ccol, ccolb=ccolb)


def mod_compute(C, R, layer, groups, ng_d):
    P = C.P
    I = C.I
    modc, gs, gate_bc, ident32, ones32, ccol = R["modc"], R["gs"], R["gate_bc"], R["ident32"], R["ones32"], R["ccol"]
    adaw_d = I["ada_w"][layer]
    adab_d = I["ada_b"][layer]
    C.scope_begin()
    bcol, bk = load_cols(C, adab_d, 48, "ab", ident32)
    gcol, gk = load_cols(C, ng_d, 8, "ng", ident32)
    aws = [C.sb("aw%d" % i, [128, 8, 512], BF16) for i in range(2)]
    ccolb = R["ccolb"]
    pb, pk = C.ps()
    adv = adaw_d.rearrange("(k p) f -> p k f", p=128)
    na = 0
    for gi, g in enumerate(groups):
        for jj in range(2):
            aw = aws[na % 2]
            awk = ("aw", na % 2)
            na += 1
            P.dma("pool", aw[:], adv[:, :, g * D + jj * 512:g * D + (jj + 1) * 512], writes=[awk])
            for j4 in range(4):
                j = jj * 4 + j4
                for k in range(8):
                    P.pe(lambda e, j=j, j4=j4, k=k, gi=gi, aw=aw: e.matmul(pb[:, gi * 8 + j:gi * 8 + j + 1], lhsT=aw[:, k, j4 * 128:(j4 + 1) * 128],
                                                                         rhs=ccolb[:, k:k + 1], start=(k == 0), stop=(k == 7)),
                         reads=[awk, "ccolb"], writes=[pk])
    for gi, g in enumerate(groups):
        P.dve(lambda e, gi=gi, g=g: e.tensor_tensor(out=modc[:, gi * 8:gi * 8 + 8], in0=pb[:, gi * 8:gi * 8 + 8],
                                                    in1=bcol[:, g * 8:g * 8 + 8], op=ALU.add),
              reads=[pk, bk], writes=["modc"])
    P.dve(lambda e: e.scalar_tensor_tensor(out=gs[:], in0=modc[:, 8:16], scalar=1.0, in1=gcol[:], op0=ALU.add, op1=ALU.mult),
          reads=["modc", gk], writes=["gs"])
    dg = C.sb("dg", [128, 128], F32)
    for j in range(8):
        P.dve(lambda e, j=j: e.tensor_scalar(out=dg[:], in0=ident32[:], scalar1=modc[:, 16 + j:17 + j], scalar2=None, op0=ALU.mult),
              reads=["ident32", "modc"], writes=["dg"])
        pb2, pk2 = C.ps()
        P.pe(lambda e, pb2=pb2: e.matmul(pb2[:, 0:128], lhsT=ones32[:], rhs=dg[:], start=True, stop=True),
             reads=["ones32", "dg"], writes=[pk2])
        P.act(lambda e, j=j, pb2=pb2: e.copy(out=gate_bc[:, j * 128:(j + 1) * 128], in_=pb2[:, 0:128]), reads=[pk2], writes=["gate_bc"])
    C.scope_end()


def rstd_col(C, X, t, small, ki, junk):
    P = C.P
    kk = ("small", ki)
    P.act(lambda e: e.activation(out=junk[:], in_=X[:, t, :], func=AF.Square, accum_out=small[:, ki:ki + 1]),
          reads=[("X", t)], writes=["junk", kk])
    P.act(lambda e: e.activation(out=small[:, ki:ki + 1], in_=small[:, ki:ki + 1], func=AF.Ln, scale=1.0 / D, bias=C.epsc[:, 0:1]),
          reads=[kk, "epsc"], writes=[kk])
    P.act(lambda e: e.activation(out=small[:, ki:ki + 1], in_=small[:, ki:ki + 1], func=AF.Exp, scale=-0.5),
          reads=[kk], writes=[kk])
    return kk


def norm_to_hT(C, R, hT):
    P = C.P
    X, modc, gs, identb = R["X"], R["modc"], R["gs"], R["identb"]
    C.scope_begin()
    small = C.sb("nsmall", [128, NT], F32)
    junk = C.sb("junk", [128, D], BF16)
    tmps = [C.sb("tmpn%d" % i, [128, D], F32) for i in range(2)]
    xn = [C.sb("xn%d" % i, [128, D], BF16) for i in range(2)]
    for t in range(NT):
        tmp = tmps[t % 2]
        tk = ("tmpn", t % 2)
        kk = rstd_col(C, X, t, small, t, junk)
        xb = xn[t % 2]
        xk = ("xn", t % 2)
        P.dve(lambda e, xb=xb, t=t: e.tensor_scalar(out=xb[:], in0=X[:, t, :], scalar1=small[:, t:t + 1], scalar2=None, op0=ALU.mult),
              reads=[("X", t), kk], writes=[xk])
        pb, pk = C.ps()
        pbb = pb[:].bitcast(BF16)
        for c in range(8):
            P.pe(lambda e, c=c, xb=xb, pbb=pbb: e.transpose(pbb[:, c * 128:(c + 1) * 128], xb[:, c * 128:(c + 1) * 128], identb[:]),
                 reads=[xk, "identb"], writes=[pk])
        P.dve(lambda e, pbb=pbb, tmp=tmp: e.tensor_tensor(out=tmp[:].rearrange("p (c i) -> p c i", c=8), in0=pbb.rearrange("p (c i) -> p c i", c=8),
                                                          in1=gs[:].unsqueeze(2).to_broadcast([128, 8, 128]), op=ALU.mult),
              reads=[pk, "gs"], writes=[tk])
        P.pool(lambda e, t=t, tmp=tmp: e.tensor_tensor(out=hT[:, :, t * 128:(t + 1) * 128], in0=tmp[:].rearrange("p (c i) -> p c i", c=8),
                                                       in1=modc[:, 0:8].unsqueeze(2).to_broadcast([128, 8, 128]), op=ALU.add),
               reads=[tk, "modc"], writes=[("hT", t)])
    C.scope_end()


def store_x(C, X, name="out"):
    P = C.P
    o_d = C.dout(name, [T, D])
    ov = o_d.rearrange("(t p) d -> p t d", p=128)
    for i in range(4):
        P.dma("sp", ov[:, 4 * i:4 * i + 4, :], X[:, 4 * i:4 * i + 4, :], reads=[("X", t) for t in range(4 * i, 4 * i + 4)], is_out=True)


def ffn_body(C, R, hT, layer, final):
    P = C.P
    I = C.I
    ng_d = I["norm_ffn_g"][layer]
    wg_d = I["ffn_w_gate"][layer]
    wu_d = I["ffn_w_up"][layer]
    wd_d = I["ffn_w_down"][layer]
    mod_compute(C, R, layer, [3, 4, 5], ng_d)
    X, gate_bc = R["X"], R["gate_bc"]
    norm_to_hT(C, R, hT)
    G = 4
    wgu = [C.sb("wgu%d" % i, [128, 8, 2, G * 128], BF16) for i in range(2)]
    wdn = [C.sb("wdn%d" % i, [128, G, D], BF16) for i in range(2)]
    hid = [C.sb("hid%d" % i, [128, G, 512], BF16) for i in range(2)]
    sg = [C.sb("sg%d" % i, [128, 512], F32) for i in range(2)]
    wgv = wg_d.rearrange("(k p) f -> p k f", p=128)
    wuv = wu_d.rearrange("(k p) f -> p k f", p=128)
    wdv = wd_d.rearrange("(c p) n -> p c n", p=128)
    groups = []
    c0 = 0
    while c0 < NFC:
        g = min(G, NFC - c0)
        groups.append((c0, g))
        c0 += g
    nsg = 0

    def load_group(gi):
        c0, g = groups[gi]
        b = gi % 2
        kgu = ("wgu", b)
        kd = ("wdn", b)
        P.dma("pool", wgu[b][:, :, 0, 0:g * 128], wgv[:, :, c0 * 128:(c0 + g) * 128], writes=[kgu])
        P.dma("pool", wgu[b][:, :, 1, 0:g * 128], wuv[:, :, c0 * 128:(c0 + g) * 128], writes=[kgu])
        P.dma("pool", wdn[b][:, 0:g, :], wdv[:, c0:c0 + g, :], writes=[kd])
        P.pool(lambda e: e.tensor_tensor(out=wdn[b][:, 0:g, :], in0=wdn[b][:, 0:g, :],
                                         in1=gate_bc[:].unsqueeze(1).to_broadcast([128, g, D]), op=ALU.mult),
               reads=[kd, "gate_bc"], writes=[kd])

    def gate_up(gi, tb, hb):
        nonlocal nsg
        c0, g = groups[gi]
        b = gi % 2
        kgu = ("wgu", b)
        hk = ("hid", hb)
        for j in range(g):
            pg, pgk = C.ps()
            pu, puk = C.ps()
            for k in range(8):
                P.pe(lambda e, k=k: e.matmul(pg[:], lhsT=wgu[b][:, k, 0, j * 128:(j + 1) * 128], rhs=hT[:, k, tb * 512:(tb + 1) * 512],
                                            start=(k == 0), stop=(k == 7)),
                     reads=[kgu] + [("hT", 4 * tb + i) for i in range(4)], writes=[pgk])
            for k in range(8):
                P.pe(lambda e, k=k: e.matmul(pu[:], lhsT=wgu[b][:, k, 1, j * 128:(j + 1) * 128], rhs=hT[:, k, tb * 512:(tb + 1) * 512],
                                            start=(k == 0), stop=(k == 7)),
                     reads=[kgu] + [("hT", 4 * tb + i) for i in range(4)], writes=[puk])
            sb_ = nsg % 2
            nsg += 1
            sk = ("sg", sb_)
            P.act(lambda e: e.activation(out=sg[sb_][:], in_=pg[:], func=AF.Silu), reads=[pgk], writes=[sk])
            P.dve(lambda e: e.tensor_tensor(out=hid[hb][:, j, :], in0=sg[sb_][:], in1=pu[:], op=ALU.mult), reads=[sk, puk], writes=[hk])

    def down(gi, tb, hb):
        c0, g = groups[gi]
        b = gi % 2
        kd = ("wdn", b)
        hk = ("hid", hb)
        for tt in range(4):
            t = tb * 4 + tt
            for half in range(2):
                po, pok = C.ps()
                for j in range(g):
                    P.pe(lambda e, j=j: e.matmul(po[:], lhsT=hid[hb][:, j, tt * 128:(tt + 1) * 128], rhs=wdn[b][:, j, half * 512:(half + 1) * 512],
                                                start=(j == 0), stop=(j == g - 1)), reads=[hk, kd], writes=[pok])
                P.dve(lambda e: e.tensor_tensor(out=X[:, t, half * 512:(half + 1) * 512], in0=X[:, t, half * 512:(half + 1) * 512], in1=po[:], op=ALU.add),
                      reads=[pok, ("X", t)], writes=[("X", t)])

    items = [(gi, tb) for gi in range(len(groups)) for tb in range(4)]
    load_group(0)
    prev = None
    for ii, (gi, tb) in enumerate(items):
        hb = ii % 2
        gate_up(gi, tb, hb)
        if prev is not None:
            down(*prev)
        if tb == 0 and gi + 1 < len(groups):
            load_group(gi + 1)
        prev = (gi, tb, hb)
    down(*prev)
    if final:
        fg_d = I["final_norm_g"]
        fgb = C.sb("fgb", [128, D], F32)
        P.dma("sp", fgb[:], fg_d.unsqueeze(0).to_broadcast([128, D]), writes=["fgb"])
        small = C.sb("fsmall", [128, NT], F32)
        fjunk = C.sb("fjunk", [128, D], BF16)
        for t in range(NT):
            kk = rstd_col(C, X, t, small, t, fjunk)
            P.dve(lambda e, t=t: e.scalar_tensor_tensor(out=X[:, t, :], in0=X[:, t, :], scalar=small[:, t:t + 1], in1=fgb[:],
                                                        op0=ALU.mult, op1=ALU.mult),
                  reads=[("X", t), kk, "fgb"], writes=[("X", t)])


def _ident():
    return np.eye(128, dtype=np.float32)


QR = 384
KVR = 256
SCALE = float(192 ** -0.5)
TWO_PI = float(2 * np.pi)


def psn(C, pool):
    i = pool[C.pcount.get(tuple(pool), 0) % len(pool)]
    C.pcount[tuple(pool)] = C.pcount.get(tuple(pool), 0) + 1
    return C.banks[i], ("ps", i)


def mla_body(C, R, hT, layer):
    P = C.P
    I = C.I
    j_ = layer // 2
    ng_d = I["norm_mix_g"][layer]
    win_d = I["mla_w_in"][j_]
    qg_d = I["mla_q_norm_g"][j_]
    kvg_d = I["mla_kv_norm_g"][j_]
    wuq_d = I["mla_w_uq"][j_]
    wukv_d = I["mla_w_ukv"][j_]
    wout_d = I["mla_w_out"][j_]
    pos_d = I["positions"]
    invf_d = I["invf"]
    tri_d = I["tri"]
    mod_compute(C, R, layer, [0, 1, 2], ng_d)
    X, gate_bc, ident32, identb = R["X"], R["gate_bc"], R["ident32"], R["identb"]
    norm_to_hT(C, R, hT)
    ALLB = list(range(8))

    trib = C.sb("trib", [128, 128], BF16)
    P.dma("pool", trib[:], tri_d, writes=["trib"])
    wbuf = C.sb("wbuf", [128, 8 * D], BF16)
    w_in = wbuf[:, 0:8 * 704].rearrange("p (k f) -> p k f", k=8)
    w_out = wbuf[:].rearrange("p (k f) -> p k f", k=8)
    P.dma("pool", w_in, win_d.rearrange("(k p) f -> p k f", p=128), writes=["wbuf"])
    w_uq = C.sb("w_uq", [128, 3, 1536], BF16)
    P.dma("pool", w_uq[:], wuq_d.rearrange("(k p) f -> p k f", p=128), writes=["w_uq"])
    w_ukv = C.sb("w_ukv", [128, 2, 2048], BF16)
    P.dma("pool", w_ukv[:], wukv_d.rearrange("(k p) f -> p k f", p=128), writes=["w_ukv"])
    wk2A = C.sb("wk2A", [128, 8, 128], BF16)
    wk2B = C.sb("wk2B", [128, 8, 128], BF16)
    for half in range(2):
        o_ = half * 64
        P.act(lambda e, o_=o_: e.copy(out=wk2A[:, :, o_:o_ + 64], in_=w_in[:, :, 640:704]), reads=["wbuf"], writes=["wk2A"])
        P.act(lambda e, o_=o_: e.mul(out=wk2B[:, :, o_:o_ + 32], in_=w_in[:, :, 672:704], mul=-1.0), reads=["wbuf"], writes=["wk2B"])
        P.act(lambda e, o_=o_: e.copy(out=wk2B[:, :, o_ + 32:o_ + 64], in_=w_in[:, :, 640:672]), reads=["wbuf"], writes=["wk2B"])
    wq2 = C.sb("wq2", [128, 3, 8, 128], BF16)
    wq4 = w_uq[:].rearrange("p k (h f) -> p k h f", h=8)
    P.act(lambda e: e.copy(out=wq2[:, :, :, 0:64], in_=wq4[:, :, :, 128:192]), reads=["w_uq"], writes=["wq2"])
    P.act(lambda e: e.mul(out=wq2[:, :, :, 64:96], in_=wq4[:, :, :, 160:192], mul=-1.0), reads=["w_uq"], writes=["wq2"])
    P.act(lambda e: e.copy(out=wq2[:, :, :, 96:128], in_=wq4[:, :, :, 128:160]), reads=["w_uq"], writes=["wq2"])
    qgc, qgk = load_cols(C, qg_d, 3, "qg", ident32)
    kvgc, kvgk = load_cols(C, kvg_d, 2, "kvg", ident32)
    g5 = C.sb("g5", [128, 5], F32)
    P.dve(lambda e: e.tensor_copy(out=g5[:, 0:3], in_=qgc[:]), reads=[qgk], writes=["g5"])
    P.dve(lambda e: e.tensor_copy(out=g5[:, 3:5], in_=kvgc[:]), reads=[kvgk], writes=["g5"])

    cosT = C.sb("cosT", [128, T], BF16)
    sinT = C.sb("sinT", [128, T], BF16)
    cs2 = C.sb("cs2", [128, T], BF16)
    invf = C.sb("invf", [128, 1], F32)
    P.dma("sp", invf[:], invf_d.rearrange("(p o) -> p o", o=1), writes=["invf"])
    C.scope_begin()
    posi = C.sb("posi", [128, 512], I32)
    xs = C.sb("xs", [128, 512], F32)
    ri = C.sb("ri", [128, 512], I32)
    rf = C.sb("rf", [128, 512], F32)
    for tb in range(4):
        sl = slice(tb * 512, (tb + 1) * 512)
        P.dma("sp", posi[:], pos_d[sl].unsqueeze(0).to_broadcast([128, 512]), writes=["posi"])
        P.dve(lambda e: e.tensor_copy(out=xs[:], in_=posi[:]), reads=["posi"], writes=["xs"])
        P.dve(lambda e: e.tensor_scalar(out=xs[:], in0=xs[:], scalar1=invf[:, 0:1], scalar2=1.0 / TWO_PI, op0=ALU.mult, op1=ALU.mult),
              reads=["xs", "invf"], writes=["xs"])
        for which, tab in ((0, sinT), (1, cosT)):
            if which == 1:
                P.dve(lambda e: e.tensor_scalar(out=xs[:], in0=xs[:], scalar1=0.25, scalar2=None, op0=ALU.add), reads=["xs"], writes=["xs"])
            P.dve(lambda e: e.tensor_copy(out=ri[:], in_=xs[:]), reads=["xs"], writes=["ri"])
            P.dve(lambda e: e.tensor_copy(out=rf[:], in_=ri[:]), reads=["ri"], writes=["rf"])
            P.dve(lambda e: e.tensor_tensor(out=rf[:], in0=xs[:], in1=rf[:], op=ALU.subtract), reads=["xs", "rf"], writes=["rf"])
            P.act(lambda e, tab=tab, sl=sl: e.activation(out=tab[:, sl], in_=rf[:], func=AF.Sin, scale=TWO_PI * (1.0 - 2e-7)),
                  reads=["rf"], writes=[("cs", tb)])
        P.act(lambda e, sl=sl: e.copy(out=cs2[0:64, sl], in_=cosT[0:64, sl]), reads=[("cs", tb)], writes=[("cs2", tb)])
        P.act(lambda e, sl=sl: e.copy(out=cs2[64:128, sl], in_=sinT[64:128, sl]), reads=[("cs", tb)], writes=[("cs2", tb)])
    C.scope_end()

    cT = C.sb("cT", [128, 5, T], BF16)
    C.scope_begin()
    mjunk = C.sb("mjunk", [128, 640], BF16)
    clats = [C.sb("clat%d" % i, [128, 640], F32) for i in range(2)]
    cns = [C.sb("cn%d" % i, [128, 640], BF16) for i in range(2)]
    sms = [C.sb("msmall%d" % i, [128, 4], F32) for i in range(2)]
    for t in range(NT):
        clat, cn, sm = clats[t % 2], cns[t % 2], sms[t % 2]
        kcl, kcn, ksm = ("clat", t % 2), ("cn", t % 2), ("msm", t % 2)
        pa, pak = psn(C, ALLB)
        pb_, pbk = psn(C, ALLB)
        for k in range(8):
            P.pe(lambda e, k=k, t=t, pa=pa: e.matmul(pa[:], lhsT=hT[:, k, t * 128:(t + 1) * 128], rhs=w_in[:, k, 0:512],
                                                   start=(k == 0), stop=(k == 7)), reads=[("hT", t), "wbuf"], writes=[pak])
        for k in range(8):
            P.pe(lambda e, k=k, t=t, pb_=pb_: e.matmul(pb_[:, 0:128], lhsT=hT[:, k, t * 128:(t + 1) * 128], rhs=w_in[:, k, 512:640],
                                                     start=(k == 0), stop=(k == 7)), reads=[("hT", t), "wbuf"], writes=[pbk])
        P.act(lambda e, pa=pa, clat=clat: e.copy(out=clat[:, 0:512], in_=pa[:]), reads=[pak], writes=[kcl])
        P.act(lambda e, pb_=pb_, clat=clat: e.copy(out=clat[:, 512:640], in_=pb_[:, 0:128]), reads=[pbk], writes=[kcl])
        P.act(lambda e, clat=clat, sm=sm: e.activation(out=mjunk[:, 0:384], in_=clat[:, 0:384], func=AF.Square, accum_out=sm[:, 0:1]),
              reads=[kcl], writes=["mjunk", ksm])
        P.act(lambda e, clat=clat, sm=sm: e.activation(out=mjunk[:, 384:640], in_=clat[:, 384:640], func=AF.Square, accum_out=sm[:, 1:2]),
              reads=[kcl], writes=["mjunk", ksm])
        P.act(lambda e, sm=sm: e.activation(out=sm[:, 0:1], in_=sm[:, 0:1], func=AF.Ln, scale=1.0 / QR, bias=C.epsc[:, 0:1]), reads=[ksm, "epsc"], writes=[ksm])
        P.act(lambda e, sm=sm: e.activation(out=sm[:, 1:2], in_=sm[:, 1:2], func=AF.Ln, scale=1.0 / KVR, bias=C.epsc[:, 0:1]), reads=[ksm, "epsc"], writes=[ksm])
        P.act(lambda e, sm=sm: e.activation(out=sm[:, 0:2], in_=sm[:, 0:2], func=AF.Exp, scale=-0.5), reads=[ksm], writes=[ksm])
        P.dve(lambda e, clat=clat, cn=cn, sm=sm: e.tensor_scalar(out=cn[:, 0:384], in0=clat[:, 0:384], scalar1=sm[:, 0:1], scalar2=None, op0=ALU.mult),
              reads=[kcl, ksm], writes=[kcn])
        P.dve(lambda e, clat=clat, cn=cn, sm=sm: e.tensor_scalar(out=cn[:, 384:640], in0=clat[:, 384:640], scalar1=sm[:, 1:2], scalar2=None, op0=ALU.mult),
              reads=[kcl, ksm], writes=[kcn])
        pt, ptk = psn(C, ALLB)
        ptb = pt[:].bitcast(BF16)
        for c in range(5):
            P.pe(lambda e, c=c, ptb=ptb, cn=cn: e.transpose(ptb[:, c * 128:(c + 1) * 128], cn[:, c * 128:(c + 1) * 128], identb[:]),
                 reads=[kcn, "identb"], writes=[ptk])
        P.dve(lambda e, t=t, ptb=ptb: e.tensor_tensor(out=cT[:, :, t * 128:(t + 1) * 128],
                                                      in0=ptb[:, 0:640].rearrange("p (c i) -> p c i", c=5),
                                                      in1=g5[:].unsqueeze(2).to_broadcast([128, 5, 128]), op=ALU.mult),
              reads=[ptk, "g5"], writes=[("cT", t)])
    C.scope_end()
    krT = C.sb("krT", [128, T], BF16)
    C.scope_begin()
    t1 = C.sb("t1", [128, 512], F32)
    t2 = C.sb("t2", [128, 512], F32)
    for tb in range(4):
        sl = slice(tb * 512, (tb + 1) * 512)
        pA, pAk = psn(C, ALLB)
        pB, pBk = psn(C, ALLB)
        hk = [("hT", 4 * tb + i) for i in range(4)]
        for k in range(8):
            P.pe(lambda e, k=k, pA=pA, sl=sl: e.matmul(pA[:], lhsT=wk2A[:, k, :], rhs=hT[:, k, sl], start=(k == 0), stop=(k == 7)),
                 reads=hk + ["wk2A"], writes=[pAk])
        for k in range(8):
            P.pe(lambda e, k=k, pB=pB, sl=sl: e.matmul(pB[:], lhsT=wk2B[:, k, :], rhs=hT[:, k, sl], start=(k == 0), stop=(k == 7)),
                 reads=hk + ["wk2B"], writes=[pBk])
        P.dve(lambda e, pA=pA, sl=sl: e.tensor_tensor(out=t1[:], in0=pA[:], in1=cosT[:, sl], op=ALU.mult), reads=[pAk, ("cs", tb)], writes=["t1"])
        P.dve(lambda e, pB=pB, sl=sl: e.tensor_tensor(out=t2[:], in0=pB[:], in1=sinT[:, sl], op=ALU.mult), reads=[pBk, ("cs", tb)], writes=["t2"])
        P.dve(lambda e, sl=sl: e.tensor_tensor(out=krT[:, sl], in0=t1[:], in1=t2[:], op=ALU.add), reads=["t1", "t2"], writes=[("krT", tb)])
    C.scope_end()

    P.dma("pool", w_out, wout_d.rearrange("(k p) f -> p k f", p=128), writes=["wbuf"])
    P.pool(lambda e: e.tensor_tensor(out=w_out, in0=w_out, in1=gate_bc[:].unsqueeze(1).to_broadcast([128, 8, D]), op=ALU.mult),
           reads=["wbuf", "gate_bc"], writes=["wbuf"])

    oT = hT
    qn = C.sb("qn", [128, T], BF16)
    qr = C.sb("qr", [128, T], BF16)
    kn = C.sb("kn", [128, T], BF16)
    V = C.sb("V", [128, NT, 128], BF16)
    onesb = C.sb("onesb", [128, 128], BF16)
    P.dve(lambda e: e.memset(onesb[:], 1.0), writes=["onesb"])
    pT = [C.sb("pT%d" % i, [128, 512], BF16) for i in range(3)]
    rec = [C.sb("rec%d" % i, [128, 512], F32) for i in range(2)]
    OB = [0, 1]
    SMB = [2, 3]
    SB_ = [4, 5]
    MB = [6, 7]
    PJB = [6, 7, 4, 5, 2, 3]
    npt = 0
    for h in range(8):
        for tb in range(4):
            sl = slice(tb * 512, (tb + 1) * 512)
            ck = [("cT", 4 * tb + i) for i in range(4)]
            p1, p1k = psn(C, PJB)
            for k in range(3):
                P.pe(lambda e, k=k, p1=p1, sl=sl, h=h: e.matmul(p1[:], lhsT=w_uq[:, k, h * 192:h * 192 + 128], rhs=cT[:, k, sl],
                                                              start=(k == 0), stop=(k == 2)), reads=ck + ["w_uq"], writes=[p1k])
            P.act(lambda e, p1=p1, sl=sl: e.copy(out=qn[:, sl], in_=p1[:]), reads=[p1k], writes=[("qn", tb)])
            p2, p2k = psn(C, PJB)
            for k in range(2):
                P.pe(lambda e, k=k, p2=p2, sl=sl, h=h: e.matmul(p2[:], lhsT=w_ukv[:, k, h * 256:h * 256 + 128], rhs=cT[:, 3 + k, sl],
                                                              start=(k == 0), stop=(k == 1)), reads=ck + ["w_ukv"], writes=[p2k])
            P.act(lambda e, p2=p2, sl=sl: e.copy(out=kn[:, sl], in_=p2[:]), reads=[p2k], writes=[("kn", tb)])
            pA, pAk = psn(C, PJB)
            for k in range(3):
                P.pe(lambda e, k=k, pA=pA, sl=sl, h=h: e.matmul(pA[:], lhsT=wq2[:, k, h, :], rhs=cT[:, k, sl],
                                                              start=(k == 0), stop=(k == 2)), reads=ck + ["wq2"], writes=[pAk])
            P.dve(lambda e, pA=pA, sl=sl: e.tensor_tensor(out=qr[:, sl], in0=pA[:], in1=cs2[:, sl], op=ALU.mult),
                  reads=[pAk, ("cs2", tb)], writes=[("qr", tb)])
            p3, p3k = psn(C, PJB)
            for i in range(4):
                t = 4 * tb + i
                for k in range(2):
                    P.pe(lambda e, k=k, p3=p3, i=i, t=t, h=h: e.matmul(p3[:, i * 128:(i + 1) * 128], lhsT=cT[:, 3 + k, t * 128:(t + 1) * 128],
                                                                     rhs=w_ukv[:, k, h * 256 + 128:h * 256 + 256], start=(k == 0), stop=(k == 1)),
                         reads=[("cT", t), "w_ukv"], writes=[p3k])
            P.act(lambda e, p3=p3, tb=tb: e.copy(out=V[:, 4 * tb:4 * tb + 4, :], in_=p3[:].rearrange("p (i d) -> p i d", i=4)),
                  reads=[p3k], writes=[("V", tb)])
        items = []
        for qb in range(4):
            for kt in range(4 * qb + 4):
                items.append((qb, kt))
        accs = {}

        def emit_scores(qb, kt):
            nonlocal npt
            q0 = max(kt, 4 * qb)
            n = (4 * qb + 4 - q0) * 128
            qsl = slice(q0 * 128, (4 * qb + 4) * 128)
            ps_, psk = psn(C, SB_)
            P.pe(lambda e: e.matmul(ps_[:, 0:n], lhsT=kn[:, kt * 128:(kt + 1) * 128], rhs=qn[:, qsl], start=True, stop=False),
                 reads=[("kn", kt // 4), ("qn", qb)], writes=[psk])
            P.pe(lambda e: e.matmul(ps_[:, 0:n], lhsT=krT[:, kt * 128:(kt + 1) * 128], rhs=qr[:, qsl], start=False, stop=True),
                 reads=[("krT", kt // 4), ("qr", qb)], writes=[psk])
            pb_i = npt % 3
            npt += 1
            ptile = pT[pb_i]
            pk_ = ("pT", pb_i)
            P.act(lambda e: e.activation(out=ptile[:, 0:n], in_=ps_[:, 0:n], func=AF.Exp, scale=SCALE), reads=[psk], writes=[pk_])
            if kt >= 4 * qb:
                P.pool(lambda e: e.tensor_tensor(out=ptile[:, 0:128], in0=ptile[:, 0:128], in1=trib[:], op=ALU.mult),
                       reads=[pk_, "trib"], writes=[pk_])
            return ptile, pk_

        def emit_pv(qb, kt, ptile, pk_):
            if kt == 0:
                accs[qb] = (psn(C, OB), psn(C, SMB))
            (po, pok), (psm, psmk) = accs[qb]
            nkt = 4 * qb + 4
            q0 = max(kt, 4 * qb)
            n = (4 * qb + 4 - q0) * 128
            off = (q0 - 4 * qb) * 128
            P.pe(lambda e: e.matmul(po[:, off:off + n], lhsT=V[:, kt, :], rhs=ptile[:, 0:n], start=(kt == 0), stop=(kt == nkt - 1)),
                 reads=[pk_, ("V", kt // 4)], writes=[pok])
            P.pe(lambda e: e.matmul(psm[:, off:off + n], lhsT=onesb[:], rhs=ptile[:, 0:n], start=(kt == 0), stop=(kt == nkt - 1)),
                 reads=[pk_, "onesb"], writes=[psmk])
            if kt == nkt - 1:
                rc = rec[qb % 2]
                rck = ("rec", qb % 2)
                P.dve(lambda e: e.reciprocal(out=rc[:], in_=psm[:]), reads=[psmk], writes=[rck])
                P.dve(lambda e: e.tensor_tensor(out=oT[:, h, qb * 512:(qb + 1) * 512], in0=po[:], in1=rc[:], op=ALU.mult),
                      reads=[pok, rck], writes=[("hT", 4 * qb + i) for i in range(4)])

        cur = emit_scores(*items[0])
        for ii, (qb, kt) in enumerate(items):
            nxt = emit_scores(*items[ii + 1]) if ii + 1 < len(items) else None
            emit_pv(qb, kt, *cur)
            cur = nxt

    for t in range(NT):
        for half in range(2):
            po, pok = psn(C, ALLB)
            for h in range(8):
                P.pe(lambda e, po=po, h=h, t=t, half=half: e.matmul(po[:], lhsT=oT[:, h, t * 128:(t + 1) * 128],
                                                                  rhs=w_out[:, h, half * 512:(half + 1) * 512], start=(h == 0), stop=(h == 7)),
                     reads=[("hT", t), "wbuf"], writes=[pok])
            P.dve(lambda e, po=po, t=t, half=half: e.tensor_tensor(out=X[:, t, half * 512:(half + 1) * 512],
                                                                  in0=X[:, t, half * 512:(half + 1) * 512], in1=po[:], op=ALU.add),
                  reads=[pok, ("X", t)], writes=[("X", t)])


def _invf64():
    f = (10000.0 ** (-np.arange(0, 64, 2, dtype=np.float32) / np.float32(64))).astype(np.float32)
    return np.concatenate([f, f, f, f]).astype(np.float32)


def _tri():
    k = np.arange(128)[:, None]
    q = np.arange(128)[None, :]
    return (k <= q).astype(np.float32)


HG = 2
NMASK = 19


def _gmasks():
    idx = np.arange(128)
    p = idx[:, None]
    f = idx[None, :]
    m = []
    m.append((p <= f).astype(np.float32))
    m.append((p > f).astype(np.float32))
    m.append((f < p).astype(np.float32))
    m.append(np.where(f <= p, 0.0, -30000.0).astype(np.float32))
    m.append(np.zeros((128, 128), np.float32))
    b = 1
    while b < 128:
        blk = idx // b
        mU = ((blk[:, None] // 2) == (blk[None, :] // 2)) & ((blk[:, None] % 2) == 0) & ((blk[None, :] % 2) == 1)
        m.append(-mU.astype(np.float32))
        m.append(-mU.T.astype(np.float32))
        b *= 2
    return np.stack(m).astype(np.float32)


def gdn_body(C, R, hT, layer):
    P = C.P
    I = C.I
    j_ = layer // 2
    ng_d = I["norm_mix_g"][layer]
    win_d = I["gdn_w_in"][j_]
    conv_d = I["gdn_conv_w"][j_]
    alog_d = I["gdn_a_log"][j_]
    dtb_d = I["gdn_dt_bias"][j_]
    gng_d = I["gdn_norm_g"][j_]
    wout_d = I["gdn_w_out"][j_]
    gm_d = I["gmasks"]
    mod_compute(C, R, layer, [0, 1, 2], ng_d)
    X, gate_bc, ident32, identb, ones32 = R["X"], R["gate_bc"], R["ident32"], R["identb"], R["ones32"]
    norm_to_hT(C, R, hT)
    ALLB = list(range(8))
    winv = win_d.rearrange("(k p) f -> p k f", p=128)

    tri32 = C.sb("tri32", [128, 2, 128], F32)
    P.dma("sp", tri32[:], gm_d[0:2].rearrange("m p f -> p m f"), writes=["tri32"])
    mk32 = C.sb("mk32", [128, 2, 128], F32)
    P.dma("sp", mk32[:], gm_d[2:4].rearrange("m p f -> p m f"), writes=["mk32"])
    lvm = C.sb("lvm", [128, 14, 128], BF16)
    P.dma("pool", lvm[:], gm_d[5:19].rearrange("m p f -> p m f"), writes=["lvm"])
    onesb = C.sb("onesb", [128, 128], BF16)
    P.dve(lambda e: e.memset(onesb[:], 1.0), writes=["onesb"])
    cw = C.sb("cw", [128, 24, 4], F32)
    cvv = conv_d.rearrange("(c p) k -> p c k", p=128)
    for c_ in range(24):
        P.dma("sp", cw[:, c_, :], cvv[:, c_, :], writes=["cw"])
    gnb = C.sb("gnb", [128, 128], F32)
    P.dma("sp", gnb[:], gng_d.unsqueeze(0).to_broadcast([128, 128]), writes=["gnb"])
    alb = C.sb("alb", [128, 8], F32)
    dtb = C.sb("dtb", [128, 8], F32)
    P.dma("sp", alb[:], alog_d.unsqueeze(0).to_broadcast([128, 8]), writes=["alb"])
    P.dma("sp", dtb[:], dtb_d.unsqueeze(0).to_broadcast([128, 8]), writes=["dtb"])

    col = {nm: C.sb("col_" + nm, [128, NT, 8], F32) for nm in ("g", "beta", "gc", "egc", "cb", "edec", "gl")}
    C.scope_begin()
    wab = C.sb("wab", [128, 8, 16], BF16)
    P.dma("pool", wab[:], winv[:, :, 4096:4112], writes=["wab"])
    abv = C.sb("abv", [128, NT, 16], F32)
    tA = C.sb("tA", [128, NT, 8], F32)
    tB = C.sb("tB", [128, NT, 8], F32)
    pab, pabk = psn(C, ALLB)
    for t in range(NT):
        for k in range(8):
            P.pe(lambda e, t=t, k=k: e.matmul(pab[:, t * 16:(t + 1) * 16], lhsT=hT[:, k, t * 128:(t + 1) * 128], rhs=wab[:, k, :],
                                              start=(k == 0), stop=(k == 7)), reads=[("hT", t), "wab"], writes=[pabk])
    P.act(lambda e: e.copy(out=abv[:].rearrange("p t c -> p (t c)"), in_=pab[:, 0:256]), reads=[pabk], writes=["abv"])
    P.dve(lambda e: e.tensor_tensor(out=tA[:], in0=abv[:, :, 0:8], in1=dtb[:].unsqueeze(1).to_broadcast([128, NT, 8]), op=ALU.add),
          reads=["abv", "dtb"], writes=["tA"])
    P.act(lambda e: e.activation(out=tB[:], in_=tA[:], func=AF.Abs), reads=["tA"], writes=["tB"])
    P.act(lambda e: e.activation(out=tB[:], in_=tB[:], func=AF.Exp, scale=-1.0), reads=["tB"], writes=["tB"])
    P.act(lambda e: e.activation(out=tB[:], in_=tB[:], func=AF.Ln, scale=1.0, bias=1.0), reads=["tB"], writes=["tB"])
    P.dve(lambda e: e.tensor_scalar(out=tA[:], in0=tA[:], scalar1=0.0, scalar2=None, op0=ALU.max), reads=["tA"], writes=["tA"])
    P.dve(lambda e: e.tensor_tensor(out=tA[:], in0=tA[:], in1=tB[:], op=ALU.add), reads=["tA", "tB"], writes=["tA"])
    P.act(lambda e: e.activation(out=alb[:], in_=alb[:], func=AF.Exp), reads=["alb"], writes=["alb"])
    P.dve(lambda e: e.scalar_tensor_tensor(out=col["g"][:], in0=tA[:], scalar=-1.0, in1=alb[:].unsqueeze(1).to_broadcast([128, NT, 8]),
                                           op0=ALU.mult, op1=ALU.mult), reads=["tA", "alb"], writes=["col_g"])
    P.act(lambda e: e.activation(out=col["beta"][:], in_=abv[:, :, 8:16], func=AF.Exp, scale=-1.0), reads=["abv"], writes=["col_beta"])
    P.act(lambda e: e.activation(out=col["beta"][:], in_=col["beta"][:], func=AF.Ln, scale=1.0, bias=1.0), reads=["col_beta"], writes=["col_beta"])
    P.act(lambda e: e.activation(out=col["beta"][:], in_=col["beta"][:], func=AF.Exp, scale=-1.0), reads=["col_beta"], writes=["col_beta"])
    pgc, pgck = psn(C, ALLB)
    prc, prck = psn(C, ALLB)
    ptt, pttk = psn(C, ALLB)
    for n in range(NT):
        P.pe(lambda e, n=n: e.matmul(pgc[:, n * 8:(n + 1) * 8], lhsT=tri32[:, 0, :], rhs=col["g"][:, n, :], start=True, stop=True),
             reads=["tri32", "col_g"], writes=[pgck])
        P.pe(lambda e, n=n: e.matmul(prc[:, n * 8:(n + 1) * 8], lhsT=tri32[:, 1, :], rhs=col["g"][:, n, :], start=True, stop=True),
             reads=["tri32", "col_g"], writes=[prck])
        P.pe(lambda e, n=n: e.matmul(ptt[:, n * 8:(n + 1) * 8], lhsT=ones32[:], rhs=col["g"][:, n, :], start=True, stop=True),
             reads=["ones32", "col_g"], writes=[pttk])
    fl = lambda t_: t_[:].rearrange("p t c -> p (t c)")
    P.act(lambda e: e.copy(out=fl(col["gc"]), in_=pgc[:, 0:128]), reads=[pgck], writes=["col_gc"])
    P.act(lambda e: e.activation(out=fl(col["egc"]), in_=pgc[:, 0:128], func=AF.Exp), reads=[pgck], writes=["col_egc"])
    P.act(lambda e: e.activation(out=fl(col["edec"]), in_=prc[:, 0:128], func=AF.Exp), reads=[prck], writes=["col_edec"])
    P.act(lambda e: e.activation(out=fl(col["gl"]), in_=ptt[:, 0:128], func=AF.Exp), reads=[pttk], writes=["col_gl"])
    P.dve(lambda e: e.scalar_tensor_tensor(out=col["cb"][:], in0=col["egc"][:], scalar=-1.0, in1=col["beta"][:], op0=ALU.mult, op1=ALU.mult),
          reads=["col_egc", "col_beta"], writes=["col_cb"])
    C.scope_end()

    W = HG * 128
    qT = C.sb("qT", [128, HG, T], BF16)
    kT = C.sb("kT", [128, HG, T], BF16)
    vb = C.sb("vb", [128, NT, HG, 128], BF16)
    sgA = C.sb("sgA", [128, NT, W], BF16)

    def bc_h(ap2):
        return ap2.unsqueeze(1).to_broadcast([128, HG, 128])

    def v3(ap):
        return ap.rearrange("p (h i) -> p h i", h=HG)

    for gI in range(8 // HG):
        h0 = gI * HG
        C.scope_begin()
        wgt = C.sb("wgt", [128, 8, W], BF16)
        P.dma("pool", wgt[:], winv[:, :, 3072 + h0 * 128:3072 + (h0 + HG) * 128], writes=["wgt"])
        raw = [C.sb("raw%d" % i, [128, T + 4], BF16) for i in range(2)]
        dcw = [C.sb("dcw%d" % i, [128, 4, 128], BF16) for i in range(2)]
        y32 = C.sb("y32", [128, T], F32)
        sqb = [C.sb("sqb%d" % i, [128, 512], BF16) for i in range(2)]
        rn = [C.sb("rn%d" % i, [128, 512], F32) for i in range(2)]
        vTb = [C.sb("vTb%d" % i, [128, 512], BF16) for i in range(2)]
        wsl = [C.sb("wsl%d" % i, [128, 8, 128], BF16) for i in range(2)]
        for i in range(2):
            P.dve(lambda e, i=i: e.memset(raw[i][:, 0:3], 0.0), writes=[("rawpad", i)])
        for t in range(NT):
            pg_, pgk_ = psn(C, ALLB)
            for k in range(8):
                P.pe(lambda e, pg_=pg_, k=k, t=t: e.matmul(pg_[:, 0:W], lhsT=hT[:, k, t * 128:(t + 1) * 128], rhs=wgt[:, k, :], start=(k == 0), stop=(k == 7)),
                     reads=[("hT", t), "wgt"], writes=[pgk_])
            P.act(lambda e, pg_=pg_, t=t: e.activation(out=sgA[:, t, :], in_=pg_[:, 0:W], func=AF.Silu), reads=[pgk_], writes=[("sgA", t)])
        nw = 0
        nblk = 0
        for hl in range(HG):
            h = h0 + hl
            for typ in (2, 0, 1):
                cidx = typ * 8 + h
                ib = nw % 2
                wb = wsl[ib]
                wk = ("wsl", ib)
                rw = raw[ib]
                dc = dcw[ib]
                nw += 1
                P.dma("pool", wb[:], winv[:, :, typ * 1024 + h * 128:typ * 1024 + (h + 1) * 128], writes=[wk])
                P.dve(lambda e, dc=dc, cidx=cidx: e.tensor_tensor(out=dc[:], in0=identb[:].unsqueeze(1).to_broadcast([128, 4, 128]),
                                                                 in1=cw[:, cidx, :].unsqueeze(2).to_broadcast([128, 4, 128]), op=ALU.mult),
                      reads=["identb", "cw"], writes=[("dcw", ib)])
                def proj_blk(tb, wb=wb, wk=wk, rw=rw, ib=ib):
                    sl = slice(tb * 512, (tb + 1) * 512)
                    pp, ppk = psn(C, ALLB)
                    for k in range(8):
                        P.pe(lambda e, k=k: e.matmul(pp[:], lhsT=wb[:, k, :], rhs=hT[:, k, sl], start=(k == 0), stop=(k == 7)),
                             reads=[wk] + [("hT", 4 * tb + i) for i in range(4)], writes=[ppk])
                    P.act(lambda e: e.copy(out=rw[:, 3 + tb * 512:3 + (tb + 1) * 512], in_=pp[:]), reads=[ppk], writes=[("raw", ib, tb)])

                def conv_blk(tb, rw=rw, dc=dc, ib=ib, typ=typ, hl=hl, h=h):
                    nonlocal nblk
                    sl = slice(tb * 512, (tb + 1) * 512)
                    pc, pck = psn(C, ALLB)
                    rkeys = [("raw", ib, tb), ("rawpad", ib)] + ([("raw", ib, tb - 1)] if tb > 0 else [])
                    for j in range(4):
                        P.pe(lambda e, j=j: e.matmul(pc[:], lhsT=dc[:, j, :], rhs=rw[:, tb * 512 + j:tb * 512 + j + 512], start=(j == 0), stop=(j == 3)),
                             reads=rkeys + [("dcw", ib)], writes=[pck])
                    if typ == 2:
                        vt = vTb[nblk % 2]
                        vk = ("vTb", nblk % 2)
                        nblk += 1
                        P.act(lambda e: e.activation(out=vt[:], in_=pc[:], func=AF.Silu), reads=[pck], writes=[vk])
                        pt_, ptk_ = psn(C, ALLB)
                        ptb = pt_[:].bitcast(BF16)
                        for i in range(4):
                            P.pe(lambda e, i=i: e.transpose(ptb[:, i * 128:(i + 1) * 128], vt[:, i * 128:(i + 1) * 128], identb[:]),
                                 reads=[vk, "identb"], writes=[ptk_])
                        P.dve(lambda e: e.tensor_tensor(
                            out=vb[:, 4 * tb:4 * tb + 4, hl, :], in0=ptb[:, 0:512].rearrange("p (t d) -> p t d", t=4),
                            in1=col["beta"][:, 4 * tb:4 * tb + 4, h:h + 1].to_broadcast([128, 4, 128]), op=ALU.mult),
                            reads=[ptk_, "col_beta"], writes=[("vb", tb)])
                    else:
                        P.act(lambda e: e.activation(out=y32[:, sl], in_=pc[:], func=AF.Silu), reads=[pck], writes=[("y32", tb)])

                for tb in range(4):
                    proj_blk(tb)
                    if tb >= 1:
                        conv_blk(tb - 1)
                conv_blk(3)
                if typ != 2:
                    dst = qT if typ == 0 else kT
                    dkey = "qT" if typ == 0 else "kT"
                    sc = float(128 ** -0.5) if typ == 0 else 1.0
                    for tb in range(4):
                        sl = slice(tb * 512, (tb + 1) * 512)
                        sb_ = sqb[tb % 2]
                        sk_ = ("sqb", tb % 2)
                        rb_ = rn[tb % 2]
                        rk_ = ("rn", tb % 2)
                        P.pool(lambda e, sb_=sb_, sl=sl: e.tensor_tensor(out=sb_[:], in0=y32[:, sl], in1=y32[:, sl], op=ALU.mult), reads=[("y32", tb)], writes=[sk_])
                        pp, ppk = psn(C, ALLB)
                        P.pe(lambda e, pp=pp, sb_=sb_: e.matmul(pp[:], lhsT=onesb[:], rhs=sb_[:], start=True, stop=True), reads=["onesb", sk_], writes=[ppk])
                        P.act(lambda e, pp=pp, rb_=rb_: e.activation(out=rb_[:], in_=pp[:], func=AF.Ln, scale=1.0, bias=C.epsc[:, 0:1]), reads=[ppk, "epsc"], writes=[rk_])
                        P.act(lambda e, rb_=rb_: e.activation(out=rb_[:], in_=rb_[:], func=AF.Exp, scale=-0.5), reads=[rk_], writes=[rk_])
                        P.dve(lambda e, dst=dst, hl=hl, sl=sl, sc=sc, rb_=rb_: e.scalar_tensor_tensor(out=dst[:, hl, sl], in0=y32[:, sl], scalar=sc, in1=rb_[:],
                                                                                                op0=ALU.mult, op1=ALU.mult),
                              reads=[("y32", tb), rk_], writes=[(dkey, tb)])
        C.scope_end()

        C.scope_begin()
        wog = C.sb("wog", [128, HG, D], BF16)
        P.dma("pool", wog[:], wout_d.rearrange("(c p) n -> p c n", p=128)[:, h0:h0 + HG, :], writes=["wog"])
        P.pool(lambda e: e.tensor_tensor(out=wog[:], in0=wog[:], in1=gate_bc[:].unsqueeze(1).to_broadcast([128, HG, D]), op=ALU.mult),
               reads=["wog", "gate_bc"], writes=["wog"])
        S32 = C.sb("S32", [128, W], F32)
        Sbf = C.sb("Sbf", [128, W], BF16)
        P.dve(lambda e: e.memset(S32[:], 0.0), writes=["S32"])
        P.dve(lambda e: e.memset(Sbf[:], 0.0), writes=["Sbf"])
        f32t = lambda nm: C.sb(nm, [128, W], F32)
        bft = lambda nm: C.sb(nm, [128, W], BF16)
        NPRE = 1
        NIF = NPRE + 1
        TS = []
        for i in range(NPRE):
            d_ = {}
            for x in ("dgc", "d2", "dec", "egr", "bsl", "tL", "tU"):
                d_[x] = f32t(x)
            for x in ("L", "U", "attn", "Mm", "Tt0", "Tt1", "Tm0", "Tm1"):
                d_[x] = bft(x)
            TS.append(d_)
        TtF = [bft("TtF%d" % i) for i in range(NIF)]
        attnT = [bft("attnT%d" % i) for i in range(NIF)]
        qdT = [bft("qdT%d" % i) for i in range(NIF)]
        Rr, vnew, vdec, ktok, og, ogT = [bft(x) for x in ("Rr", "vnew", "vdec", "ktok", "og", "ogT")]
        tR, o32, osq = [f32t(x) for x in ("tR", "o32", "osq")]
        oss = C.sb("oss", [128, HG], F32)
        identb_h = identb[:].unsqueeze(1).to_broadcast([128, HG, 128])
        PB_PREP = [0, 1, 2, 3, 4]
        PB_REC = [5, 6]
        PB_OUT = [7]
        obank = {}

        def colb(nm, n):
            return col[nm][:, n, h0:h0 + HG].unsqueeze(2).to_broadcast([128, HG, 128])

        def tr_heads(S, dst_ap, src, skey, dkey, pool):
            pt_, ptk_ = psn(C, pool)
            ptb = pt_[:].bitcast(BF16)
            for hl in range(HG):
                hs = slice(hl * 128, (hl + 1) * 128)
                S.pe(lambda e, ptb=ptb, hs=hs: e.transpose(ptb[:, hs], src[:, hs], identb[:]), reads=[skey, "identb"], writes=[ptk_])
            S.act(lambda e, ptb=ptb: e.copy(out=dst_ap, in_=ptb[:, 0:W]), reads=[ptk_], writes=[dkey])

        def prep(n):
            S = Stream()
            csl = slice(n * 128, (n + 1) * 128)
            pb = n % NIF
            ts = n % NPRE
            D_ = TS[ts]
            dgc, d2, dec, egr, bsl, tL, tU = [D_[x] for x in ("dgc", "d2", "dec", "egr", "bsl", "tL", "tU")]
            L_, U_, attn, Mm = [D_[x] for x in ("L", "U", "attn", "Mm")]
            Tt = [D_["Tt0"], D_["Tt1"]]
            Tm = [D_["Tm0"], D_["Tm1"]]
            K_ = lambda nm: (nm, "ts", ts)
            S.dve(lambda e: e.tensor_tensor(out=v3(dgc[:]), in0=bc_h(ident32[:]), in1=colb("gc", n), op=ALU.mult), reads=["ident32", "col_gc"], writes=[K_("dgc")])
            pgr, pgrk = psn(C, PB_PREP)
            S.pe(lambda e: e.matmul(pgr[:, 0:W], lhsT=ones32[:], rhs=dgc[:], start=True, stop=True), reads=["ones32", K_("dgc")], writes=[pgrk])
            S.dve(lambda e: e.scalar_tensor_tensor(out=v3(d2[:]), in0=v3(pgr[:, 0:W]), scalar=-1.0, in1=colb("gc", n), op0=ALU.mult, op1=ALU.add),
                  reads=[pgrk, "col_gc"], writes=[K_("d2")])
            S.dve(lambda e: e.tensor_tensor(out=v3(d2[:]), in0=v3(d2[:]), in1=bc_h(mk32[:, 1, :]), op=ALU.add), reads=[K_("d2"), "mk32"], writes=[K_("d2")])
            S.act(lambda e: e.activation(out=dec[:], in_=d2[:], func=AF.Exp), reads=[K_("d2")], writes=[K_("dec")])
            S.act(lambda e: e.activation(out=egr[:], in_=pgr[:, 0:W], func=AF.Exp), reads=[pgrk], writes=[K_("egr")])
            S.dve(lambda e: e.tensor_tensor(out=v3(bsl[:]), in0=bc_h(mk32[:, 0, :]), in1=colb("beta", n), op=ALU.mult), reads=["mk32", "col_beta"], writes=[K_("bsl")])
            pkk, pkkk = psn(C, PB_PREP)
            pqk, pqkk = psn(C, PB_PREP)
            for hl in range(HG):
                S.pe(lambda e, hl=hl: e.matmul(pkk[:, hl * 128:(hl + 1) * 128], lhsT=kT[:, hl, csl], rhs=kT[:, hl, csl], start=True, stop=True),
                     reads=[("kT", n // 4)], writes=[pkkk])
            for hl in range(HG):
                S.pe(lambda e, hl=hl: e.matmul(pqk[:, hl * 128:(hl + 1) * 128], lhsT=qT[:, hl, csl], rhs=kT[:, hl, csl], start=True, stop=True),
                     reads=[("kT", n // 4), ("qT", n // 4)], writes=[pqkk])
            S.dve(lambda e: e.tensor_tensor(out=tL[:], in0=pkk[:, 0:W], in1=dec[:], op=ALU.mult), reads=[pkkk, K_("dec")], writes=[K_("tL")])
            S.dve(lambda e: e.tensor_tensor(out=L_[:], in0=tL[:], in1=bsl[:], op=ALU.mult), reads=[K_("tL"), K_("bsl")], writes=[K_("L")])
            S.dve(lambda e: e.tensor_tensor(out=attn[:], in0=pqk[:, 0:W], in1=dec[:], op=ALU.mult), reads=[pqkk, K_("dec")], writes=[K_("attn")])
            S.dve(lambda e: e.tensor_tensor(out=v3(qdT[pb][:]), in0=qT[:, :, csl], in1=v3(egr[:]), op=ALU.mult), reads=[("qT", n // 4), K_("egr")], writes=[("qdT", pb)])
            tr_heads(S, U_[:], L_, K_("L"), K_("U"), PB_PREP)
            tr_heads(S, attnT[pb][:], attn, K_("attn"), ("attnT", pb), PB_PREP)
            S.dve(lambda e: e.tensor_tensor(out=v3(tU[:]), in0=v3(U_[:]), in1=bc_h(lvm[:, 0, :]), op=ALU.mult), reads=[K_("U"), "lvm"], writes=[K_("tU")])
            S.dve(lambda e: e.tensor_tensor(out=v3(Tt[0][:]), in0=v3(tU[:]), in1=identb_h, op=ALU.add), reads=[K_("tU"), "identb"], writes=[("Tt", ts, 0)])
            tr_heads(S, Tm[0][:], Tt[0], ("Tt", ts, 0), ("Tm", ts, 0), PB_PREP)
            cur = 0
            for lv in range(1, 7):
                nxt = 1 - cur
                last = (lv == 6)
                tt_c, tm_c = Tt[cur], Tm[cur]
                pm, pmk = psn(C, PB_PREP)
                for hl in range(HG):
                    hs = slice(hl * 128, (hl + 1) * 128)
                    S.pe(lambda e, pm=pm, hs=hs, tt_c=tt_c: e.matmul(pm[:, hs], lhsT=L_[:, hs], rhs=tt_c[:, hs], start=True, stop=True),
                         reads=[K_("L"), ("Tt", ts, cur)], writes=[pmk])
                S.dve(lambda e, pm=pm, lv=lv: e.tensor_tensor(out=v3(Mm[:]), in0=v3(pm[:, 0:W]), in1=bc_h(lvm[:, 2 * lv, :]), op=ALU.mult),
                      reads=[pmk, "lvm"], writes=[K_("Mm")])
                pt2, pt2k = psn(C, PB_PREP)
                S.pe(lambda e, pt2=pt2, tt_c=tt_c: e.matmul(pt2[:, 0:W], lhsT=identb[:], rhs=tt_c[:], start=True, stop=False),
                     reads=["identb", ("Tt", ts, cur)], writes=[pt2k])
                for hl in range(HG):
                    hs = slice(hl * 128, (hl + 1) * 128)
                    S.pe(lambda e, pt2=pt2, hs=hs, tm_c=tm_c, hl=hl: e.matmul(pt2[:, hs], lhsT=tm_c[:, hs], rhs=Mm[:, hs], start=False, stop=(hl == HG - 1),
                                                                            skip_group_check=True),
                         reads=[("Tm", ts, cur), K_("Mm")], writes=[pt2k])
                if last:
                    S.act(lambda e, pt2=pt2: e.copy(out=TtF[pb][:], in_=pt2[:, 0:W]), reads=[pt2k], writes=[("TtF", pb)])
                else:
                    S.act(lambda e, pt2=pt2, nxt=nxt: e.copy(out=Tt[nxt][:], in_=pt2[:, 0:W]), reads=[pt2k], writes=[("Tt", ts, nxt)])
                    tr_heads(S, Tm[nxt][:], Tt[nxt], ("Tt", ts, nxt), ("Tm", ts, nxt), PB_PREP)
                cur = nxt
            return S

        def rec(n):
            S = Stream()
            csl = slice(n * 128, (n + 1) * 128)
            pb = n % NIF
            ttf = TtF[pb]
            pks, pksk = psn(C, PB_REC)
            for hl in range(HG):
                hs = slice(hl * 128, (hl + 1) * 128)
                S.pe(lambda e, hs=hs, hl=hl: e.matmul(pks[:, hs], lhsT=kT[:, hl, csl], rhs=Sbf[:, hs], start=True, stop=True),
                     reads=[("kT", n // 4), "Sbf"], writes=[pksk])
            for hl in range(HG):
                hs = slice(hl * 128, (hl + 1) * 128)
                S.dve(lambda e, hs=hs, hl=hl: e.scalar_tensor_tensor(out=Rr[:, hs], in0=pks[:, hs], scalar=col["cb"][:, n, h0 + hl:h0 + hl + 1], in1=vb[:, n, hl, :],
                                                                     op0=ALU.mult, op1=ALU.add), reads=[pksk, "col_cb", ("vb", n // 4)], writes=["Rr"])
            pvn, pvnk = psn(C, PB_REC)
            for hl in range(HG):
                hs = slice(hl * 128, (hl + 1) * 128)
                S.pe(lambda e, hs=hs: e.matmul(pvn[:, hs], lhsT=ttf[:, hs], rhs=Rr[:, hs], start=True, stop=True),
                     reads=[("TtF", pb), "Rr"], writes=[pvnk])
            S.act(lambda e: e.copy(out=vnew[:], in_=pvn[:, 0:W]), reads=[pvnk], writes=["vnew"])
            S.dve(lambda e: e.tensor_tensor(out=v3(vdec[:]), in0=v3(pvn[:, 0:W]), in1=colb("edec", n), op=ALU.mult), reads=[pvnk, "col_edec"], writes=["vdec"])
            tr_heads_k(S, n)
            pss, pssk = psn(C, PB_REC)
            for hl in range(HG):
                hs = slice(hl * 128, (hl + 1) * 128)
                S.pe(lambda e, hs=hs: e.matmul(pss[:, hs], lhsT=ktok[:, hs], rhs=vdec[:, hs], start=True, stop=True),
                     reads=["ktok", "vdec"], writes=[pssk])
            po_, pok_ = psn(C, PB_REC)
            for hl in range(HG):
                hs = slice(hl * 128, (hl + 1) * 128)
                S.pe(lambda e, hs=hs: e.matmul(po_[:, hs], lhsT=qdT[pb][:, hs], rhs=Sbf[:, hs], start=True, stop=False),
                     reads=[("qdT", pb), "Sbf"], writes=[pok_])
                S.pe(lambda e, hs=hs: e.matmul(po_[:, hs], lhsT=attnT[pb][:, hs], rhs=vnew[:, hs], start=False, stop=True),
                     reads=[("attnT", pb), "vnew"], writes=[pok_])
            obank[n] = (po_, pok_)
            for hl in range(HG):
                hs = slice(hl * 128, (hl + 1) * 128)
                S.dve(lambda e, hs=hs, hl=hl: e.scalar_tensor_tensor(out=S32[:, hs], in0=S32[:, hs], scalar=col["gl"][:, n, h0 + hl:h0 + hl + 1], in1=pss[:, hs],
                                                                     op0=ALU.mult, op1=ALU.add), reads=["S32", "col_gl", pssk], writes=["S32"])
            S.act(lambda e: e.copy(out=Sbf[:], in_=S32[:]), reads=["S32"], writes=["Sbf"])
            return S

        def tr_heads_k(S, n):
            csl = slice(n * 128, (n + 1) * 128)
            pkt, pktk = psn(C, PB_REC)
            pktb = pkt[:].bitcast(BF16)
            for hl in range(HG):
                S.pe(lambda e, hl=hl: e.transpose(pktb[:, hl * 128:(hl + 1) * 128], kT[:, hl, csl], identb[:]),
                     reads=[("kT", n // 4), "identb"], writes=[pktk])
            S.act(lambda e: e.copy(out=ktok[:], in_=pktb[:, 0:W]), reads=[pktk], writes=["ktok"])

        def outp(n):
            S = Stream()
            po_, pok_ = obank.pop(n)
            S.act(lambda e: e.copy(out=o32[:], in_=po_[:, 0:W]), reads=[pok_], writes=["o32"])
            S.act(lambda e: e.activation(out=osq[:], in_=o32[:], func=AF.Square), reads=["o32"], writes=["osq"])
            S.dve(lambda e: e.tensor_reduce(out=oss[:], in_=v3(osq[:]), axis=AX.X, op=ALU.add), reads=["osq"], writes=["oss"])
            S.act(lambda e: e.activation(out=oss[:], in_=oss[:], func=AF.Ln, scale=1.0 / 128, bias=C.epsc[:, 0:1]), reads=["oss", "epsc"], writes=["oss"])
            S.act(lambda e: e.activation(out=oss[:], in_=oss[:], func=AF.Exp, scale=-0.5), reads=["oss"], writes=["oss"])
            S.dve(lambda e: e.tensor_tensor(out=v3(o32[:]), in0=v3(o32[:]), in1=oss[:].unsqueeze(2).to_broadcast([128, HG, 128]), op=ALU.mult),
                  reads=["o32", "oss"], writes=["o32"])
            S.dve(lambda e: e.tensor_tensor(out=v3(o32[:]), in0=v3(o32[:]), in1=bc_h(gnb[:]), op=ALU.mult), reads=["o32", "gnb"], writes=["o32"])
            S.dve(lambda e: e.tensor_tensor(out=og[:], in0=o32[:], in1=sgA[:, n, :], op=ALU.mult), reads=["o32", ("sgA", n)], writes=["og"])
            tr_heads(S, ogT[:], og, "og", "ogT", PB_OUT)
            for half in range(2):
                py, pyk = psn(C, PB_OUT)
                for hl in range(HG):
                    hs = slice(hl * 128, (hl + 1) * 128)
                    S.pe(lambda e, py=py, hs=hs, hl=hl, half=half: e.matmul(py[:], lhsT=ogT[:, hs], rhs=wog[:, hl, half * 512:(half + 1) * 512],
                                                                          start=(hl == 0), stop=(hl == HG - 1)), reads=["ogT", "wog"], writes=[pyk])
                S.dve(lambda e, py=py, half=half: e.tensor_tensor(out=X[:, n, half * 512:(half + 1) * 512], in0=X[:, n, half * 512:(half + 1) * 512],
                                                                 in1=py[:], op=ALU.add), reads=[pyk, ("X", n)], writes=[("X", n)])
            return S

        pend = {}

        def prep_slices(m):
            ops = prep(m).ops
            k = (len(ops) + NPRE - 1) // NPRE
            out_ = []
            for i in range(NPRE):
                st = Stream()
                st.ops = ops[i * k:(i + 1) * k]
                out_.append(st)
            return out_

        for r in range(-NPRE, NT + 1):
            streams = []
            if 0 <= r < NT:
                streams.append(rec(r))
            if 1 <= r <= NT:
                streams.append(outp(r - 1))
            for m in range(r + 1, r + NPRE + 1):
                if 0 <= m < NT:
                    if m not in pend:
                        pend[m] = prep_slices(m)
                    streams.append(pend[m][r - m + NPRE])
            merge_streams(P, streams)
        C.scope_end()


def build_fused(parts=None):
    if parts is None:
        parts = []
        for layer in range(4):
            parts.append(("gdn" if layer % 2 == 0 else "mla", layer))
            parts.append(("ffn", layer))
    C = Ctx()
    hT = C.sb("hT", [128, 8, T], BF16)
    R = setup(C)
    for kind, layer in parts:
        C.scope_begin()
        if kind == "gdn":
            gdn_body(C, R, hT, layer)
        elif kind == "mla":
            mla_body(C, R, hT, layer)
        else:
            ffn_body(C, R, hT, layer, layer == 3)
        C.scope_end()
    store_x(C, R["X"])
    C.P.finish()
    C.P.emit()
    return C.nc


def make_maps(inp):
    consts = {"ident": _ident(), "invf": _invf64(), "tri": _tri(), "gmasks": _gmasks()}
    shared = {}
    for nm in INPUT_SHAPES:
        if nm in ("x", "c", "positions") or nm in consts:
            continue
        shared[nm] = np.ascontiguousarray(np.asarray(inp[nm]), dtype=np.float32)
    maps = []
    for b in range(8):
        m = dict(shared)
        m.update(consts)
        m["x"] = np.ascontiguousarray(inp["x"][b], dtype=np.float32)
        m["c"] = np.ascontiguousarray(inp["c"][b], dtype=np.float32)
        m["positions"] = np.ascontiguousarray(inp["positions"][b]).astype(np.int32)
        maps.append(m)
    return maps


_NC = {}


def kernel(**inputs):
    inp = {k: np.asarray(v) for k, v in inputs.items()}
    if "nc" not in _NC:
        _NC["nc"] = build_fused()
    r = run_bass_kernel_spmd(_NC["nc"], make_maps(inp), core_ids=list(range(8)))
    return np.ascontiguousarray(np.stack([r.results[b]["out"] for b in range(8)], axis=0), dtype=np.float32)
```
